# Optimizing a Trainium2 kernel written in Bass

```python
import math
import jax
import jax.numpy as jnp
from jax import lax
import numpy as np

D_MODEL = 1024
BATCH = 2
SEQ = 8192
DEPTH = 2
DEC_BATCH = 32
DEC_SEQ = 4
PAST_LEN = 16384
PAGE_SIZE = 128

N_HEADS = 8
HEAD_DIM = 64
N_KV = 2
HPG = N_HEADS // N_KV
CMP_STRIDE = 16
CMP_LEN = 2 * CMP_STRIDE
SLC_BLOCK = 64
SLC_TOPK = 16
WINDOW = 512
Q_BLOCK = 128
SSM_WIDTH = D_MODEL // 2
SSM_GROUP = 16
N_SSM_GROUPS = SSM_WIDTH // SSM_GROUP
SSM_STATE = 64
N_MEM = 256
MEM_HEADS = 4
MEM_HEAD_DIM = 128
MEM_WIDTH = MEM_HEADS * MEM_HEAD_DIM
D_FF = 4 * D_MODEL
NSA_WIDTH = N_HEADS * HEAD_DIM
KV_WIDTH = N_KV * HEAD_DIM
MIX_WIDTH = NSA_WIDTH + SSM_WIDTH + MEM_WIDTH
OFF_KV = NSA_WIDTH
OFF_GN = OFF_KV + 6 * KV_WIDTH
OFF_U = OFF_GN + 3 * N_HEADS
OFF_MQ = OFF_U + SSM_WIDTH
OFF_GB = OFF_MQ + MEM_WIDTH
IN_WIDTH = OFF_GB + MIX_WIDTH
EPS = 1e-6
NEG_INF = -1e30
FORCED_SCORE = 1e9

kernel_name = 'hybrid_nsa_s5_memory_decode_step'


def rmsnorm(x, g):
    xf = x.astype(jnp.float32)
    y = xf * lax.rsqrt(jnp.mean(xf * xf, axis=-1, keepdims=True) + EPS)
    return (y * g.astype(jnp.float32)).astype(x.dtype)


def alibi_slopes(n):
    return jnp.exp2(-8.0 * (jnp.arange(n, dtype=jnp.float32) + 1.0) / n)


def masked_probs(s, mask):
    s = jnp.where(mask, s, NEG_INF)
    m = jnp.max(s, axis=-1, keepdims=True)
    e = jnp.where(mask, jnp.exp(s - m), 0.0)
    return e / jnp.maximum(jnp.sum(e, axis=-1, keepdims=True), 1e-30)


def in_proj(h, w_in):
    B, T, _ = h.shape
    z = h @ w_in
    q = z[..., :OFF_KV].reshape(B, T, N_HEADS, HEAD_DIM)
    kv = z[..., OFF_KV:OFF_GN].reshape(B, T, 3, 2, N_KV, HEAD_DIM)
    gn = jax.nn.sigmoid(z[..., OFF_GN:OFF_U]).reshape(B, T, 3, N_HEADS)
    u = z[..., OFF_U:OFF_MQ]
    mq = z[..., OFF_MQ:OFF_GB].reshape(B, T, MEM_HEADS, MEM_HEAD_DIM)
    gb = jax.nn.sigmoid(z[..., OFF_GB:])
    return q, kv, gn, u, mq, gb


def compress(rows, pe, w1, w2):
    B, L, G, dh = rows.shape
    ch = rows.reshape(B, L // CMP_STRIDE, CMP_STRIDE, G, dh)
    lo = jnp.einsum('bnlgd,lde->bgne', ch + pe[:CMP_STRIDE, None, :], w1[:CMP_STRIDE])
    hi = jnp.einsum('bnlgd,lde->bgne', ch + pe[CMP_STRIDE:, None, :], w1[CMP_STRIDE:])
    return jax.nn.gelu(lo[:, :, :-1] + hi[:, :, 1:]) @ w2


def nsa_block(q, g, pos_q, ck, cv, kb, vb, kw, vw, pos_w):
    B, G, Hg, Tq, dh = q.shape
    f32 = jnp.float32
    scale = dh ** -0.5
    slope = alibi_slopes(N_HEADS).reshape(1, G, Hg, 1, 1)
    n_c = ck.shape[2]
    c_end = jnp.arange(n_c, dtype=jnp.int32) * CMP_STRIDE + (CMP_LEN - 1)
    d_c = pos_q[:, None] - c_end[None, :]
    s = jnp.einsum('bghtd,bgnd->bghtn', q, ck, preferred_element_type=f32) * scale - slope * d_c.astype(f32)
    p_c = masked_probs(s, d_c >= 0)
    o_c = jnp.einsum('bghtn,bgnd->bghtd', p_c.astype(cv.dtype), cv)
    n_b = kb.shape[2]
    per = SLC_BLOCK // CMP_STRIDE
    imp = jnp.pad(jnp.sum(p_c, axis=2), ((0, 0), (0, 0), (0, 0), (0, n_b * per - n_c)))
    imp = imp.reshape(B, G, Tq, n_b, per).sum(-1)
    blk = jnp.arange(n_b, dtype=jnp.int32)[None, :]
    cur = (pos_q // SLC_BLOCK)[:, None]
    valid = blk <= cur
    forced = (blk == 0) | (blk == cur) | (blk == cur - 1)
    score = jnp.where(valid, jnp.where(forced, FORCED_SCORE, imp), -1.0)
    _, idx = lax.top_k(score, min(SLC_TOPK, n_b))
    bi = jnp.arange(B)[:, None, None, None]
    gi = jnp.arange(G)[None, :, None, None]
    ks = kb[bi, gi, idx].reshape(B, G, Tq, -1, dh)
    vs = vb[bi, gi, idx].reshape(B, G, Tq, -1, dh)
    kpos = (idx[..., None] * SLC_BLOCK + jnp.arange(SLC_BLOCK, dtype=jnp.int32)).reshape(B, G, Tq, -1)
    d_s = (pos_q[None, None, :, None] - kpos)[:, :, None]
    s = jnp.einsum('bghtd,bgtnd->bghtn', q, ks, preferred_element_type=f32) * scale - slope * d_s.astype(f32)
    p_s = masked_probs(s, d_s >= 0)
    o_s = jnp.einsum('bghtn,bgtnd->bghtd', p_s.astype(vs.dtype), vs)
    d_w = pos_q[:, None] - pos_w[None, :]
    s = jnp.einsum('bghtd,bgld->bghtl', q, kw, preferred_element_type=f32) * scale - slope * d_w.astype(f32)
    p_w = masked_probs(s, (d_w >= 0) & (d_w <= WINDOW) & (pos_w >= 0)[None, :])
    o_w = jnp.einsum('bghtl,bgld->bghtd', p_w.astype(vw.dtype), vw)
    gate = g.transpose(2, 0, 3, 1).reshape(3, B, G, Hg, Tq)[..., None]
    o = gate[0] * o_c + gate[1] * o_s + gate[2] * o_w
    return o.transpose(0, 3, 1, 2, 4).reshape(B, Tq, G * Hg * dh)


def nsa_prompt(q, kv, gn, p):
    B, T = q.shape[:2]
    ck = compress(kv[:, :, 0, 0], p['cmp_pe'][0], p['cmp_w1'][0], p['cmp_w2'][0])
    cv = compress(kv[:, :, 0, 1], p['cmp_pe'][1], p['cmp_w1'][1], p['cmp_w2'][1])
    n_b = T // SLC_BLOCK
    kb = kv[:, :, 1, 0].transpose(0, 2, 1, 3).reshape(B, N_KV, n_b, SLC_BLOCK, HEAD_DIM)
    vb = kv[:, :, 1, 1].transpose(0, 2, 1, 3).reshape(B, N_KV, n_b, SLC_BLOCK, HEAD_DIM)
    pad = ((0, 0), (0, 0), (WINDOW, 0), (0, 0))
    kw = jnp.pad(kv[:, :, 2, 0].transpose(0, 2, 1, 3), pad)
    vw = jnp.pad(kv[:, :, 2, 1].transpose(0, 2, 1, 3), pad)
    nq = T // Q_BLOCK
    qb = q.reshape(B, nq, Q_BLOCK, N_KV, HPG, HEAD_DIM).transpose(1, 0, 3, 4, 2, 5)
    gb = gn.reshape(B, nq, Q_BLOCK, 3, N_HEADS).transpose(1, 0, 2, 3, 4)

    def body(args):
        j, qj, gj = args
        start = j * Q_BLOCK
        pos_q = start + jnp.arange(Q_BLOCK, dtype=jnp.int32)
        kwj = lax.dynamic_slice_in_dim(kw, start, WINDOW + Q_BLOCK, axis=2)
        vwj = lax.dynamic_slice_in_dim(vw, start, WINDOW + Q_BLOCK, axis=2)
        pos_w = start - WINDOW + jnp.arange(WINDOW + Q_BLOCK, dtype=jnp.int32)
        return nsa_block(qj, gj, pos_q, ck, cv, kb, vb, kwj, vwj, pos_w)

    o = lax.map(body, (jnp.arange(nq, dtype=jnp.int32), qb, gb))
    return o.transpose(1, 0, 2, 3).reshape(B, T, NSA_WIDTH)


def nsa_sample(q, kv, gn, cmp_pool, slc_pool, win_buf, page_table, p):
    B, S = q.shape[:2]
    past = page_table.shape[1] * PAGE_SIZE
    L = past + S
    l_pad = -(-L // SLC_BLOCK) * SLC_BLOCK

    def paged_rows(pool, new):
        old = pool[page_table].reshape((B, past) + pool.shape[2:])
        rows = jnp.concatenate([old, new.astype(old.dtype)], axis=1)
        return jnp.pad(rows, ((0, 0), (0, l_pad - L), (0, 0), (0, 0), (0, 0)))

    rc = paged_rows(cmp_pool, kv[:, :, 0])
    ck = compress(rc[:, :, 0], p['cmp_pe'][0], p['cmp_w1'][0], p['cmp_w2'][0])
    cv = compress(rc[:, :, 1], p['cmp_pe'][1], p['cmp_w1'][1], p['cmp_w2'][1])
    rs = paged_rows(slc_pool, kv[:, :, 1])
    n_b = l_pad // SLC_BLOCK
    kb = rs[:, :, 0].transpose(0, 2, 1, 3).reshape(B, N_KV, n_b, SLC_BLOCK, HEAD_DIM)
    vb = rs[:, :, 1].transpose(0, 2, 1, 3).reshape(B, N_KV, n_b, SLC_BLOCK, HEAD_DIM)
    wrows = jnp.concatenate([win_buf, kv[:, :, 2].astype(win_buf.dtype)], axis=1)
    wb = win_buf.shape[1]
    pos_w = past - wb + jnp.arange(wb + S, dtype=jnp.int32)
    kw = wrows[:, :, 0].transpose(0, 2, 1, 3)
    vw = wrows[:, :, 1].transpose(0, 2, 1, 3)
    pos_q = past + jnp.arange(S, dtype=jnp.int32)
    qh = q.reshape(B, S, N_KV, HPG, HEAD_DIM).transpose(0, 2, 3, 1, 4)
    o = nsa_block(qh, gn, pos_q, ck, cv, kb, vb, kw, vw, pos_w)
    return o, wrows[:, S:]


def complex_affine_combine(e1, e2):
    a1r, a1i, b1r, b1i = e1
    a2r, a2i, b2r, b2i = e2
    ar = a1r * a2r - a1i * a2i
    ai = a1r * a2i + a1i * a2r
    br = a2r * b1r - a2i * b1i + b2r
    bi = a2r * b1i + a2i * b1r + b2i
    return ar, ai, br, bi


def ssm_discretise(p):
    f32 = jnp.float32
    ar = p['a_re'].astype(f32)
    ai = p['a_im'].astype(f32)
    dt = jnp.exp(p['log_dt'].astype(f32))[:, None]
    mag = jnp.exp(dt * ar)
    abr = mag * jnp.cos(dt * ai)
    abi = mag * jnp.sin(dt * ai)
    den = ar * ar + ai * ai
    fr = ((abr - 1.0) * ar + abi * ai) / den
    fi = (abi * ar - (abr - 1.0) * ai) / den
    br = p['b_re'].astype(f32)
    bi = p['b_im'].astype(f32)
    bbr = fr[..., None] * br - fi[..., None] * bi
    bbi = fr[..., None] * bi + fi[..., None] * br
    return abr, abi, bbr, bbi


def ssm_branch(u, h0_re, h0_im, p):
    B, T, _ = u.shape
    f32 = jnp.float32
    abr, abi, bbr, bbi = ssm_discretise(p)
    uf = u.astype(f32).reshape(B, T, N_SSM_GROUPS, SSM_GROUP)
    bu_r = jnp.einsum('btgc,gpc->tbgp', uf, bbr)
    bu_i = jnp.einsum('btgc,gpc->tbgp', uf, bbi)
    h0r = h0_re.astype(f32)
    h0i = h0_im.astype(f32)
    bu_r = bu_r.at[0].add(abr * h0r - abi * h0i)
    bu_i = bu_i.at[0].add(abr * h0i + abi * h0r)
    a_r = jnp.broadcast_to(abr, bu_r.shape)
    a_i = jnp.broadcast_to(abi, bu_i.shape)
    _, _, h_r, h_i = lax.associative_scan(complex_affine_combine, (a_r, a_i, bu_r, bu_i), axis=0)
    y = (jnp.einsum('tbgp,gcp->btgc', h_r, p['c_re'].astype(f32))
         - jnp.einsum('tbgp,gcp->btgc', h_i, p['c_im'].astype(f32))
         + p['d'].astype(f32) * uf)
    y = jax.nn.gelu(y.reshape(B, T, SSM_WIDTH)).astype(u.dtype)
    o = y * jax.nn.sigmoid(y @ p['w_glu'] + p['b_glu'])
    return o, h_r[-1].astype(h0_re.dtype), h_i[-1].astype(h0_re.dtype)


def memory_kv(mem, g, w):
    B, M, _ = mem.shape
    return (rmsnorm(mem, g) @ w).reshape(B, M, 2, MEM_HEADS, MEM_HEAD_DIM)


def memory_attend(mq, mkv):
    B, T = mq.shape[:2]
    s = jnp.einsum('bthd,bmhd->bhtm', mq, mkv[:, :, 0], preferred_element_type=jnp.float32) * (MEM_HEAD_DIM ** -0.5)
    pr = jax.nn.softmax(s, axis=-1)
    o = jnp.einsum('bhtm,bmhd->bthd', pr.astype(mkv.dtype), mkv[:, :, 1])
    return o.reshape(B, T, MEM_WIDTH)


def merge(x, o_a, o_b, o_c, gb, w_o):
    mixed = jnp.concatenate([o_a.astype(x.dtype), o_b.astype(x.dtype), o_c.astype(x.dtype)], axis=-1) * gb
    return x + mixed @ w_o


def ffn(x, g, w1, w2):
    h = rmsnorm(x, g)
    return x + jnp.square(jax.nn.relu(h @ w1)) @ w2


def setup_inputs(seed: int = 0) -> dict:
    key = jax.random.key(seed)
    ks = jax.random.split(key, 40)
    f32 = jnp.float32

    def nrm(i, shape, scale=1.0):
        return jax.random.normal(ks[i], shape, f32) * scale

    n_pages = PAST_LEN // PAGE_SIZE
    n_phys = (DEC_BATCH * n_pages * 5) // 4
    win_buf = min(WINDOW, PAST_LEN)
    page_table = jax.random.permutation(ks[0], n_phys)[:DEC_BATCH * n_pages].reshape(DEC_BATCH, n_pages).astype(jnp.int32)
    a_im = math.pi * jnp.arange(SSM_STATE, dtype=f32)[None, None, :] + nrm(17, (DEPTH, N_SSM_GROUPS, SSM_STATE), 0.01)
    return {
        'x_prompt': nrm(1, (BATCH, SEQ, D_MODEL)),
        'x_sample': nrm(2, (DEC_BATCH, DEC_SEQ, D_MODEL)),
        'cache_cmp_kv': nrm(3, (DEPTH, n_phys, PAGE_SIZE, 2, N_KV, HEAD_DIM)),
        'cache_slc_kv': nrm(4, (DEPTH, n_phys, PAGE_SIZE, 2, N_KV, HEAD_DIM)),
        'cache_win_kv': nrm(5, (DEPTH, DEC_BATCH, win_buf, 2, N_KV, HEAD_DIM)),
        'state_ssm_re': nrm(6, (DEPTH, DEC_BATCH, N_SSM_GROUPS, SSM_STATE), 0.5),
        'state_ssm_im': nrm(7, (DEPTH, DEC_BATCH, N_SSM_GROUPS, SSM_STATE), 0.5),
        'cache_mem_kv': nrm(8, (DEPTH, DEC_BATCH, N_MEM, 2, MEM_HEADS, MEM_HEAD_DIM)),
        'page_table': page_table,
        'mem_prompt': nrm(9, (BATCH, N_MEM, D_MODEL)),
        'norm1_g': 1.0 + nrm(10, (DEPTH, D_MODEL), 0.02),
        'w_in': nrm(11, (DEPTH, D_MODEL, IN_WIDTH), D_MODEL ** -0.5),
        'cmp_pe': nrm(12, (DEPTH, 2, CMP_LEN, HEAD_DIM), 0.1),
        'cmp_w1': nrm(13, (DEPTH, 2, CMP_LEN, HEAD_DIM, HEAD_DIM), (CMP_LEN * HEAD_DIM) ** -0.5),
        'cmp_w2': nrm(14, (DEPTH, 2, HEAD_DIM, HEAD_DIM), HEAD_DIM ** -0.5),
        'ssm_a_re': -0.5 + nrm(15, (DEPTH, N_SSM_GROUPS, SSM_STATE), 0.01),
        'ssm_a_im': a_im,
        'ssm_log_dt': jax.random.uniform(ks[16], (DEPTH, N_SSM_GROUPS), f32, math.log(1e-3), math.log(1e-1)),
        'ssm_b_re': nrm(18, (DEPTH, N_SSM_GROUPS, SSM_STATE, SSM_GROUP), (2.0 * SSM_GROUP) ** -0.5),
        'ssm_b_im': nrm(19, (DEPTH, N_SSM_GROUPS, SSM_STATE, SSM_GROUP), (2.0 * SSM_GROUP) ** -0.5),
        'ssm_c_re': nrm(20, (DEPTH, N_SSM_GROUPS, SSM_GROUP, SSM_STATE), (2.0 * SSM_STATE) ** -0.5),
        'ssm_c_im': nrm(21, (DEPTH, N_SSM_GROUPS, SSM_GROUP, SSM_STATE), (2.0 * SSM_STATE) ** -0.5),
        'ssm_d': nrm(22, (DEPTH, N_SSM_GROUPS, SSM_GROUP)),
        'w_glu': nrm(23, (DEPTH, SSM_WIDTH, SSM_WIDTH), SSM_WIDTH ** -0.5),
        'b_glu': nrm(24, (DEPTH, SSM_WIDTH), 0.01),
        'mem_norm_g': 1.0 + nrm(25, (DEPTH, D_MODEL), 0.02),
        'w_mem_kv': nrm(26, (DEPTH, D_MODEL, 2 * MEM_WIDTH), D_MODEL ** -0.5),
        'w_o': nrm(27, (DEPTH, MIX_WIDTH, D_MODEL), MIX_WIDTH ** -0.5),
        'norm2_g': 1.0 + nrm(28, (DEPTH, D_MODEL), 0.02),
        'w_ff1': nrm(29, (DEPTH, D_MODEL, D_FF), D_MODEL ** -0.5),
        'w_ff2': nrm(30, (DEPTH, D_FF, D_MODEL), D_FF ** -0.5),
        'final_norm_g': 1.0 + nrm(31, (D_MODEL,), 0.02),
    }


def reference(x_prompt, x_sample, cache_cmp_kv, cache_slc_kv, cache_win_kv, state_ssm_re, state_ssm_im,
              cache_mem_kv, page_table, mem_prompt, norm1_g, w_in, cmp_pe, cmp_w1, cmp_w2,
              ssm_a_re, ssm_a_im, ssm_log_dt, ssm_b_re, ssm_b_im, ssm_c_re, ssm_c_im, ssm_d,
              w_glu, b_glu, mem_norm_g, w_mem_kv, w_o, norm2_g, w_ff1, w_ff2, final_norm_g):
    xp = x_prompt
    xs = x_sample
    cmp_p, cmp_s, slc_p, slc_s, win_p, win_s = [], [], [], [], [], []
    hr_p, hi_p, hr_s, hi_s, mkv_p = [], [], [], [], []
    for l in range(DEPTH):
        p = {'cmp_pe': cmp_pe[l], 'cmp_w1': cmp_w1[l], 'cmp_w2': cmp_w2[l],
             'a_re': ssm_a_re[l], 'a_im': ssm_a_im[l], 'log_dt': ssm_log_dt[l],
             'b_re': ssm_b_re[l], 'b_im': ssm_b_im[l], 'c_re': ssm_c_re[l], 'c_im': ssm_c_im[l],
             'd': ssm_d[l], 'w_glu': w_glu[l], 'b_glu': b_glu[l]}
        h = rmsnorm(xp, norm1_g[l])
        q, kv, gn, u, mq, gb = in_proj(h, w_in[l])
        o_a = nsa_prompt(q, kv, gn, p)
        z0 = jnp.zeros((xp.shape[0], N_SSM_GROUPS, SSM_STATE), jnp.float32)
        o_b, hr, hi = ssm_branch(u, z0, z0, p)
        mkv = memory_kv(mem_prompt, mem_norm_g[l], w_mem_kv[l])
        o_c = memory_attend(mq, mkv)
        xp = ffn(merge(xp, o_a, o_b, o_c, gb, w_o[l]), norm2_g[l], w_ff1[l], w_ff2[l])
        cmp_p.append(kv[:, :, 0])
        slc_p.append(kv[:, :, 1])
        win_p.append(kv[:, -min(WINDOW, xp.shape[1]):, 2])
        hr_p.append(hr)
        hi_p.append(hi)
        mkv_p.append(mkv)
        h = rmsnorm(xs, norm1_g[l])
        q, kv, gn, u, mq, gb = in_proj(h, w_in[l])
        o_a, new_win = nsa_sample(q, kv, gn, cache_cmp_kv[l], cache_slc_kv[l], cache_win_kv[l], page_table, p)
        o_b, hr, hi = ssm_branch(u, state_ssm_re[l], state_ssm_im[l], p)
        o_c = memory_attend(mq, cache_mem_kv[l])
        xs = ffn(merge(xs, o_a, o_b, o_c, gb, w_o[l]), norm2_g[l], w_ff1[l], w_ff2[l])
        cmp_s.append(kv[:, :, 0])
        slc_s.append(kv[:, :, 1])
        win_s.append(new_win)
        hr_s.append(hr)
        hi_s.append(hi)
    y_prompt = rmsnorm(xp, final_norm_g)
    y_sample = rmsnorm(xs, final_norm_g)
    return (y_prompt, y_sample,
            jnp.stack(cmp_p), jnp.stack(cmp_s),
            jnp.stack(slc_p), jnp.stack(slc_s),
            jnp.stack(win_p), jnp.stack(win_s),
            jnp.stack(hr_p), jnp.stack(hi_p),
            jnp.stack(hr_s), jnp.stack(hi_s),
            jnp.stack(mkv_p))
```

```python
import math
from contextlib import ExitStack

import numpy as np
import concourse.bass as bass
import concourse.mybir as mybir
from concourse.bass_utils import run_bass_kernel_spmd

F32 = mybir.dt.float32
BF16 = mybir.dt.bfloat16
I32 = mybir.dt.int32
U32 = mybir.dt.uint32
AF = mybir.ActivationFunctionType
ALU = mybir.AluOpType
AX = mybir.AxisListType

import os
KCUT = int(os.environ.get('KCUT', '0'))
KTILES = int(os.environ.get('KTILES', '0'))
KOUTQ = os.environ.get('KOUTQ', 'pool')
KVBENG = os.environ.get('KVBENG', 'act')
KSSM = os.environ.get('KSSM', '')
KSSMT = os.environ.get('KSSMT', '')
KNSA = os.environ.get('KNSA', '')
BRANCHES = set('abc')
D = 1024
DEPTH = 2
N_CORES = 8
NH = 8
HD = 64
NG = 2
HPG = 4
IN_W = 3864
OFF_KV = 512
OFF_GN = 1280
OFF_U = 1304
OFF_MQ = 1816
OFF_GB = 2328
MIXW = 1536
DFF = 4096
EPS = 1e-6
WINDOW = 512
TOPK = 16
NSG = 32
SST = 64
NMEM = 256
P = 128


class Sched:
    LIMIT = 30000
    NDMA = 40

    def __init__(self, nc):
        self.nc = nc
        self.eng = dict(pe=nc.tensor, dve=nc.vector, act=nc.scalar, pool=nc.gpsimd, sp=nc.sync)
        self.sem = {}
        self.cnt = {}
        self.nsem = 0
        for e in self.eng:
            self._new_sem(e)
        self.waited = {e: {} for e in self.eng}
        self.lastw = {}
        self.readers = {}
        self.dma_sems = [nc.alloc_semaphore(f"dq{i}") for i in range(self.NDMA)]
        self.dma_val = [0] * self.NDMA
        self.dma_rng = {'sp': (0, 16), 'act': (16, 24), 'pool': (24, self.NDMA)}
        self.dma_i = {q: lo for q, (lo, hi) in self.dma_rng.items()}
        self.n_instr = 0

    def _new_sem(self, e):
        self.sem[e] = self.nc.alloc_semaphore(f"es_{e}_{self.nsem}")
        self.nsem += 1
        self.cnt[e] = 0

    def _wait(self, e, tok):
        sem, val, src = tok
        if src == e and e == 'pe':
            return
        key = id(sem)
        if self.waited[e].get(key, 0) >= val:
            return
        self.eng[e].wait_ge(sem, val)
        self.waited[e][key] = val

    def _deps(self, e, R, W):
        for k in R:
            if k in self.lastw:
                self._wait(e, self.lastw[k])
        for k in W:
            if k in self.lastw:
                self._wait(e, self.lastw[k])
            for tok in self.readers.get(k, {}).values():
                self._wait(e, tok)

    def _record(self, tok, R, W):
        for k in W:
            self.lastw[k] = tok
            self.readers[k] = {}
        for k in R:
            self.readers.setdefault(k, {})[tok[2] if tok[2] != 'dma' else ('dma', id(tok[0]))] = tok

    def op(self, e, fn, R=(), W=()):
        self._deps(e, R, W)
        ins = fn(self.eng[e])
        if self.cnt[e] >= self.LIMIT:
            self._new_sem(e)
        self.cnt[e] += 1
        ins.then_inc(self.sem[e], 1)
        self._record((self.sem[e], self.cnt[e], e), R, W)
        self.n_instr += 1
        return ins

    def dma(self, q, out, in_, R=(), W=(), **kw):
        self._deps(q, R, W)
        i = self.dma_i[q]
        lo, hi = self.dma_rng[q]
        self.dma_i[q] = lo + (i + 1 - lo) % (hi - lo)
        if self.dma_val[i] > 0:
            self._wait(q, (self.dma_sems[i], self.dma_val[i], 'dma'))
        ins = self.eng[q].dma_start(out=out, in_=in_, **kw)
        self.dma_val[i] += 16
        ins.then_inc(self.dma_sems[i], 16)
        self._record((self.dma_sems[i], self.dma_val[i], 'dma'), R, W)
        self.n_instr += 1
        return ins

    def idma(self, out, in_, idx_ap, R=(), W=()):
        q = 'pool'
        self.barrier()
        self._deps(q, R, W)
        i = self.dma_i[q]
        lo, hi = self.dma_rng[q]
        self.dma_i[q] = lo + (i + 1 - lo) % (hi - lo)
        if self.dma_val[i] > 0:
            self._wait(q, (self.dma_sems[i], self.dma_val[i], 'dma'))
        ins = self.eng[q].indirect_dma_start(out=out, out_offset=None, in_=in_, in_offset=bass.IndirectOffsetOnAxis(ap=idx_ap, axis=0))
        self.dma_val[i] += 16
        ins.then_inc(self.dma_sems[i], 16)
        self._record((self.dma_sems[i], self.dma_val[i], 'dma'), R, W)
        self.n_instr += 1
        self.barrier()
        return ins

    def barrier(self):
        toks = [(self.sem[e], self.cnt[e], e) for e in self.eng if self.cnt[e] > 0]
        toks += [(self.dma_sems[i], self.dma_val[i], 'dma') for i in range(self.NDMA) if self.dma_val[i] > 0]
        for e in self.eng:
            for tok in toks:
                if tok[2] == e:
                    continue
                self._wait(e, tok)

    def finish(self):
        for i in range(self.NDMA):
            if self.dma_val[i] > 0:
                self._wait('sp', (self.dma_sems[i], self.dma_val[i], 'dma'))
        for e in self.eng:
            if e != 'sp' and self.cnt[e] > 0:
                self._wait('sp', (self.sem[e], self.cnt[e], e))


class Cfg:
    def __init__(self, T=8192, PAST=16384, NS=4, S=4, NPHYS=5120):
        self.T = T
        self.NT = T // P
        self.PAST = PAST
        self.NS = NS
        self.S = S
        self.NSTOK = NS * S
        self.NPHYS = NPHYS
        self.NPAGE = PAST // P
        self.WB = min(WINDOW, PAST)
        self.WP = min(WINDOW, T)


def slopes():
    return [2.0 ** (-(h + 1)) for h in range(NH)]


class MK:
    def __init__(self, cfg):
        self.cfg = cfg
        self.nc = bass.Bass("TRN2", target_bir_lowering=False)
        self.S = None
        self.uid = 0

    def sb(self, st, name, shape, dt):
        self.uid += 1
        return st.enter_context(self.nc.sbuf_tensor(f"{name}_{self.uid}", list(shape), dt))

    def ps(self, st, name, shape, dt):
        self.uid += 1
        return st.enter_context(self.nc.psum_tensor(f"{name}_{self.uid}", list(shape), dt))

    def dram(self, name, shape, dt, kind="Internal"):
        return self.nc.dram_tensor(name, list(shape), dt, kind=kind).ap()

    def mm(self, out, lhsT, rhs, start, stop, R, W):
        return self.S.op('pe', lambda e: e.matmul(out, lhsT=lhsT, rhs=rhs, start=start, stop=stop), R, W)

    def tr(self, out, in_, ident, R, W):
        return self.S.op('pe', lambda e: e.transpose(out, in_, ident), R, W)

    def act(self, out, in_, func, R, W, **kw):
        return self.S.op('act', lambda e: e.activation(out=out, in_=in_, func=func, **kw), R, W)

    def tt(self, eng, out, in0, in1, op, R, W):
        return self.S.op(eng, lambda e: e.tensor_tensor(out=out, in0=in0, in1=in1, op=op), R, W)

    def ts(self, eng, out, in0, s1, s2, op0, op1, R, W):
        if op1 is None:
            return self.S.op(eng, lambda e: e.tensor_scalar(out=out, in0=in0, scalar1=s1, scalar2=None, op0=op0), R, W)
        return self.S.op(eng, lambda e: e.tensor_scalar(out=out, in0=in0, scalar1=s1, scalar2=s2, op0=op0, op1=op1), R, W)

    def stt(self, out, in0, scalar, in1, op0, op1, R, W):
        return self.S.op('dve', lambda e: e.scalar_tensor_tensor(out=out, in0=in0, scalar=scalar, in1=in1, op0=op0, op1=op1), R, W)

    def cp(self, eng, out, in_, R, W):
        if eng == 'act':
            return self.S.op('act', lambda e: e.copy(out=out, in_=in_), R, W)
        return self.S.op(eng, lambda e: e.tensor_copy(out=out, in_=in_), R, W)

    def memset(self, eng, out, val, W):
        return self.S.op(eng, lambda e: e.memset(out, val), (), W)

    def ld(self, out, in_, R, W, q='sp', **kw):
        return self.S.dma(q, out, in_, R, W, **kw)

    def build(self):
        cfg = self.cfg
        nc = self.nc
        T, NT, NS, NSTOK = cfg.T, cfg.NT, cfg.NS, cfg.NSTOK
        NTT = NT + 1
        d = self.dram
        I = {}
        I['x_prompt'] = d("x_prompt", [T, D], F32, "ExternalInput")
        I['x_sample'] = d("x_sample", [NSTOK, D], F32, "ExternalInput")
        I['cache_cmp'] = d("cache_cmp", [DEPTH * cfg.NPHYS * P, 256], F32, "ExternalInput")
        I['cache_slc'] = d("cache_slc", [DEPTH * cfg.NPHYS * P, 256], F32, "ExternalInput")
        I['cache_win'] = d("cache_win", [DEPTH, NS, cfg.WB, 256], F32, "ExternalInput")
        I['ssm_re'] = d("ssm_re", [DEPTH, NS, NSG, SST], F32, "ExternalInput")
        I['ssm_im'] = d("ssm_im", [DEPTH, NS, NSG, SST], F32, "ExternalInput")
        I['cache_mem'] = d("cache_mem", [DEPTH, NS, NMEM, 1024], F32, "ExternalInput")
        I['page_table'] = d("page_table", [NS, cfg.NPAGE], I32, "ExternalInput")
        I['mem_prompt'] = d("mem_prompt", [NMEM, D], F32, "ExternalInput")
        I['norm1_g'] = d("norm1_g", [DEPTH, D], F32, "ExternalInput")
        I['w_in'] = d("w_in", [DEPTH, D, IN_W], F32, "ExternalInput")
        I['cmp_pe'] = d("cmp_pe", [DEPTH, 2, 32, 64], F32, "ExternalInput")
        I['cmp_w1'] = d("cmp_w1", [DEPTH, 2, 32, 64, 64], F32, "ExternalInput")
        I['cmp_w2'] = d("cmp_w2", [DEPTH, 2, 64, 64], F32, "ExternalInput")
        I['ssm_a_re'] = d("ssm_a_re", [DEPTH, NSG, SST], F32, "ExternalInput")
        I['ssm_a_im'] = d("ssm_a_im", [DEPTH, NSG, SST], F32, "ExternalInput")
        I['ssm_log_dt'] = d("ssm_log_dt", [DEPTH, NSG], F32, "ExternalInput")
        I['ssm_b_re'] = d("ssm_b_re", [DEPTH, NSG, SST, 16], F32, "ExternalInput")
        I['ssm_b_im'] = d("ssm_b_im", [DEPTH, NSG, SST, 16], F32, "ExternalInput")
        I['ssm_c_re'] = d("ssm_c_re", [DEPTH, NSG, 16, SST], F32, "ExternalInput")
        I['ssm_c_im'] = d("ssm_c_im", [DEPTH, NSG, 16, SST], F32, "ExternalInput")
        I['ssm_d'] = d("ssm_d", [DEPTH, NSG * 16], F32, "ExternalInput")
        I['w_glu'] = d("w_glu", [DEPTH, 512, 512], F32, "ExternalInput")
        I['b_glu'] = d("b_glu", [DEPTH, 512], F32, "ExternalInput")
        I['mem_norm_g'] = d("mem_norm_g", [DEPTH, D], F32, "ExternalInput")
        I['w_mem_kv'] = d("w_mem_kv", [DEPTH, D, 1024], F32, "ExternalInput")
        I['w_o'] = d("w_o", [DEPTH, MIXW, D], F32, "ExternalInput")
        I['norm2_g'] = d("norm2_g", [DEPTH, D], F32, "ExternalInput")
        I['w_ff1'] = d("w_ff1", [DEPTH, D, DFF], F32, "ExternalInput")
        I['w_ff2'] = d("w_ff2", [DEPTH, DFF, D], F32, "ExternalInput")
        I['final_norm_g'] = d("final_norm_g", [1, D], F32, "ExternalInput")
        O = {}
        O['y_prompt'] = d("y_prompt", [T, D], F32, "ExternalOutput")
        O['y_sample'] = d("y_sample", [NSTOK, D], F32, "ExternalOutput")
        O['cmp_p'] = d("cmp_p", [DEPTH, T, 256], F32, "ExternalOutput")
        O['cmp_s'] = d("cmp_s", [DEPTH, NSTOK, 256], F32, "ExternalOutput")
        O['slc_p'] = d("slc_p", [DEPTH, T, 256], F32, "ExternalOutput")
        O['slc_s'] = d("slc_s", [DEPTH, NSTOK, 256], F32, "ExternalOutput")
        O['win_p'] = d("win_p", [DEPTH, cfg.WP, 256], F32, "ExternalOutput")
        O['win_s'] = d("win_s", [DEPTH, NS, cfg.WB, 256], F32, "ExternalOutput")
        O['hr_p'] = d("hr_p", [DEPTH, NSG, SST], F32, "ExternalOutput")
        O['hi_p'] = d("hi_p", [DEPTH, NSG, SST], F32, "ExternalOutput")
        O['hr_s'] = d("hr_s", [DEPTH, NS, NSG, SST], F32, "ExternalOutput")
        O['hi_s'] = d("hi_s", [DEPTH, NS, NSG, SST], F32, "ExternalOutput")
        O['mem_p'] = d("mem_p", [DEPTH, NMEM, 1024], F32, "ExternalOutput")
        self.I, self.O = I, O
        X = {}
        X['x1'] = d("scr_x1", [NTT * P, D], F32)
        X['x2'] = d("scr_x2", [NTT * P, D], F32)
        X['qT'] = d("scr_qT", [NTT, P, 512], BF16)
        X['kTs'] = d("scr_kTs", [NTT, P, P], BF16)
        X['kTw'] = d("scr_kTw", [NTT, P, P], BF16)
        X['cTk'] = d("scr_cTk", [NTT, P, P], BF16)
        X['cTv'] = d("scr_cTv", [NTT, P, P], BF16)
        X['vs'] = d("scr_vs", [NTT, P, 130], BF16)
        X['vw'] = d("scr_vw", [NTT, P, 130], BF16)
        X['uT'] = d("scr_uT", [NTT, P, 512], BF16)
        X['mqT'] = d("scr_mqT", [NTT, P, 512], BF16)
        X['gb'] = d("scr_gb", [NTT, P, 1024], BF16)
        X['gbT'] = d("scr_gbT", [NTT, P, 512], BF16)
        X['gn'] = d("scr_gn", [NTT, P, 24], F32)
        X['kaug'] = d("scr_kaug", [3, T], BF16)
        NCS = cfg.PAST // 16 - 1
        NCSL = (NCS + P - 1) // P
        X['sck'] = d("scr_sck", [NS, NG, 64, NCSL * P], BF16)
        X['scv'] = d("scr_scv", [NS, NCSL, P, P], BF16)
        self.X = X

        self.S = Sched(nc)
        with ExitStack() as top:
            self.consts(top)
            import os as _os
            stop = _os.environ.get("KSTOP", "")
            for l in range(DEPTH):
                self.phase_a(l)
                self.S.barrier()
                if stop == f"a{l}":
                    break
                self.phase_b(l)
                self.S.barrier()
                if stop == f"b{l}":
                    break
                self.phase_c(l)
                self.S.barrier()
                if stop == f"c{l}":
                    break
            self.S.finish()
        return nc

    def consts(self, st):
        S = self.S
        self.ident_b = self.sb(st, "ident_b", [P, P], BF16)
        self.ident_f = self.sb(st, "ident_f", [P, P], F32)
        it = self.sb(st, "iota_tmp", [P, P], I32)
        self.iota_jp = it
        S.op('pool', lambda e: e.iota(it[:], pattern=[[1, P]], base=0, channel_multiplier=-1), (), ['iota_tmp'])
        self.ts('dve', self.ident_b[:], it[:], 0, None, ALU.is_equal, None, ['iota_tmp'], ['ident_b'])
        self.ts('dve', self.ident_f[:], it[:], 0, None, ALU.is_equal, None, ['iota_tmp'], ['ident_f'])
        self.fill0 = self.nc.gpsimd.to_reg(0.0)
        self.fillm1 = self.nc.gpsimd.to_reg(-1.0)
        self.ones_f = self.sb(st, "ones_f", [P, P], F32)
        self.memset('dve', self.ones_f[:], 1.0, ['ones_f'])
        self.nmax = self.sb(st, "nmax", [P, 16], F32)
        self.gtile = self.sb(st, "gtile", [P, D], F32)

    def rmsnorm(self, st_bufs, x, xkey, hout, hkey, gkey):
        junk, ss, rstd = st_bufs
        self.act(junk[:], x, AF.Square, [xkey], ['nrm_junk', 'nrm_ss'], accum_out=ss[:, 0:1])
        self.act(rstd[:, 0:1], ss[:, 0:1], AF.Sqrt, ['nrm_ss'], ['nrm_rstd'], scale=1.0 / D, bias=EPS)
        self.S.op('dve', lambda e: e.reciprocal(out=rstd[:, 1:2], in_=rstd[:, 0:1]), ['nrm_rstd'], ['nrm_rstd2'])
        self.stt(hout, x, rstd[:, 1:2], self.gtile[:], ALU.mult, ALU.mult, [xkey, 'nrm_rstd2', gkey], [hkey])

    def load_gain(self, g_ap_row):
        self.ld(self.gtile[:], g_ap_row.partition_broadcast(P), (), ['gtile'])

    def phase_a(self, l):
        cfg, S, I, O, X = self.cfg, self.S, self.I, self.O, self.X
        NT = cfg.NT
        with ExitStack() as st:
            win = self.sb(st, "w_in", [P, 8, IN_W], BF16)
            for k in range(8):
                self.ld(win[:, k, :], I['w_in'][l, k * P:(k + 1) * P, :], (), [f'w_in{k}'], q='pool')
            self.S.barrier()
            self.load_gain(I['norm1_g'][l:l + 1, :])
            xt = [self.sb(st, f"xt{i}", [P, D], F32) for i in range(2)]
            hb = [self.sb(st, f"hb{i}", [P, D], BF16) for i in range(2)]
            hT = [self.sb(st, f"hT{i}", [P, D], BF16) for i in range(2)]
            junk = self.sb(st, "junk", [P, D], BF16)
            ss = self.sb(st, "ss", [P, 2], F32)
            rstd = self.sb(st, "rstd", [P, 2], F32)
            kvst = [self.sb(st, f"kvst{i}", [P, 768], F32) for i in range(2)]
            qb = self.sb(st, "qb", [P, 512], BF16)
            sq = self.sb(st, "sq", [P, 512], F32)
            n2 = self.sb(st, "n2", [P, 16], F32)
            gnst = self.sb(st, "gnst", [P, 24], F32)
            kvb = self.sb(st, "kvb", [P, 768], BF16)
            tsb = [self.sb(st, f"tsb{i}", [P, 1024], BF16) for i in range(2)]
            vst = [self.sb(st, f"vst{i}", [P, 2, 2, 65], BF16) for i in range(2)]
            ub = self.sb(st, "ub", [P, 512], BF16)
            mqb = self.sb(st, "mqb", [P, 512], BF16)
            gbb = [self.sb(st, f"gbb{i}", [P, MIXW], BF16) for i in range(2)]
            pz = [self.ps(st, f"pz{i}", [P, 512], F32) for i in range(3)]
            pt = [self.ps(st, f"pt{i}", [P, 1024], BF16) for i in range(2)]
            for i in range(2):
                self.memset('pool', vst[i][:], 1.0, [f'vst{i}'])
            if l == 0:
                self.memset('dve', self.nmax[:], 0.0, ['nmax'])
            else:
                self.memset('dve', self.nmax[:], 0.0, ['nmax'])
            zc = 0
            ptc = 0
            for t in range(NT + 1):
                if KTILES and t >= KTILES:
                    continue
                b = t % 2
                samp = (t == NT)
                rows = cfg.NSTOK if samp else P
                if l == 0:
                    if samp:
                        self.memset('pool', xt[b][:], 0.0, [f'xt{b}'])
                        self.ld(xt[b][0:rows, :], I['x_sample'][:, :], (), [f'xt{b}'])
                    else:
                        self.ld(xt[b][:], I['x_prompt'][t * P:(t + 1) * P, :], (), [f'xt{b}'])
                else:
                    self.ld(xt[b][:], X['x2'][t * P:(t + 1) * P, :], ['x2'], [f'xt{b}'])
                self.rmsnorm((junk, ss, rstd), xt[b][:], f'xt{b}', hb[b][:], f'hb{b}', 'gtile')
                if KCUT == 1:
                    continue
                pb = pt[ptc % 2]; pk = f'pt{ptc % 2}'; ptc += 1
                for k in range(8):
                    self.tr(pb[:, k * P:(k + 1) * P], hb[b][:, k * P:(k + 1) * P], self.ident_b[:], [f'hb{b}', 'ident_b'], [pk])
                self.cp('act', hT[b][:], pb[:], [pk], [f'hT{b}'])

                def proj(c0, w):
                    nonlocal zc
                    z = pz[zc % 3]; zk = f'pz{zc % 3}'; zc += 1
                    for k in range(8):
                        self.mm(z[:, 0:w], hT[b][:, k * P:(k + 1) * P], win[:, k, c0:c0 + w], k == 0, k == 7,
                                [f'hT{b}', f'w_in{k}'], [zk])
                    return z, zk

                if KCUT == 2:
                    continue
                z, zk = proj(0, 512)
                self.cp('act', qb[:].rearrange("p (h g d) -> p h g d", h=HPG, g=NG),
                        z[:].rearrange("p (g h d) -> p h g d", g=NG, h=HPG), [zk], ['qb'])
                self.tt('dve', sq[:], qb[:], qb[:], ALU.mult, ['qb'], ['sq'])
                S.op('dve', lambda e: e.tensor_reduce(out=n2[:, 0:8], in_=sq[:].rearrange("p (h d) -> p h d", d=HD),
                                                      op=ALU.add, axis=AX.X), ['sq'], ['n2'])
                pb = pt[ptc % 2]; pk = f'pt{ptc % 2}'; ptc += 1
                for hl in range(HPG):
                    self.tr(pb[:, hl * P:(hl + 1) * P], qb[:, hl * P:(hl + 1) * P], self.ident_b[:], ['qb', 'ident_b'], [pk])
                self.cp('act', tsb[b][:, 0:512], pb[:, 0:512], [pk], [f'tsbq{b}'])
                self.ld(X['qT'][t], tsb[b][:, 0:512], [f'tsbq{b}'], ['XqT'], q='sp')
                if KCUT == 3:
                    continue
                z, zk = proj(512, 512)
                self.cp('act', kvst[b][:, 0:512], z[:], [zk], [f'kvst{b}'])
                if KCUT == 31:
                    continue
                self.cp(KVBENG, kvb[:, 0:512], z[:], [zk], ['kvb'])
                if KCUT == 32:
                    continue
                z2, zk2 = proj(1024, 280)
                self.cp('act', kvst[b][:, 512:768], z2[:, 0:256], [zk2], [f'kvst{b}'])
                self.cp(KVBENG, kvb[:, 512:768], z2[:, 0:256], [zk2], ['kvb'])
                if KCUT == 33:
                    continue
                self.act(gnst[:, :], z2[:, 256:280], AF.Sigmoid, [zk2], ['gnst'])
                self.ld(X['gn'][t], gnst[:, :], ['gnst'], ['Xgn'])
                if KCUT == 34:
                    continue
                if samp:
                    self.ld(O['cmp_s'][l], kvst[b][0:rows, 0:256], [f'kvst{b}'], ['Ocmp_s'], q='sp')
                    self.ld(O['slc_s'][l], kvst[b][0:rows, 256:512], [f'kvst{b}'], ['Oslc_s'], q='sp')
                    for e_ in range(cfg.NS):
                        self.ld(O['win_s'][l, e_, cfg.WB - cfg.S:cfg.WB, :], kvst[b][e_ * cfg.S:(e_ + 1) * cfg.S, 512:768],
                                [f'kvst{b}'], ['Owin_s'], q='sp')
                else:
                    self.ld(O['cmp_p'][l, t * P:(t + 1) * P, :], kvst[b][:, 0:256], [f'kvst{b}'], ['Ocmp_p'], q='sp')
                    self.ld(O['slc_p'][l, t * P:(t + 1) * P, :], kvst[b][:, 256:512], [f'kvst{b}'], ['Oslc_p'], q='sp')
                    r0 = t * P - (cfg.T - cfg.WP)
                    if r0 >= 0:
                        self.ld(O['win_p'][l, r0:r0 + P, :], kvst[b][:, 512:768], [f'kvst{b}'], ['Owin_p'], q='sp')
                if KCUT == 4:
                    continue
                self.tt('dve', sq[:, 0:128], kvb[:, 256:384], kvb[:, 256:384], ALU.mult, ['kvb'], ['sq'])
                self.tt('dve', sq[:, 128:256], kvb[:, 512:640], kvb[:, 512:640], ALU.mult, ['kvb'], ['sq'])
                S.op('dve', lambda e: e.tensor_reduce(out=n2[:, 8:12], in_=sq[:, 0:256].rearrange("p (h d) -> p h d", d=HD),
                                                      op=ALU.add, axis=AX.X), ['sq'], ['n2'])
                pb = pt[ptc % 2]; pk = f'pt{ptc % 2}'; ptc += 1
                for j, c0 in enumerate((0, 128, 256, 512)):
                    self.tr(pb[:, j * P:(j + 1) * P], kvb[:, c0:c0 + 128], self.ident_b[:], ['kvb', 'ident_b'], [pk])
                self.cp('act', tsb[b][:, 512:1024], pb[:, 0:512], [pk], [f'tsbk{b}'])
                self.ld(X['cTk'][t], tsb[b][:, 512:640], [f'tsbk{b}'], ['XcTk'], q='sp')
                self.ld(X['cTv'][t], tsb[b][:, 640:768], [f'tsbk{b}'], ['XcTv'], q='sp')
                self.ld(X['kTs'][t], tsb[b][:, 768:896], [f'tsbk{b}'], ['XkTs'], q='sp')
                self.ld(X['kTw'][t], tsb[b][:, 896:1024], [f'tsbk{b}'], ['XkTw'], q='sp')
                if KCUT == 5:
                    continue
                self.cp('dve', vst[b][:, 0, :, 0:64], kvb[:, 384:512].rearrange("p (g d) -> p g d", g=NG), ['kvb'], [f'vst{b}'])
                self.cp('dve', vst[b][:, 1, :, 0:64], kvb[:, 640:768].rearrange("p (g d) -> p g d", g=NG), ['kvb'], [f'vst{b}'])
                self.ld(X['vs'][t], vst[b][:, 0].rearrange("p g d -> p (g d)"), [f'vst{b}'], ['Xvs'], q='sp')
                self.ld(X['vw'][t], vst[b][:, 1].rearrange("p g d -> p (g d)"), [f'vst{b}'], ['Xvw'], q='sp')
                if KCUT == 6:
                    continue
                z, zk = proj(OFF_U, 512)
                self.cp('act', ub[:], z[:], [zk], ['ub'])
                pb = pt[ptc % 2]; pk = f'pt{ptc % 2}'; ptc += 1
                for j in range(4):
                    self.tr(pb[:, j * P:(j + 1) * P], ub[:, j * P:(j + 1) * P], self.ident_b[:], ['ub', 'ident_b'], [pk])
                z, zk = proj(OFF_MQ, 512)
                self.cp('act', mqb[:], z[:], [zk], ['mqb'])
                self.tt('dve', sq[:], mqb[:], mqb[:], ALU.mult, ['mqb'], ['sq'])
                S.op('dve', lambda e: e.tensor_reduce(out=n2[:, 12:16], in_=sq[:].rearrange("p (h d) -> p h d", d=128),
                                                      op=ALU.add, axis=AX.X), ['sq'], ['n2'])
                self.tt('dve', self.nmax[:], self.nmax[:], n2[:], ALU.max, ['n2', 'nmax'], ['nmax'])
                for j in range(4):
                    self.tr(pb[:, (4 + j) * P:(5 + j) * P], mqb[:, j * P:(j + 1) * P], self.ident_b[:], ['mqb', 'ident_b'], [pk])
                tb2 = tsb[b]
                self.cp('act', hb[b][:], pb[:], [pk], [f'hb{b}'])
                self.ld(X['uT'][t], hb[b][:, 0:512], [f'hb{b}'], ['XuT'], q='sp')
                self.ld(X['mqT'][t], hb[b][:, 512:1024], [f'hb{b}'], ['XmqT'], q='sp')
                if KCUT == 7:
                    continue
                for c in range(3):
                    z, zk = proj(OFF_GB + 512 * c, 512)
                    self.act(gbb[b][:, c * 512:(c + 1) * 512], z[:], AF.Sigmoid, [zk], [f'gbb{b}'])
                self.ld(X['gb'][t][:, 0:512], gbb[b][:, 0:512], [f'gbb{b}'], ['Xgb'], q='sp')
                self.ld(X['gb'][t][:, 512:1024], gbb[b][:, 1024:1536], [f'gbb{b}'], ['Xgb'], q='sp')
                pb = pt[ptc % 2]; pk = f'pt{ptc % 2}'; ptc += 1
                for j in range(4):
                    self.tr(pb[:, j * P:(j + 1) * P], gbb[b][:, 512 + j * P:512 + (j + 1) * P], self.ident_b[:], [f'gbb{b}', 'ident_b'], [pk])
                self.cp('act', tsb[b][:, 0:512], pb[:, 0:512], [pk], [f'tsbq{b}'])
                self.ld(X['gbT'][t], tsb[b][:, 0:512], [f'tsbq{b}'], ['XgbT'], q='sp')

    def gmax_bcast(self, st, src, srck, n, name):
        S = self.S
        pT = self.bk[6]
        pB = self.bk[7]
        vm = self.sb(st, f"gm_vm_{name}", [16, 1], F32)
        dg = self.sb(st, f"gm_dg_{name}", [16, 16], F32)
        out = self.sb(st, f"gm_out_{name}", [P, 16], F32)
        k = f"gm_{name}"
        self.tr(pT[0:n, 0:P], src, self.ident_f[:], [srck, 'ident_f'], ['bk6'])
        S.op('dve', lambda e: e.reduce_max(out=vm[0:n, :], in_=pT[0:n, 0:P], axis=AX.X), ['bk6'], [k + 'vm'])
        self.ts('dve', dg[0:n, 0:n], self.ident_f[0:n, 0:n], vm[0:n, 0:1], None, ALU.mult, None, [k + 'vm', 'ident_f'], [k + 'dg'])
        self.mm(pB[:, 0:n], self.ones_f[0:n, :], dg[0:n, 0:n], True, True, [k + 'dg', 'ones_f'], ['bk7'])
        self.cp('act', out[:, 0:n], pB[:, 0:n], ['bk7'], [k + 'out'])
        return out, k + 'out'

    def phase_b(self, l):
        cfg, S, I, O, X = self.cfg, self.S, self.I, self.O, self.X
        NT = cfg.NT
        with ExitStack() as st:
            self._pb_stack = st
            self.bk = [self.ps(st, f"bk{i}", [P, 512], F32) for i in range(8)]
            self.bkb = [b[:].bitcast(BF16) for b in self.bk]
            gm, gmk = self.gmax_bcast(st, self.nmax[:, 0:16], 'nmax', 16, f"a{l}")
            mem = self.mem_setup(st, l, gm, gmk)
            if 'a' in BRANCHES:
                ns = self.nsa_setup(st, l, gm, gmk)
            if 'b' in BRANCHES:
                ss = self.ssm_setup(st, l)
                sbufs = self.ssm_bufs(st)
                s_init = self.sb(st, "s_init", [P, NSG * cfg.NS], F32)
                s_h1 = self.sb(st, "s_h1", [P, NSG * cfg.NS], F32)
                s_h2 = self.sb(st, "s_h2", [P, NSG * cfg.NS], F32)
                self._st_tiles = (self.sb(st, "s_stT1", [P, P], F32), self.sb(st, "s_stT2", [P, P], F32))
                self.memset('dve', s_init[:], 0.0, ['s_init'])
            wo = self.sb(st, "w_o", [P, 12, D], BF16)
            for k in range(12):
                self.ld(wo[:, k, :], I['w_o'][l, k * P:(k + 1) * P, :], (), [f'w_o{k}'], q='pool')
            self.S.barrier()
            xt = [self.sb(st, f"bx{i}", [P, D], F32) for i in range(1)] * 2
            mixT = [self.sb(st, f"mixT{i}", [P, 12, P], BF16) for i in range(1)] * 2
            gbt = [self.sb(st, f"gbt{i}", [P, 1024], BF16) for i in range(1)] * 2
            mixc = self.sb(st, "mixc", [P, 512], BF16)
            gates_s = self.sb(st, "gates_s", [4, 1024], BF16)
            gnt_s = self.sb(st, "gnt_s", [4, 24], F32)
            gnt_p = self.sb(st, "gnt_p", [P, 24], F32)
            py = [self.bk[5], self.bk[6]]
            for t in range(NT + 1):
                b = 0
                samp = t == NT
                rows = cfg.NSTOK if samp else P
                if l == 0:
                    if samp:
                        self.memset('pool', xt[b][:], 0.0, [f'bx{b}'])
                        self.ld(xt[b][0:rows, :], I['x_sample'][:, :], (), [f'bx{b}'])
                    else:
                        self.ld(xt[b][:], I['x_prompt'][t * P:(t + 1) * P, :], (), [f'bx{b}'])
                else:
                    self.ld(xt[b][:], X['x2'][t * P:(t + 1) * P, :], ['x2'], [f'bx{b}'])
                self.memset('pool', mixT[b][:], 0.0, [f'mixT{b}'])
                if 'b' in BRANCHES and KSSM != 'setup' and not (KSSMT == 'p' and samp) and not (KSSMT == 's' and not samp):
                    if samp:
                        self.ssm_sample_init(ss, l, s_init, 's_init', s_h1, s_h2)
                        self.ssm_chunk(ss, sbufs, l, t, 4, cfg.NS, s_init, 's_init', X['gbT'][t], mixT[b], f'mixT{b}', True)
                    else:
                        self.ssm_chunk(ss, sbufs, l, t, P, 1, s_init, 's_init', X['gbT'][t], mixT[b], f'mixT{b}', True if t == NT - 1 else None)
                if samp:
                    for e_ in range(cfg.NS):
                        S4 = cfg.S
                        self.ld(gates_s[:, :], X['gb'][t][e_ * S4:(e_ + 1) * S4, :], ['Xgb'], ['gates_s'])
                        self.ld(gnt_s[:, :], X['gn'][t][e_ * S4:(e_ + 1) * S4, :], ['Xgn'], ['gnt_s'])
                        if 'c' in BRANCHES:
                            self.mem_sample(mem, l, e_, None, None, gm, gmk)
                            self.mem_attend(mem, t, S4, X['mqT'][t].rearrange("p (h q) -> p h q", h=4)[:, :, e_ * S4:(e_ + 1) * S4],
                                            gates_s[0:S4, 512:1024], 'gates_s', mixc, mem['mkT_s'], mem['mv_s'], 'mkv_s', mem['negM_s'])
                            pb = self.bkb[7]; pk = 'bk7'
                            for j in range(4):
                                self.tr(pb[:, j * P:j * P + S4], mixc[0:S4, j * P:(j + 1) * P], self.ident_b[0:S4, 0:S4], ['mixc', 'ident_b'], [pk])
                            self.cp('act', mixT[b][:, 8:12, e_ * S4:(e_ + 1) * S4], pb[:, 0:512].rearrange("p (k t) -> p k t", k=4)[:, :, 0:S4], [pk], [f'mixT{b}'])
                        if 'a' in BRANCHES and KNSA != 'p':
                            self.nsa_sample(ns, l, e_, gnt_s, gates_s, mixT[b], f'mixT{b}')
                if not samp:
                    self.ld(gbt[b][:], X['gb'][t], ['Xgb'], [f'gbt{b}'])
                    if 'a' in BRANCHES:
                        self.ld(gnt_p[:, :], X['gn'][t], ['Xgn'], ['gnt_p'])
                        self.nsa_tile(ns, l, t, gnt_p, gbt[b][:, 0:512], f'gbt{b}', mixT[b], f'mixT{b}')
                    if 'c' in BRANCHES:
                        self.mem_attend(mem, t, P, X['mqT'][t], gbt[b][:, 512:1024], f'gbt{b}', mixc, mem['mkT'], mem['mv'], 'mkv', mem['negM'])
                        pb = self.bkb[7]; pk = 'bk7'
                        for j in range(4):
                            self.tr(pb[:, j * P:(j + 1) * P], mixc[:, j * P:(j + 1) * P], self.ident_b[:], ['mixc', 'ident_b'], [pk])
                        self.cp('act', mixT[b][:, 8:12, :], pb[:, 0:512].rearrange("p (k t) -> p k t", k=4), [pk], [f'mixT{b}'])
                for c in range(2):
                    z = py[c]; zk = f'bk{5 + c}'
                    for k in range(12):
                        self.mm(z[:, :], mixT[b][:, k, :], wo[:, k, c * 512:(c + 1) * 512], k == 0, k == 11, [f'mixT{b}', f'w_o{k}'], [zk])
                    self.tt('dve', xt[b][:, c * 512:(c + 1) * 512], z[:, :], xt[b][:, c * 512:(c + 1) * 512], ALU.add, [zk, f'bx{b}'], [f'bx{b}'])
                self.ld(X['x1'][t * P:(t + 1) * P, :], xt[b][:], [f'bx{b}'], ['x1'], q='sp')


    def ssm_setup(self, st, l):
        cfg, S, I = self.cfg, self.S, self.I
        NS = cfg.NS
        nc = self.nc
        T_ = {}
        TAU = 129
        cos = self.sb(st, "s_cos", [P, NSG, TAU], BF16)
        sin = self.sb(st, "s_sin", [P, NSG, TAU], BF16)
        cc = self.sb(st, "s_cc", [P, 8, NSG], F32)
        sm = self.sb(st, "s_small", [P, 16, NSG], F32)
        WB = [self.sb(st, f"s_WB{i}", [P, 4, 8, P], BF16) for i in range(2)]
        WC = [self.sb(st, f"s_WC{i}", [P, 4, 8, P], BF16) for i in range(2)]
        dcol = self.sb(st, "s_dcol", [P, 4], F32)
        bcol = self.sb(st, "s_bcol", [P, 4], F32)
        pswap = self.sb(st, "s_pswap", [P, P], F32)
        wglu = self.sb(st, "s_wglu", [P, 4, 512], BF16)
        for k in range(4):
            self.ld(wglu[:, k, :], I['w_glu'][l, k * P:(k + 1) * P, :], (), [f's_wglu{k}'], q='pool')
        S.barrier()
        ARE, AIM, DT, MAG, TURN, ABR, ABI, FR, FI, F2, F4, CRL, CIL, T1, T2, T3 = [sm[:, i, :] for i in range(16)]
        with nc.allow_non_contiguous_dma(reason="small parameter loads"):
            for h in range(2):
                self.ld(sm[h * 64:(h + 1) * 64, 0, :], I['ssm_a_re'][l].rearrange("g n -> n g"), (), ['s_sm'])
                self.ld(sm[h * 64:(h + 1) * 64, 1, :], I['ssm_a_im'][l].rearrange("g n -> n g"), (), ['s_sm'])
            self.ld(sm[:, 2, :], I['ssm_log_dt'][l:l + 1, :].partition_broadcast(P), (), ['s_sm'])
            self.ld(dcol[:], I['ssm_d'][l].rearrange("(j p) -> p j", p=P), (), ['s_dcol'])
            self.ld(bcol[:], I['b_glu'][l].rearrange("(j p) -> p j", p=P), (), ['s_bcol'])
        K = ['s_sm']
        self.act(DT, DT, AF.Exp, K, K)
        self.tt('dve', T1, DT, ARE, ALU.mult, K, K)
        self.act(MAG, T1, AF.Exp, K, K)
        self.tt('dve', TURN, DT, AIM, ALU.mult, K, K)
        self.ts('dve', TURN, TURN, 1.0 / (2.0 * math.pi), None, ALU.mult, None, K, K)
        TSEL = (1, 3, 127, 128)
        with ExitStack() as s2:
            GC = 8
            ti = self.sb(s2, "s_ti", [P, GC, TAU], I32)
            arg = self.sb(s2, "s_arg", [P, GC, TAU], F32)
            tf = self.sb(s2, "s_tf", [P, GC, TAU], F32)
            tau_i = self.sb(s2, "s_taui", [P, TAU], I32)
            tau = self.sb(s2, "s_tau", [P, TAU], F32)
            S.op('pool', lambda e: e.iota(tau_i[:], pattern=[[1, TAU]], base=0, channel_multiplier=0), (), ['s_taui'])
            self.cp('dve', tau[:], tau_i[:], ['s_taui'], ['s_tau'])

            def wrap(x, xk):
                self.cp('dve', ti[:], x, [xk], ['s_ti'])
                self.cp('dve', tf[:], ti[:], ['s_ti'], ['s_tf'])
                self.tt('dve', x, x, tf[:], ALU.subtract, [xk, 's_tf'], [xk])
                self.ts('dve', tf[:], x, 0.5, None, ALU.is_gt, None, [xk], ['s_tf'])
                self.tt('dve', x, x, tf[:], ALU.subtract, [xk, 's_tf'], [xk])
                self.ts('dve', tf[:], x, -0.5, None, ALU.is_lt, None, [xk], ['s_tf'])
                self.tt('dve', x, x, tf[:], ALU.add, [xk, 's_tf'], [xk])
            for g0 in range(0, NSG, GC):
                self.tt('dve', arg[:], TURN[:, g0:g0 + GC].unsqueeze(2).to_broadcast([P, GC, TAU]), tau[:].unsqueeze(1).to_broadcast([P, GC, TAU]),
                        ALU.mult, K + ['s_tau'], ['s_arg'])
                wrap(arg[:], 's_arg')
                self.act(tf[:], arg[:], AF.Sin, ['s_arg'], ['s_tf'], scale=2.0 * math.pi)
                self.cp('dve', sin[:, g0:g0 + GC, :], tf[:], ['s_tf'], ['s_sin'])
                for i_, tt_ in enumerate(TSEL):
                    self.cp('dve', cc[:, 4 + i_, g0:g0 + GC], tf[:, :, tt_], ['s_tf'], ['s_cc'])
                self.ts('dve', arg[:], arg[:], 0.25, None, ALU.add, None, ['s_arg'], ['s_arg'])
                wrap(arg[:], 's_arg')
                self.act(tf[:], arg[:], AF.Sin, ['s_arg'], ['s_tf'], scale=2.0 * math.pi)
                self.cp('dve', cos[:, g0:g0 + GC, :], tf[:], ['s_tf'], ['s_cos'])
                for i_, tt_ in enumerate(TSEL):
                    self.cp('dve', cc[:, i_, g0:g0 + GC], tf[:, :, tt_], ['s_tf'], ['s_cc'])
            S.barrier()
        CS = ['s_cos', 's_sin']
        self.tt('dve', ABR, MAG, cc[:, 0, :], ALU.mult, K + ['s_cc'], K)
        self.tt('dve', ABI, MAG, cc[:, 4, :], ALU.mult, K + ['s_cc'], K)
        self.tt('dve', T1, ARE, ARE, ALU.mult, K, K)
        self.tt('dve', T2, AIM, AIM, ALU.mult, K, K)
        self.tt('dve', T1, T1, T2, ALU.add, K, K)
        S.op('dve', lambda e: e.reciprocal(out=T1, in_=T1), K, K)
        self.ts('dve', T2, ABR, -1.0, None, ALU.add, None, K, K)
        self.tt('dve', FR, T2, ARE, ALU.mult, K, K)
        self.tt('dve', T3, ABI, AIM, ALU.mult, K, K)
        self.tt('dve', FR, FR, T3, ALU.add, K, K)
        self.tt('dve', FR, FR, T1, ALU.mult, K, K)
        self.tt('dve', FI, ABI, ARE, ALU.mult, K, K)
        self.tt('dve', T3, T2, AIM, ALU.mult, K, K)
        self.tt('dve', FI, FI, T3, ALU.subtract, K, K)
        self.tt('dve', FI, FI, T1, ALU.mult, K, K)
        self.ts('dve', sm[0:64, 9, :], sm[0:64, 8, :], -1.0, None, ALU.mult, None, K, K)
        self.cp('dve', sm[64:128, 9, :], sm[64:128, 8, :], K, K)
        self.cp('dve', sm[0:64, 10, :], sm[0:64, 7, :], K, K)
        self.ts('dve', sm[64:128, 10, :], sm[64:128, 7, :], -1.0, None, ALU.mult, None, K, K)
        self.ts('dve', pswap[:], self.iota_jp[:], 64, None, ALU.is_equal, None, ['iota_tmp'], ['s_pswap'])
        with ExitStack() as s2:
            tmpf = self.sb(s2, "s_tmpf", [P, P], F32)
            self.ts('dve', tmpf[:], self.iota_jp[:], -64, None, ALU.is_equal, None, ['iota_tmp'], ['s_tmpf'])
            self.tt('dve', pswap[:], pswap[:], tmpf[:], ALU.subtract, ['s_tmpf', 's_pswap'], ['s_pswap'])
            bn1 = self.sb(s2, "s_bn1", [P, NSG, 16], F32)
            bn2 = self.sb(s2, "s_bn2", [P, NSG, 16], F32)
            bb = self.sb(s2, "s_bb", [P, NSG, 16], F32)
            bt = self.sb(s2, "s_bt", [P, NSG, 16], F32)
            with nc.allow_non_contiguous_dma(reason="small parameter loads"):
                self.ld(bn1[0:64], I['ssm_b_re'][l].rearrange("g n c -> n g c"), (), ['s_bn1'])
                self.ld(bn1[64:128], I['ssm_b_im'][l].rearrange("g n c -> n g c"), (), ['s_bn1'])
                self.ld(bn2[0:64], I['ssm_b_im'][l].rearrange("g n c -> n g c"), (), ['s_bn2'])
                self.ld(bn2[64:128], I['ssm_b_re'][l].rearrange("g n c -> n g c"), (), ['s_bn2'])
            rm = self.sb(s2, "s_rm", [P, 8], F32)
            rmi = self.sb(s2, "s_rmi", [P, 8], I32)
            S.op('pool', lambda e: e.iota(rmi[:], pattern=[[-16, 8]], base=0, channel_multiplier=1), (), ['s_rmi'])
            rm2 = self.sb(s2, "s_rm2", [P, 8], F32)
            self.ts('dve', rm[:], rmi[:], 0, None, ALU.is_ge, None, ['s_rmi'], ['s_rm'])
            self.ts('dve', rm2[:], rmi[:], 16, None, ALU.is_lt, None, ['s_rmi'], ['s_rm2'])
            self.tt('dve', rm[:], rm[:], rm2[:], ALU.mult, ['s_rm', 's_rm2'], ['s_rm'])
            bc = lambda a: a.unsqueeze(2).to_broadcast([P, NSG, 16])
            for tbl, (Fa, Na, Fb, Nb) in enumerate(((FR, bn1, F2, bn2), (F4, bn2, FI, bn1))):
                self.tt('dve', bb[:], bc(Fa), Na[:], ALU.mult, K + ['s_bn1', 's_bn2'], ['s_bb'])
                self.tt('dve', bt[:], bc(Fb), Nb[:], ALU.mult, K + ['s_bn1', 's_bn2'], ['s_bt'])
                self.tt('dve', bb[:], bb[:], bt[:], ALU.add, ['s_bb', 's_bt'], ['s_bb'])
                for j in range(4):
                    pT = self.bk[j % 2]
                    self.tr(pT[:, 0:P], bb[:, 8 * j:8 * j + 8, :].rearrange("p g c -> p (g c)"), self.ident_f[:], ['s_bb', 'ident_f'], [f'bk{j % 2}'])
                    self.tt('dve', WB[tbl][:, j, :, :], pT[:, 0:P].unsqueeze(1).to_broadcast([P, 8, P]),
                            rm[:].unsqueeze(2).to_broadcast([P, 8, P]), ALU.mult, [f'bk{j % 2}', 's_rm'], [f's_WB{tbl}'])
            cn = self.sb(s2, "s_cn", [P, 4, P], F32)
            cm_i = self.sb(s2, "s_cmi", [P, 8, P], I32)
            cm = self.sb(s2, "s_cm", [P, 8, P], F32)
            cm2 = self.sb(s2, "s_cm2", [P, 8, P], F32)
            S.op('pool', lambda e: e.iota(cm_i[:], pattern=[[-16, 8], [1, P]], base=0, channel_multiplier=0), (), ['s_cmi'])
            self.ts('dve', cm[:], cm_i[:], 0, None, ALU.is_ge, None, ['s_cmi'], ['s_cm'])
            self.ts('dve', cm2[:], cm_i[:], 16, None, ALU.is_lt, None, ['s_cmi'], ['s_cm2'])
            self.tt('dve', cm[:], cm[:], cm2[:], ALU.mult, ['s_cm', 's_cm2'], ['s_cm'])
            cre = I['ssm_c_re'][l].rearrange("(j gg) c n -> (gg c) j n", j=4)
            cim = I['ssm_c_im'][l].rearrange("(j gg) c n -> (gg c) j n", j=4)
            for tbl in range(2):
                with nc.allow_non_contiguous_dma(reason="small parameter loads"):
                    if tbl == 0:
                        self.ld(cn[:, :, 0:64], cre, (), ['s_cn'])
                        self.ld(cn[:, :, 64:128], cim, (), ['s_cn'])
                        self.ts('dve', cn[:, :, 64:128], cn[:, :, 64:128], -1.0, None, ALU.mult, None, ['s_cn'], ['s_cn'])
                    else:
                        self.ld(cn[:, :, 0:64], cim, (), ['s_cn'])
                        self.ld(cn[:, :, 64:128], cre, (), ['s_cn'])
                        self.ts('dve', cn[:], cn[:], -1.0, None, ALU.mult, None, ['s_cn'], ['s_cn'])
                for j in range(4):
                    pT = self.bk[j % 2]
                    self.tr(pT[:, 0:P], cn[:, j, :], self.ident_f[:], ['s_cn', 'ident_f'], [f'bk{j % 2}'])
                    self.tt('dve', WC[tbl][:, j, :, :], pT[:, 0:P].unsqueeze(1).to_broadcast([P, 8, P]), cm[:], ALU.mult,
                            [f'bk{j % 2}', 's_cm'], [f's_WC{tbl}'])
            S.barrier()
        return dict(cos=cos, sin=sin, cc=cc, sm=sm, WB=WB, WC=WC, dcol=dcol, bcol=bcol, pswap=pswap, wglu=wglu)

    def ssm_chunk(self, ss, bufs, l, t, L, NSEG, init, initk, gbT_src, mixT_dst, mixk, final):
        cfg, S, I, O, X = self.cfg, self.S, self.I, self.O, self.X
        W = NSEG * L
        uT, gin, gout, G12, t1, t2, gend, cst, yt, yg, sg, gbTt = bufs
        sm = ss['sm']
        ABR, ABI, CRL, CIL = sm[:, 5, :], sm[:, 6, :], sm[:, 11, :], sm[:, 12, :]
        cos, sin = ss['cos'], ss['sin']
        MAG = sm[:, 3, :]
        self.ld(uT[:, :, 0:W], X['uT'][t].rearrange("p (j q) -> p j q", j=4)[:, :, 0:W], ['XuT'], ['s_uT'])
        self.ld(gbTt[:, :, 0:W], gbT_src.rearrange("p (j q) -> p j q", j=4)[:, :, 0:W], ['XgbT'], ['s_gbT'])

        def tabv(tab, g0, ng, lo, n):
            v = tab[:, g0:g0 + ng, lo:lo + n]
            if NSEG == 1:
                return v
            return v.unsqueeze(2).to_broadcast([P, ng, NSEG, n])

        def dv(buf, g0, ng):
            v = buf[:, (g0 % 8) * W:(g0 % 8 + ng) * W]
            if NSEG == 1:
                return v.rearrange("p (g q) -> p g q", g=ng)
            return v.rearrange("p (g e q) -> p g e q", g=ng, e=NSEG)

        for half in range(4):
            for blk in range(2):
                g0 = half * 8 + blk * 4
                A = self.bk[(blk % 2) * 2]; Ak = f'bk{(blk % 2) * 2}'
                B = self.bk[(blk % 2) * 2 + 1]; Bk = f'bk{(blk % 2) * 2 + 1}'
                for gi in range(4):
                    g = g0 + gi
                    j, gg = g // 8, g % 8
                    self.mm(A[:, gi * W:(gi + 1) * W], ss['WB'][0][:, j, gg, :], uT[:, j, 0:W], True, True, ['s_uT', 's_WB0'], [Ak])
                    self.mm(B[:, gi * W:(gi + 1) * W], ss['WB'][1][:, j, gg, :], uT[:, j, 0:W], True, True, ['s_uT', 's_WB1'], [Bk])
                if KSSM == 'c05':
                    continue
                sh = (lambda v: v.rearrange("p (g q) -> p g q", g=4)) if NSEG == 1 else (lambda v: v.rearrange("p (g e q) -> p g e q", g=4, e=NSEG))
                self.tt('dve', sh(t1[:, 0:4 * W]), sh(A[:, 0:4 * W]), tabv(cos, g0, 4, 0, L), ALU.mult, [Ak, 's_cos'], ['s_t1'])
                self.tt('dve', sh(t2[:, 0:4 * W]), sh(B[:, 0:4 * W]), tabv(sin, g0, 4, 0, L), ALU.mult, [Bk, 's_sin'], ['s_t2'])
                self.tt('pool', gin[:, (g0 % 8) * W:(g0 % 8 + 4) * W], t1[:, 0:4 * W], t2[:, 0:4 * W], ALU.add, ['s_t1', 's_t2'], ['s_gin'])
            if KSSM in ('c1', 'c05'):
                continue
            h0 = half * 8
            for gl in range(8):
                g = h0 + gl
                for e_ in range(NSEG):
                    c0 = (gl * NSEG + e_) * L
                    S.op('dve', lambda e, c0=c0, g=g, e_=e_: e.tensor_tensor_scan(
                        out=gout[:, c0:c0 + L], data0=MAG[:, g:g + 1].to_broadcast([P, L]), data1=gin[:, c0:c0 + L],
                        initial=init[:, g * NSEG + e_:g * NSEG + e_ + 1], op0=ALU.mult, op1=ALU.add), ['s_gin', 's_sm', initk], ['s_gout'])
            if KSSM == 'c2':
                continue
            self.cp('dve', gend[:, h0 * NSEG:(h0 + 8) * NSEG], gout[:, 0:8 * W].rearrange("p (s q) -> p s q", q=L)[:, :, L - 1], ['s_gout'], ['s_gend'])
            if KSSM == 'c25':
                continue
            self.tt('dve', dv(G12, h0, 8), dv(gout, h0, 8), tabv(cos, h0, 8, 0, L), ALU.mult, ['s_gout', 's_cos'], ['s_G1'])
            self.tt('pool', dv(G12[:, 8 * W:16 * W], h0, 8), dv(gout, h0, 8), tabv(sin, h0, 8, 0, L), ALU.mult, ['s_gout', 's_sin'], ['s_G2'])
            if KSSM == 'c3':
                continue
            for jj in range(1):
                j = half
                pc = self.bk[4 + half % 2]; pck = f'bk{4 + half % 2}'
                for gg in range(8):
                    gl = gg
                    self.mm(pc[:, 0:W], ss['WC'][0][:, j, gg, :], G12[:, gl * W:(gl + 1) * W], gg == 0, False, ['s_G1', 's_WC0'], [pck])
                    self.mm(pc[:, 0:W], ss['WC'][1][:, j, gg, :], G12[:, 8 * W + gl * W:8 * W + (gl + 1) * W], False, gg == 7, ['s_G2', 's_WC1'], [pck])
                self.stt(yt[:, 0:W], uT[:, j, 0:W], ss['dcol'][:, j:j + 1], pc[:, 0:W], ALU.mult, ALU.add, ['s_uT', 's_dcol', pck], ['s_yt'])
                self.tt('dve', sg[:, 0:W], yt[:, 0:W], yt[:, 0:W], ALU.mult, ['s_yt'], ['s_sg'])
                self.ts('dve', sg[:, 0:W], sg[:, 0:W], 0.044715, 1.0, ALU.mult, ALU.add, ['s_sg'], ['s_sg'])
                self.tt('dve', sg[:, 0:W], sg[:, 0:W], yt[:, 0:W], ALU.mult, ['s_sg', 's_yt'], ['s_sg'])
                self.act(sg[:, 0:W], sg[:, 0:W], AF.Sigmoid, ['s_sg'], ['s_sg'], scale=2.0 * math.sqrt(2.0 / math.pi))
                self.tt('dve', yg[:, j, 0:W], sg[:, 0:W], yt[:, 0:W], ALU.mult, ['s_sg', 's_yt'], ['s_yg'])
        if KSSM in ('c05', 'c1', 'c2', 'c25', 'c3', 'c4'):
            return
        for jo in range(4):
            pg = self.bk[6 + jo % 2]; pgk = f'bk{6 + jo % 2}'
            for k in range(4):
                self.mm(pg[:, 0:W], ss['wglu'][:, k, jo * P:(jo + 1) * P], yg[:, k, 0:W], k == 0, k == 3, ['s_yg', f's_wglu{k}'], [pgk])
            self.act(sg[:, 0:W], pg[:, 0:W], AF.Sigmoid, [pgk, 's_bcol'], ['s_sg'], bias=ss['bcol'][:, jo:jo + 1])
            self.tt('dve', sg[:, 0:W], sg[:, 0:W], yg[:, jo, 0:W], ALU.mult, ['s_sg', 's_yg'], ['s_sg'])
            self.tt('dve', mixT_dst[:, 4 + jo, 0:W], sg[:, 0:W], gbTt[:, jo, 0:W], ALU.mult, ['s_sg', 's_gbT'], [mixk])
        if KSSM == 'c5':
            return
        psw = self.bk[0]
        NC = NSG * NSEG
        self.mm(psw[:, 0:NC], ss['pswap'][:], gend[:, 0:NC], True, True, ['s_gend', 's_pswap'], ['bk0'])
        bcs = (lambda a: a) if NSEG == 1 else (lambda a: a.unsqueeze(2).to_broadcast([P, NSG, NSEG]))
        shp = (lambda v: v) if NSEG == 1 else (lambda v: v.rearrange("p (g e) -> p g e", e=NSEG))
        if final is None:
            self.tt('dve', shp(cst[:, 0:NC]), shp(gend[:, 0:NC]), bcs(ss['cc'][:, 3, :]), ALU.mult, ['s_gend', 's_cc'], ['s_cst'])
            self.tt('dve', shp(init[:, 0:NC]), shp(psw[:, 0:NC]), bcs(ss['cc'][:, 7, :]), ALU.mult, ['bk0', 's_cc'], [initk])
            self.tt('dve', init[:, 0:NC], init[:, 0:NC], cst[:, 0:NC], ALU.add, [initk, 's_cst'], [initk])
        else:
            self.tt('dve', shp(cst[:, 0:NC]), shp(gend[:, 0:NC]), bcs(ss['cc'][:, 2 if L == P else 1, :]), ALU.mult, ['s_gend', 's_cc'], ['s_cst'])
            self.tt('dve', shp(gend[:, 0:NC]), shp(psw[:, 0:NC]), bcs(ss['cc'][:, 6 if L == P else 5, :]), ALU.mult, ['bk0', 's_cc'], ['s_gend'])
            self.tt('dve', cst[:, 0:NC], cst[:, 0:NC], gend[:, 0:NC], ALU.add, ['s_gend', 's_cst'], ['s_cst'])
            t1_, t2_ = self._st_tiles
            if NSEG == 1:
                self.tr(self.bk[6][0:NSG, 0:P], cst[:, 0:NSG], self.ident_f[:], ['s_cst', 'ident_f'], ['bk6'])
                self.cp('act', t1_[0:NSG, :], self.bk[6][0:NSG, 0:P], ['bk6'], ['s_stT1'])
                self.ld(O['hr_p'][l], t1_[0:NSG, 0:64], ['s_stT1'], ['Ohr_p'])
                self.ld(O['hi_p'][l], t1_[0:NSG, 64:128], ['s_stT1'], ['Ohi_p'])
            else:
                self.cp('dve', gend[:, 0:NC].rearrange("p (e g) -> p e g", e=NSEG), cst[:, 0:NC].rearrange("p (g e) -> p e g", e=NSEG), ['s_cst'], ['s_gend'])
                self.tr(self.bk[6][0:NC, 0:P], gend[:, 0:NC], self.ident_f[:], ['s_gend', 'ident_f'], ['bk6'])
                self.cp('act', t1_[0:NC, :], self.bk[6][0:NC, 0:P], ['bk6'], ['s_stT1'])
                self.ld(O['hr_s'][l].rearrange("e g n -> (e g) n"), t1_[0:NC, 0:64], ['s_stT1'], ['Ohr_s'])
                self.ld(O['hi_s'][l].rearrange("e g n -> (e g) n"), t1_[0:NC, 64:128], ['s_stT1'], ['Ohi_s'])

    def ssm_bufs(self, st):
        mk = lambda n, shp, dt: self.sb(st, n, shp, dt)
        return (mk("s_uT", [P, 4, P], BF16), mk("s_gin", [P, 8 * P], F32), mk("s_gout", [P, 8 * P], F32), mk("s_G12", [P, 16 * P], BF16),
                mk("s_t1", [P, 512], F32), mk("s_t2", [P, 512], F32), mk("s_gend", [P, NSG * self.cfg.NS], F32),
                mk("s_cst", [P, NSG * self.cfg.NS], F32), mk("s_yt", [P, P], F32), mk("s_yg", [P, 4, P], BF16), mk("s_sg", [P, P], F32),
                mk("s_gbT", [P, 4, P], BF16))

    def ssm_sample_init(self, ss, l, init, initk, h1, h2):
        cfg, S, I = self.cfg, self.S, self.I
        NS = cfg.NS
        sm = ss['sm']
        ABR, ABI = ss['cc'][:, 0, :], ss['cc'][:, 4, :]
        NC = NSG * NS
        t1_, t2_ = self._st_tiles
        re_v = I['ssm_re'][l].rearrange("e g n -> (e g) n")
        im_v = I['ssm_im'][l].rearrange("e g n -> (e g) n")
        self.ld(t1_[0:NC, 0:64], re_v, (), ['s_stT1'])
        self.ld(t1_[0:NC, 64:128], im_v, (), ['s_stT1'])
        self.ld(t2_[0:NC, 0:64], im_v, (), ['s_stT2'])
        self.ld(t2_[0:NC, 64:128], re_v, (), ['s_stT2'])
        for (src, sk, dst, dk, bi) in ((t1_, 's_stT1', h1, 's_h1', 6), (t2_, 's_stT2', h2, 's_h2', 7)):
            self.tr(self.bk[bi][:, 0:NC], src[0:NC, :], self.ident_f[0:NC, 0:NC], [sk, 'ident_f'], [f'bk{bi}'])
            self.cp('act', dst[:, 0:NC].rearrange("p (g e) -> p g e", e=NS), self.bk[bi][:, 0:NC].rearrange("p (e g) -> p g e", e=NS), [f'bk{bi}'], [dk])
        self.ts('dve', h2[0:64, :], h2[0:64, :], -1.0, None, ALU.mult, None, ['s_h2'], ['s_h2'])
        b3 = lambda a: a.unsqueeze(2).to_broadcast([P, NSG, NS])
        v3 = lambda a: a.rearrange("p (g e) -> p g e", e=NS)
        self.tt('dve', v3(h1[:, :]), v3(h1[:, :]), b3(ABR), ALU.mult, ['s_h1', 's_cc'], ['s_h1'])
        self.tt('dve', v3(h2[:, :]), v3(h2[:, :]), b3(ABI), ALU.mult, ['s_h2', 's_cc'], ['s_h2'])
        self.tt('dve', init[:, 0:NSG * NS], h1[:, :], h2[:, :], ALU.add, ['s_h1', 's_h2'], [initk])


    def make_aug_rows(self, st, dsts, N, kind, row0=64):
        S = self.S
        CH = min(N, 1024)
        ii = self.sb(st, f"aug_i_{kind}", [1, 2, CH], I32)
        af = self.sb(st, f"aug_f_{kind}", [1, 3, CH], F32)
        ab = self.sb(st, f"aug_b_{kind}", [1, 3, CH], BF16)
        k = f'aug_{kind}'
        sh, msk = (7, 127) if kind == 'tok' else (3, 7)
        for c0 in range(0, N, CH):
            S.op('pool', lambda e: e.iota(ii[:], pattern=[[0, 2], [1, CH]], base=c0, channel_multiplier=0), (), [k + 'i'])
            S.op('dve', lambda e: e.tensor_scalar(out=ii[:, 0, :], in0=ii[:, 0, :], scalar1=sh, scalar2=None, op0=ALU.logical_shift_right), [k + 'i'], [k + 'i'])
            S.op('dve', lambda e: e.tensor_scalar(out=ii[:, 1, :], in0=ii[:, 1, :], scalar1=msk, scalar2=None, op0=ALU.bitwise_and), [k + 'i'], [k + 'i'])
            self.cp('dve', af[:, 1:3, :], ii[:], [k + 'i'], [k + 'f'])
            if kind == 'tok':
                self.ts('dve', af[:, 2, :], af[:, 2, :], -64.0, None, ALU.add, None, [k + 'f'], [k + 'f'])
            else:
                self.ts('dve', af[:, 2, :], af[:, 2, :], 16.0, 31.0 - 64.0, ALU.mult, ALU.add, [k + 'f'], [k + 'f'])
            self.memset('dve', af[:, 0, :], 1.0, [k + 'f'])
            self.cp('dve', ab[:], af[:], [k + 'f'], [k + 'b'])
            for (dt_, dk) in dsts:
                for r in range(3):
                    self.ld(dt_[row0 + r:row0 + r + 1, c0:c0 + CH], ab[0:1, r, :], [k + 'b'], [dk])

    def nsa_setup(self, st, l, gm, gmk):
        cfg, S, I, O, X = self.cfg, self.S, self.I, self.O, self.X
        nc = self.nc
        T, NT = cfg.T, cfg.NT
        NCT = T // 16 - 1
        NCTL = (NCT + P - 1) // P
        N = {}
        CHK = 8
        KS = [[self.sb(st, f"n_KSc{c}{g}", [67, CHK * P], BF16) for g in range(NG)] for c in range(2)]
        VS = [self.sb(st, f"n_VSc{c}", [P, CHK, 130], BF16) for c in range(2)]
        CK = [self.sb(st, f"n_CK{g}", [67, NCTL * P], BF16) for g in range(NG)]
        CVP = [self.sb(st, f"n_CVP{g}", [P, NCTL, 65], BF16) for g in range(NG)]
        POOLM = self.sb(st, "n_POOLM", [P, NCTL, P], BF16)
        OTS = self.sb(st, "n_OTS", [P, 4, 512], F32)
        QB = [self.sb(st, f"n_QB{g}", [67, 512], BF16) for g in range(NG)]
        QA = [self.sb(st, f"n_QA{g}", [67, 512], BF16) for g in range(NG)]
        ESEL = self.sb(st, "n_ESEL", [P, 32, P], BF16)
        negM = self.sb(st, "n_negM", [P, 2], F32)
        with ExitStack() as s2:
            if l == 0:
                self.make_aug_rows(s2, [(X['kaug'], 'Xkaug')], T, 'tok', row0=0)
            self.make_aug_rows(s2, [(CK[g], f'n_CK{g}') for g in range(NG)], NCTL * P, 'cmp')
            qs = self.sb(s2, "n_qs", [1, 3, 512], F32)
            qsb = self.sb(s2, "n_qsb", [1, 3, 512], BF16)
            sl = slopes()
            for g in range(NG):
                for hl in range(HPG):
                    sv = sl[4 * g + hl]
                    self.memset('dve', qs[:, 0:2, hl * P:(hl + 1) * P], 1024.0 * sv, ['n_qs'])
                    self.memset('dve', qs[:, 2, hl * P:(hl + 1) * P], 8.0 * sv, ['n_qs'])
                self.cp('dve', qsb[:], qs[:], ['n_qs'], ['n_qsb'])
                for r in range(3):
                    self.ld(QB[g][64 + r:65 + r, :], qsb[0:1, r, :], ['n_qsb'], [f'n_QB{g}'])
                    self.ld(QA[g][64 + r:65 + r, :], qsb[0:1, r, :], ['n_qsb'], [f'n_QA{g}'])
                S.barrier()
            ei = self.sb(s2, "n_ei", [P, 32, 2], I32)
            ef = self.sb(s2, "n_ef", [P, 32, 2], F32)
            pi_ = self.sb(s2, "n_pi", [P, 1], I32)
            S.op('pool', lambda e: e.iota(ei[:], pattern=[[2, 32], [1, 2]], base=0, channel_multiplier=0), (), ['n_ei'])
            S.op('pool', lambda e: e.iota(pi_[:], pattern=[[0, 1]], base=0, channel_multiplier=1), (), ['n_pi'])
            S.op('dve', lambda e: e.tensor_scalar(out=pi_[:], in0=pi_[:], scalar1=63, scalar2=None, op0=ALU.bitwise_and), ['n_pi'], ['n_pi'])
            pf = self.sb(s2, "n_pf", [P, 1], F32)
            self.cp('dve', pf[:], pi_[:], ['n_pi'], ['n_pf'])
            self.cp('dve', ef[:], ei[:], ['n_ei'], ['n_ef'])
            self.ts('dve', ef[:], ef[:], pf[:, 0:1], None, ALU.is_equal, None, ['n_ef', 'n_pf'], ['n_ef'])
            self.cp('dve', ESEL[:].rearrange("p r (j k) -> p r j k", j=2), ef[:].unsqueeze(3).to_broadcast([P, 32, 2, 64]), ['n_ef'], ['n_ESEL'])
            w1s = self.sb(s2, "n_w1s", [P, 32, 64], F32)
            W1B = self.sb(s2, "n_W1B", [P, 32, P], BF16)
            w2s = self.sb(s2, "n_w2s", [P, 64], F32)
            W2B = self.sb(s2, "n_W2B", [P, P], BF16)
            W2P = [self.sb(s2, f"n_W2P{g}", [P, 64], BF16) for g in range(NG)]
            pes = self.sb(s2, "n_pes", [P, 32], F32)
            peb = self.sb(s2, "n_peb", [P, 32], BF16)
            bias = self.sb(s2, "n_bias", [P, 1], F32)
            cT = self.sb(s2, "n_cT", [P, T], BF16)
            H1 = self.sb(s2, "n_H1", [P, 512], BF16)
            hx = self.sb(s2, "n_hx", [P, 512], F32)
            hy = self.sb(s2, "n_hy", [P, 512], F32)
            sqk = self.sb(s2, "n_sqk", [64, NCTL * P], BF16)
            ckn = self.sb(s2, "n_ckn", [1, 4], F32)
            ones_b = self.sb(s2, "n_ones_b", [P, 1], BF16)
            self.memset('dve', ones_b[:], 1.0, ['n_ones_b'])
            self.memset('dve', ckn[:], 0.0, ['n_ckn'])
            for g in range(NG):
                self.memset('pool', CVP[g][:], 0.0, [f'n_CVP{g}'])
                self.memset('pool', CK[g][0:64, :], 0.0, [f'n_CK{g}'])
            for kv in range(2):
                src = 'cTk' if kv == 0 else 'cTv'
                self.ld(cT[:].rearrange("p (t k) -> p t k", k=P), X[src][0:NT].rearrange("t p k -> p t k"), ['X' + src], ['n_cT'])
                self.memset('pool', W1B[:], 0.0, ['n_W1B'])
                self.memset('pool', W2B[:], 0.0, ['n_W2B'])
                for g in range(NG):
                    self.memset('pool', W2P[g][:], 0.0, [f'n_W2P{g}'])
                with nc.allow_non_contiguous_dma(reason="small parameter loads"):
                    for g in range(NG):
                        self.ld(w1s[64 * g:64 * g + 64], I['cmp_w1'][l, kv].rearrange("l d e -> d l e"), (), ['n_w1s'])
                        self.ld(w2s[64 * g:64 * g + 64], I['cmp_w2'][l, kv], (), ['n_w2s'])
                        self.ld(pes[64 * g:64 * g + 64], I['cmp_pe'][l, kv].rearrange("l d -> d l"), (), ['n_pes'])
                for g in range(NG):
                    self.cp('dve', W1B[64 * g:64 * g + 64, :, 64 * g:64 * g + 64], w1s[64 * g:64 * g + 64], ['n_w1s'], ['n_W1B'])
                    self.cp('dve', W2B[64 * g:64 * g + 64, 64 * g:64 * g + 64], w2s[64 * g:64 * g + 64], ['n_w2s'], ['n_W2B'])
                    self.cp('dve', W2P[g][64 * g:64 * g + 64, :], w2s[64 * g:64 * g + 64], ['n_w2s'], [f'n_W2P{g}'])
                self.cp('dve', peb[:], pes[:], ['n_pes'], ['n_peb'])
                pb_ = self.bk[7]
                for l_ in range(32):
                    self.mm(pb_[:, 0:1], W1B[:, l_, :], peb[:, l_:l_ + 1], l_ == 0, l_ == 31, ['n_W1B', 'n_peb'], ['bk7'])
                self.cp('act', bias[:], pb_[:, 0:1], ['bk7'], ['n_bias'])
                for n0 in range(0, NCT, 512):
                    nn = min(512, NCT - n0)
                    acc = self.bk[0]
                    for l_ in range(32):
                        off = 16 * n0 + l_ + (0 if l_ < 16 else 0)
                        rhs = cT[:, off:off + 16 * (nn - 1) + 1:16]
                        self.mm(acc[:, 0:nn], W1B[:, l_, :], rhs, l_ == 0, l_ == 31, ['n_W1B', 'n_cT'], ['bk0'])
                    self.ts('dve', hx[:, 0:nn], acc[:, 0:nn], bias[:, 0:1], None, ALU.add, None, ['bk0', 'n_bias'], ['n_hx'])
                    self.tt('dve', hy[:, 0:nn], hx[:, 0:nn], hx[:, 0:nn], ALU.mult, ['n_hx'], ['n_hy'])
                    self.ts('dve', hy[:, 0:nn], hy[:, 0:nn], 0.044715, 1.0, ALU.mult, ALU.add, ['n_hy'], ['n_hy'])
                    self.tt('dve', hy[:, 0:nn], hy[:, 0:nn], hx[:, 0:nn], ALU.mult, ['n_hy', 'n_hx'], ['n_hy'])
                    self.act(hy[:, 0:nn], hy[:, 0:nn], AF.Sigmoid, ['n_hy'], ['n_hy'], scale=2.0 * math.sqrt(2.0 / math.pi))
                    self.tt('dve', H1[:, 0:nn], hy[:, 0:nn], hx[:, 0:nn], ALU.mult, ['n_hy', 'n_hx'], ['n_H1'])
                    if kv == 0:
                        for g in range(NG):
                            po = self.bk[1 + g]
                            self.mm(po[0:64, 0:nn], W2P[g][:, :], H1[:, 0:nn], True, True, ['n_H1', f'n_W2P{g}'], [f'bk{1 + g}'])
                            self.cp('act', CK[g][0:64, n0:n0 + nn], po[0:64, 0:nn], [f'bk{1 + g}'], [f'n_CK{g}'])
                            self.tt('dve', sqk[:, 0:nn], CK[g][0:64, n0:n0 + nn], CK[g][0:64, n0:n0 + nn], ALU.mult, [f'n_CK{g}'], ['n_sqk'])
                            pn = self.bk[3]
                            self.mm(pn[0:1, 0:nn], ones_b[0:64, 0:1], sqk[:, 0:nn], True, True, ['n_sqk', 'n_ones_b'], ['bk3'])
                            S.op('dve', lambda e: e.reduce_max(out=ckn[:, 1:2], in_=pn[0:1, 0:nn], axis=AX.X), ['bk3'], ['n_ckn'])
                            self.tt('dve', ckn[:, 0:1], ckn[:, 0:1], ckn[:, 1:2], ALU.max, ['n_ckn'], ['n_ckn'])
                    else:
                        for c0 in range(0, nn, P):
                            cn_ = min(P, nn - c0)
                            c = (n0 + c0) // P
                            po = self.bk[1]
                            self.mm(po[0:cn_, 0:P], H1[:, c0:c0 + cn_], W2B[:, :], True, True, ['n_H1', 'n_W2B'], ['bk1'])
                            for g in range(NG):
                                self.cp('act', CVP[g][0:cn_, c, 0:64], po[0:cn_, 64 * g:64 * g + 64], ['bk1'], [f'n_CVP{g}'])
                if KNSA != 'p':
                    self.sample_compress(s2, l, kv, W1B, W2B, W2P, bias, cT, H1, hx, hy)
                S.barrier()
            pi2 = self.sb(s2, "n_pi2", [P, NCTL, P], I32)
            pf2 = self.sb(s2, "n_pf2", [P, NCTL, P], F32)
            pf3 = self.sb(s2, "n_pf3", [P, NCTL, P], F32)
            S.op('pool', lambda e: e.iota(pi2[:], pattern=[[-128, NCTL], [4, P]], base=0, channel_multiplier=-1), (), ['n_pi2'])
            self.ts('dve', pf2[:], pi2[:], 0, None, ALU.is_le, None, ['n_pi2'], ['n_pf2'])
            self.ts('dve', pf3[:], pi2[:], -3, None, ALU.is_ge, None, ['n_pi2'], ['n_pf3'])
            self.tt('dve', POOLM[:], pf2[:], pf3[:], ALU.mult, ['n_pf2', 'n_pf3'], ['n_POOLM'])
            for g in range(NG):
                self.memset('dve', CVP[g][:, :, 64:65], 1.0, [f'n_CVP{g}'])
            kb_ = self.sb(s2, "n_kb", [P, 4], F32)
            pk_ = self.bk[4]
            self.mm(pk_[:, 0:1], self.ones_f[0:1, :], ckn[0:1, 0:1], True, True, ['n_ckn', 'ones_f'], ['bk4'])
            self.cp('act', kb_[:, 0:1], pk_[:, 0:1], ['bk4'], ['n_kb'])
            S.op('dve', lambda e: e.reduce_max(out=kb_[:, 1:2], in_=gm[:, 8:12], axis=AX.X), [gmk], ['n_kb'])
            self.tt('dve', kb_[:, 0:1], kb_[:, 0:1], kb_[:, 1:2], ALU.max, ['n_kb'], ['n_kb'])
            S.op('dve', lambda e: e.reduce_max(out=kb_[:, 2:3], in_=gm[:, 0:8], axis=AX.X), [gmk], ['n_kb'])
            self.tt('dve', kb_[:, 0:1], kb_[:, 0:1], kb_[:, 2:3], ALU.mult, ['n_kb'], ['n_kb'])
            self.act(kb_[:, 3:4], kb_[:, 0:1], AF.Sqrt, ['n_kb'], ['n_kb'], scale=1.0 / 64.0)
            self.ts('dve', negM[:, 0:1], kb_[:, 3:4], -1.0, None, ALU.mult, None, ['n_kb'], ['n_negM'])
            self.ts('dve', negM[:, 1:2], kb_[:, 3:4], -2.0, -10.0, ALU.mult, ALU.add, ['n_kb'], ['n_negM'])
            S.barrier()
        B = dict(
            ET=[self.sb(st, f"n_ET{g}", [P, 512], BF16) for g in range(NG)],
            PM=[self.sb(st, f"n_PM{g}", [P, 512], BF16) for g in range(NG)],
            KW=[self.sb(st, f"n_KW{g}", [67, 5 * P], BF16) for g in range(NG)],
            VW=self.sb(st, "n_VW", [P, 5, 130], BF16),
            rz=self.sb(st, "n_rz", [P, 3, 8], F32),
            coef=self.sb(st, "n_coef", [P, 3, 8], F32),
            imp=self.sb(st, "n_imp", [P, NG, P], F32),
            sc2=self.sb(st, "n_sc2", [P, P], F32),
            m8=self.sb(st, "n_m8", [P, 16], F32),
            selb=self.sb(st, "n_selb", [P, NG, P], BF16),
            selT=self.sb(st, "n_selT", [64, 2, NG * P], BF16),
            oacc=self.sb(st, "n_oacc", [P, 8, 64], F32),
            otmp=self.sb(st, "n_otmp", [P, 4, 64], F32),
            mixa=self.sb(st, "n_mixa", [P, 512], BF16),
        )
        SB = self.nsa_sample_bufs(st, l, QB)
        return dict(KS=KS, VS=VS, CK=CK, CVP=CVP, POOLM=POOLM, OTS=OTS, QB=QB, QA=QA, ESEL=ESEL, negM=negM, B=B, NCT=NCT, SB=SB)

    def attn_scores(self, ns, g, lhsT, nk, QA, NQW, negcol, mask):
        S = self.S
        ET = ns['B']['ET'][g]
        ps = self.bk[g]
        self.mm(ps[0:nk, 0:NQW], lhsT, QA, True, True, [f'n_KSc0{g}', f'n_KSc1{g}', f'n_CK{g}', f'n_KW{g}', f'n_QA{g}'], [f'bk{g}'])
        self.act(ET[0:nk, 0:NQW], ps[0:nk, 0:NQW], AF.Exp, [f'bk{g}', 'n_negM'], [f'n_ET{g}'], scale=0.125, bias=negcol[0:nk, :])
        if mask is not None:
            pattern, base, cm = mask
            S.op('pool', lambda e: e.affine_select(out=ET[0:nk, 0:NQW], in_=ET[0:nk, 0:NQW], pattern=pattern, compare_op=ALU.is_ge,
                                                   fill=self.fill0, base=base, channel_multiplier=cm), [f'n_ET{g}'], [f'n_ET{g}'])
        return ET

    def page_index(self, st, e_, name, l=0, tiles=None):
        cfg, S, I = self.cfg, self.S, self.I
        NPG = cfg.NPAGE
        if tiles is None:
            tiles = (self.sb(st, f"ptb_{name}", [P, NPG], I32), self.sb(st, f"ptf_{name}", [P, NPG], F32),
                     self.sb(st, f"pio_{name}", [P, 1], I32), self.sb(st, f"pif_{name}", [P, 1], F32))
        ptb, ptf, pio, pif = tiles
        k = f'idx_{id(ptb)}'
        self.ld(ptb[:], I['page_table'][e_:e_ + 1, :].partition_broadcast(P), (), [k])
        S.op('pool', lambda e: e.iota(pio[:], pattern=[[0, 1]], base=l * cfg.NPHYS * P, channel_multiplier=1), (), [k + 'p'])
        self.cp('dve', pif[:], pio[:], [k + 'p'], [k + 'pf'])
        self.cp('dve', ptf[:], ptb[:], [k], [k + 'f'])
        self.ts('dve', ptf[:], ptf[:], 128.0, pif[:, 0:1], ALU.mult, ALU.add, [k + 'f', k + 'pf'], [k + 'f'])
        self.cp('dve', ptb[:], ptf[:], [k + 'f'], [k])
        return ptb, k

    def sample_compress(self, s2, l, kv, W1B, W2B, W2P, bias, cT, H1, hx, hy):
        cfg, S, I, X = self.cfg, self.S, self.I, self.X
        NCS = cfg.PAST // 16 - 1
        CH = 128
        with ExitStack() as s3:
            pg = [self.sb(s3, f"sc_pg{i}", [P, 256], F32) for i in range(2)]
            cks = self.sb(s3, "sc_cks", [64, CH], BF16)
            cvs = self.sb(s3, "sc_cvs", [P, P], BF16)
            for e_ in range(cfg.NS):
                if e_ == 0:
                    NPG_ = cfg.NPAGE
                    pit = (self.sb(s3, "sc_ptb", [P, NPG_], I32), self.sb(s3, "sc_ptf", [P, NPG_], F32),
                           self.sb(s3, "sc_pio", [P, 1], I32), self.sb(s3, "sc_pif", [P, 1], F32))
                idx, idxk = self.page_index(s3, e_, "sc", l, pit)
                for n0 in range(0, NCS, CH):
                    nn = min(CH, NCS - n0)
                    p0 = n0 // 8
                    npg = min(17, cfg.NPAGE - p0)
                    for pj in range(npg):
                        b = pj % 2
                        S.idma(pg[b][:], I['cache_cmp'], idx[:, p0 + pj:p0 + pj + 1].bitcast(U32), [idxk], [f'sc_pg{b}'])
                        self.tr(self.bk[4 + b][:, 0:P], pg[b][:, kv * P:(kv + 1) * P], self.ident_f[:], [f'sc_pg{b}', 'ident_f'], [f'bk{4 + b}'])
                        self.cp('act', cT[:, pj * P:(pj + 1) * P], self.bk[4 + b][:, 0:P], [f'bk{4 + b}'], ['n_cT'])
                    acc = self.bk[0]
                    for l_ in range(32):
                        rhs = cT[:, l_:l_ + 16 * (nn - 1) + 1:16]
                        self.mm(acc[:, 0:nn], W1B[:, l_, :], rhs, l_ == 0, l_ == 31, ['n_W1B', 'n_cT'], ['bk0'])
                    self.ts('dve', hx[:, 0:nn], acc[:, 0:nn], bias[:, 0:1], None, ALU.add, None, ['bk0', 'n_bias'], ['n_hx'])
                    self.tt('dve', hy[:, 0:nn], hx[:, 0:nn], hx[:, 0:nn], ALU.mult, ['n_hx'], ['n_hy'])
                    self.ts('dve', hy[:, 0:nn], hy[:, 0:nn], 0.044715, 1.0, ALU.mult, ALU.add, ['n_hy'], ['n_hy'])
                    self.tt('dve', hy[:, 0:nn], hy[:, 0:nn], hx[:, 0:nn], ALU.mult, ['n_hy', 'n_hx'], ['n_hy'])
                    self.act(hy[:, 0:nn], hy[:, 0:nn], AF.Sigmoid, ['n_hy'], ['n_hy'], scale=2.0 * math.sqrt(2.0 / math.pi))
                    self.tt('dve', H1[:, 0:nn], hy[:, 0:nn], hx[:, 0:nn], ALU.mult, ['n_hy', 'n_hx'], ['n_H1'])
                    if kv == 0:
                        for g in range(NG):
                            po = self.bk[1 + g]
                            self.mm(po[0:64, 0:nn], W2P[g][:, :], H1[:, 0:nn], True, True, ['n_H1', f'n_W2P{g}'], [f'bk{1 + g}'])
                            self.cp('act', cks[:, 0:nn], po[0:64, 0:nn], [f'bk{1 + g}'], ['sc_cks'])
                            self.ld(X['sck'][e_, g, :, n0:n0 + nn], cks[:, 0:nn], ['sc_cks'], ['Xsck'])
                    else:
                        po = self.bk[1]
                        self.mm(po[0:nn, 0:P], H1[:, 0:nn], W2B[:, :], True, True, ['n_H1', 'n_W2B'], ['bk1'])
                        self.cp('act', cvs[0:nn, :], po[0:nn, 0:P], ['bk1'], ['sc_cvs'])
                        self.ld(X['scv'][e_, n0 // P, 0:nn, :], cvs[0:nn, :], ['sc_cvs'], ['Xscv'])
            S.barrier()

    def nsa_sample_bufs(self, st, l, QB):
        cfg, S = self.cfg, self.S
        NCS = cfg.PAST // 16 - 1
        NCSL = (NCS + P - 1) // P
        NBS = (cfg.PAST + cfg.S + 63) // 64
        NBT = (NBS + P - 1) // P
        SB = dict(NCS=NCS, NCSL=NCSL, NBS=NBS, NBT=NBT)
        SB['QA'] = [self.sb(st, f"ns_QA{g}", [67, 16], BF16) for g in range(NG)]
        SB['CK'] = [self.sb(st, f"ns_CK{g}", [67, NCSL * P], BF16) for g in range(NG)]
        SB['CVP'] = [self.sb(st, f"ns_CVP{g}", [P, NCSL, 65], BF16) for g in range(NG)]
        SB['POOL'] = self.sb(st, "ns_POOL", [P, NCSL, P], BF16)
        SB['KT'] = [self.sb(st, f"ns_KT{g}", [67, P], BF16) for g in range(NG)]
        SB['VP'] = self.sb(st, "ns_VP", [P, 130], BF16)
        SB['pg'] = [self.sb(st, f"ns_pg{i}", [P, 256], F32) for i in range(2)]
        SB['pit'] = (self.sb(st, "ns_ptb", [P, cfg.NPAGE], I32), self.sb(st, "ns_ptf", [P, cfg.NPAGE], F32),
                     self.sb(st, "ns_pio", [P, 1], I32), self.sb(st, "ns_pif", [P, 1], F32))
        SB['selT'] = self.sb(st, "ns_selT", [64, 2 * NBT, 8], BF16)
        SB['imp'] = self.sb(st, "ns_imp", [4, NG, NBT * P], F32)
        SB['sc2'] = self.sb(st, "ns_sc2", [4, NBT * P], F32)
        SB['selb'] = self.sb(st, "ns_selb", [4, NG, NBT * P], BF16)
        with ExitStack() as s2:
            self.make_aug_rows(s2, [(SB['CK'][g], f'ns_CK{g}') for g in range(NG)], NCSL * P, 'cmp')
            self.make_aug_rows(s2, [(SB['KT'][g], f'ns_KT{g}') for g in range(NG)], P, 'tok')
            zr = self.sb(s2, "ns_zr", [1, P], BF16)
            self.memset('dve', zr[:], 0.0, ['ns_zr'])
            for g in range(NG):
                self.ld(SB['KT'][g][65:66, :], zr[0:1, :], ['ns_zr'], [f'ns_KT{g}'])
                with self.nc.allow_non_contiguous_dma(reason="tiny static rows"):
                    self.ld(SB['QA'][g][65:67, :].rearrange("p (h q) -> p h q", h=4), QB[g][65:67, :].rearrange("p (h q) -> p h q", h=4)[:, :, 0:4], [f'n_QB{g}'], [f'ns_QA{g}'])
            pi2 = self.sb(s2, "ns_pi2", [P, NCSL, P], I32)
            pf2 = self.sb(s2, "ns_pf2", [P, NCSL, P], F32)
            pf3 = self.sb(s2, "ns_pf3", [P, NCSL, P], F32)
            for c in range(NCSL):
                S.op('pool', lambda e, c=c: e.iota(pi2[:, c, :], pattern=[[4, P]], base=512 * (c // 4) - 128 * c, channel_multiplier=-1), (), ['ns_pi2'])
            self.ts('dve', pf2[:], pi2[:], 0, None, ALU.is_le, None, ['ns_pi2'], ['ns_pf2'])
            self.ts('dve', pf3[:], pi2[:], -3, None, ALU.is_ge, None, ['ns_pi2'], ['ns_pf3'])
            self.tt('dve', SB['POOL'][:], pf2[:], pf3[:], ALU.mult, ['ns_pf2', 'ns_pf3'], ['ns_POOL'])
            for g in range(NG):
                self.memset('dve', SB['CVP'][g][:], 1.0, [f'ns_CVP{g}'])
            self.memset('dve', SB['VP'][:], 1.0, ['ns_VP'])
            S.barrier()
        return SB

    def nsa_sample(self, ns, l, e_, gnt_s, gates_s, mixT_dst, mixk):
        cfg, S, I, O, X = self.cfg, self.S, self.I, self.O, self.X
        SB, B = ns['SB'], ns['B']
        NT, S4, PAST = cfg.NT, cfg.S, cfg.PAST
        NCS, NCSL, NBS, NBT = SB['NCS'], SB['NCSL'], SB['NBS'], SB['NBT']
        QA, QB, CK, CVP, KT, VP, pg, selT = SB['QA'], ns['QB'], SB['CK'], SB['CVP'], SB['KT'], SB['VP'], SB['pg'], SB['selT']
        negc = ns['negM'][:, 1:2]
        NQW = 4 * S4
        rz, coef, oacc, otmp, OTS = B['rz'], B['coef'], B['oacc'], B['otmp'], ns['OTS']
        tq = (PAST - 64) / 128.0
        cols = slice(e_ * S4, (e_ + 1) * S4)
        qv = lambda a: a.rearrange("p (h q) -> p h q", h=4)

        def set_qrow(g, tval):
            self.ts('dve', qv(QA[g][64:65, :]), qv(QB[g][64:65, :])[:, :, 0:S4], -float(tval), None, ALU.mult, None, [f'n_QB{g}'], [f'ns_QA{g}'])

        def scores(g, lhsT, nk, keys, mask=None):
            ET = B['ET'][g]
            ps = self.bk[g]
            self.mm(ps[0:nk, 0:NQW], lhsT, QA[g][0:67, :], True, True, keys + [f'ns_QA{g}'], [f'bk{g}'])
            self.act(ET[0:nk, 0:NQW], ps[0:nk, 0:NQW], AF.Exp, [f'bk{g}', 'n_negM'], [f'n_ET{g}'], scale=0.125, bias=negc[0:nk, :])
            if mask is not None:
                pattern, base, cm = mask
                S.op('pool', lambda e: e.affine_select(out=ET[0:nk, 0:NQW], in_=ET[0:nk, 0:NQW], pattern=pattern, compare_op=ALU.is_ge,
                                                       fill=self.fill0, base=base, channel_multiplier=cm), [f'n_ET{g}'], [f'n_ET{g}'])
            return ET

        def finish_branch(br, first):
            for g in range(NG):
                self.cp('act', OTS[0:65, g, 0:NQW], self.bk[2 + g][0:65, 0:NQW], [f'bk{2 + g}'], ['n_OTS'])
                for hl in range(HPG):
                    self.tr(self.bk[6 + g][0:S4, hl * 65:(hl + 1) * 65], OTS[0:65, g, hl * S4:(hl + 1) * S4], self.ident_f[0:65, 0:65], ['n_OTS', 'ident_f'], [f'bk{6 + g}'])
                ov = self.bk[6 + g][0:S4, 0:260].rearrange("p (h w) -> p h w", h=HPG)
                k_ = f'bk{6 + g}'
                self.ts('dve', rz[0:S4, br, 4 * g:4 * g + 4], ov[:, :, 64], 1e-36, None, ALU.max, None, [k_], ['n_rz'])
                S.op('dve', lambda e: e.reciprocal(out=rz[0:S4, br, 4 * g:4 * g + 4], in_=rz[0:S4, br, 4 * g:4 * g + 4]), ['n_rz'], ['n_rz'])
                self.tt('dve', coef[0:S4, br, 4 * g:4 * g + 4], rz[0:S4, br, 4 * g:4 * g + 4], gnt_s[0:S4, 8 * br + 4 * g:8 * br + 4 * g + 4], ALU.mult, ['n_rz', 'gnt_s'], ['n_coef'])
                bc_ = coef[0:S4, br, 4 * g:4 * g + 4].unsqueeze(2).to_broadcast([S4, HPG, 64])
                if first:
                    self.tt('dve', oacc[0:S4, 4 * g:4 * g + 4, :], ov[:, :, 0:64], bc_, ALU.mult, [k_, 'n_coef'], ['n_oacc'])
                else:
                    self.tt('dve', otmp[0:S4], ov[:, :, 0:64], bc_, ALU.mult, [k_, 'n_coef'], ['n_otmp'])
                    self.tt('dve', oacc[0:S4, 4 * g:4 * g + 4, :], oacc[0:S4, 4 * g:4 * g + 4, :], otmp[0:S4], ALU.add, ['n_oacc', 'n_otmp'], ['n_oacc'])

        for g in range(NG):
            self.ld(qv(QA[g][0:64, :]), qv(X['qT'][NT][64 * g:64 * g + 64, :])[:, :, cols], ['XqT'], [f'ns_QA{g}'])
            set_qrow(g, tq)
            self.ld(CK[g][0:64, 0:NCS], X['sck'][e_, g][:, 0:NCS], ['Xsck'], [f'ns_CK{g}'])
            for c in range(NCSL):
                nk = min(P, NCS - c * P)
                self.ld(CVP[g][0:nk, c, 0:64], X['scv'][e_, c, 0:nk, 64 * g:64 * g + 64], ['Xscv'], [f'ns_CVP{g}'])
        for c in range(NCSL):
            nk = min(P, NCS - c * P)
            bt = c // 4
            for g in range(NG):
                ET = scores(g, CK[g][0:67, c * P:c * P + nk], nk, [f'ns_CK{g}'])
                self.mm(self.bk[2 + g][0:65, 0:NQW], CVP[g][0:nk, c, :], ET[0:nk, 0:NQW], c == 0, c == NCSL - 1, [f'n_ET{g}', f'ns_CVP{g}'], [f'bk{2 + g}'])
                pb_ = 4 + 2 * g + bt
                self.mm(self.bk[pb_][:, 0:NQW], SB['POOL'][0:nk, c, :], ET[0:nk, 0:NQW], c % 4 == 0, (c % 4 == 3) or (c == NCSL - 1), [f'n_ET{g}', 'ns_POOL'], [f'bk{pb_}'])
        imp, sc2, selb = SB['imp'], SB['sc2'], SB['selb']
        self.memset('dve', imp[:], 0.0, ['ns_imp'])
        nbt_c = (NCSL + 3) // 4
        for g in range(NG):
            for bt in range(nbt_c):
                pb_ = 4 + 2 * g + bt
                self.cp('act', OTS[:, 2 + bt, 0:NQW], self.bk[pb_][:, 0:NQW], [f'bk{pb_}'], ['n_OTS'])
            self.cp('act', OTS[0:65, g, 0:NQW], self.bk[2 + g][0:65, 0:NQW], [f'bk{2 + g}'], ['n_OTS'])
            for hl in range(HPG):
                self.tr(self.bk[g][0:S4, hl * 65:(hl + 1) * 65], OTS[0:65, g, hl * S4:(hl + 1) * S4], self.ident_f[0:65, 0:65], ['n_OTS', 'ident_f'], [f'bk{g}'])
            ov = self.bk[g][0:S4, 0:260].rearrange("p (h w) -> p h w", h=HPG)
            self.ts('dve', rz[0:S4, 0, 4 * g:4 * g + 4], ov[:, :, 64], 1e-36, None, ALU.max, None, [f'bk{g}'], ['n_rz'])
            S.op('dve', lambda e: e.reciprocal(out=rz[0:S4, 0, 4 * g:4 * g + 4], in_=rz[0:S4, 0, 4 * g:4 * g + 4]), ['n_rz'], ['n_rz'])
            self.tt('dve', coef[0:S4, 0, 4 * g:4 * g + 4], rz[0:S4, 0, 4 * g:4 * g + 4], gnt_s[0:S4, 4 * g:4 * g + 4], ALU.mult, ['n_rz', 'gnt_s'], ['n_coef'])
            self.tt('dve', oacc[0:S4, 4 * g:4 * g + 4, :], ov[:, :, 0:64], coef[0:S4, 0, 4 * g:4 * g + 4].unsqueeze(2).to_broadcast([S4, HPG, 64]), ALU.mult, [f'bk{g}', 'n_coef'], ['n_oacc'])
            for bt in range(nbt_c):
                for hl in range(HPG):
                    self.tr(self.bk[2 + g][0:S4, hl * P:(hl + 1) * P], OTS[:, 2 + bt, hl * S4:(hl + 1) * S4], self.ident_f[:], ['n_OTS', 'ident_f'], [f'bk{2 + g}'])
                for hl in range(HPG):
                    h = 4 * g + hl
                    self.stt(imp[0:S4, g, bt * P:(bt + 1) * P], self.bk[2 + g][0:S4, hl * P:(hl + 1) * P], rz[0:S4, 0, h:h + 1], imp[0:S4, g, bt * P:(bt + 1) * P],
                             ALU.mult, ALU.add, [f'bk{2 + g}', 'n_rz', 'ns_imp'], ['ns_imp'])
        cur = PAST // 64
        if NBT * P > NBS:
            self.memset('dve', imp[0:S4, :, NBS:NBT * P], -1.0, ['ns_imp'])
        self.memset('dve', imp[0:S4, :, 0:1], 1e9, ['ns_imp'])
        self.memset('dve', imp[0:S4, :, cur - 1:cur + 1], 1e9, ['ns_imp'])
        m8 = B['m8']
        for g in range(NG):
            S.op('dve', lambda e: e.max(out=m8[0:S4, 0:8], in_=imp[0:S4, g, :]), ['ns_imp'], ['n_m8'])
            S.op('dve', lambda e: e.match_replace(out=sc2[0:S4, :], in_to_replace=m8[0:S4, 0:8], in_values=imp[0:S4, g, :], imm_value=-3e9), ['ns_imp', 'n_m8'], ['ns_sc2'])
            S.op('dve', lambda e: e.max(out=m8[0:S4, 8:16], in_=sc2[0:S4, :]), ['ns_sc2'], ['n_m8'])
            self.ts('dve', selb[0:S4, g, :], imp[0:S4, g, :], m8[0:S4, 15:16], None, ALU.is_ge, None, ['ns_imp', 'n_m8'], ['ns_selb'])
            for hb in range(2 * NBT):
                self.tr(self.bkb[7][0:64, (hb * NG + g) * S4:(hb * NG + g + 1) * S4], selb[0:S4, g, hb * 64:(hb + 1) * 64], self.ident_b[0:S4, 0:S4], ['ns_selb', 'ident_b'], ['bk7'])
        self.cp('act', selT[:].rearrange("p t c -> p (t c)"), self.bkb[7][0:64, 0:2 * NBT * 8], ['bk7'], ['ns_selT'])
        idx, idxk = self.page_index_cached(ns, e_, l)
        npg = cfg.NPAGE

        def key_tile(src_rows, nk, jpage, first, last, sel_bt, sel_r, extra_mask):
            for g in range(NG):
                self.tr(self.bk[6][0:64, g * P:g * P + nk], src_rows[0:nk, 64 * g:64 * g + 64], self.ident_f[0:nk, 0:nk], ['ns_pgcur', 'ident_f'], ['bk6'])
                self.cp('act', KT[g][0:64, 0:nk], self.bk[6][0:64, g * P:g * P + nk], ['bk6'], [f'ns_KT{g}'])
            self.cp('dve', VP[0:nk, :].rearrange("p (g w) -> p g w", g=NG)[:, :, 0:64], src_rows[0:nk, 128:256].rearrange("p (g d) -> p g d", g=NG), ['ns_pgcur'], ['ns_VP'])
            if sel_bt is not None:
                a2, r = sel_r // 32, sel_r % 32
                self.mm(self.bk[7][0:nk, 0:8], ns['ESEL'][0:64, r, 0:nk], selT[0:64, 2 * sel_bt + a2, :], True, True, ['n_ESEL', 'ns_selT'], ['bk7'])
            for g in range(NG):
                set_qrow(g, tq - jpage)
                ET = scores(g, KT[g][0:67, 0:nk], nk, [f'ns_KT{g}'], extra_mask)
                src = ET
                if sel_bt is not None:
                    PM = B['PM'][g]
                    self.tt('dve', PM[0:nk, 0:NQW].rearrange("p (h q) -> p h q", h=HPG), ET[0:nk, 0:NQW].rearrange("p (h q) -> p h q", h=HPG),
                            self.bk[7][0:nk, g * S4:(g + 1) * S4].unsqueeze(1).to_broadcast([nk, HPG, S4]), ALU.mult, [f'n_ET{g}', 'bk7'], [f'n_PM{g}'])
                    src = PM
                self.mm(self.bk[2 + g][0:65, 0:NQW], VP[0:nk, 65 * g:65 * g + 65], src[0:nk, 0:NQW], first, last, [f'n_ET{g}', f'n_PM{g}', 'ns_VP'], [f'bk{2 + g}'])

        newr = SB['pg'][0]
        for j in range(npg):
            b = j % 2
            S.idma(pg[b][:], I['cache_slc'], idx[:, j:j + 1].bitcast(U32), [idxk], ['ns_pgcur'])
            key_tile(pg[b], P, j, j == 0, False, j // 64, j % 64, None)
        self.ld(newr[0:S4, :], O['slc_s'][l, e_ * S4:(e_ + 1) * S4, :], ['Oslc_s'], ['ns_pgcur'])
        key_tile(newr, S4, PAST // P, False, True, cur // P, (cur % P) // 2, ([[0, 4], [1, S4]], 0, -1))
        finish_branch(1, False)
        WB = cfg.WB
        nwt = WB // P
        for i in range(nwt):
            b = i % 2
            self.ld(pg[b][:], I['cache_win'][l, e_, i * P:(i + 1) * P, :], (), ['ns_pgcur'])
            mask = ([[0, 4], [-1, S4]], 0, 1) if i == 0 else None
            key_tile(pg[b], P, (PAST - WB) // P + i, i == 0, False, None, None, mask)
        self.ld(newr[0:S4, :], O['win_s'][l, e_, WB - S4:WB, :], ['Owin_s'], ['ns_pgcur'])
        key_tile(newr, S4, PAST // P, False, True, None, None, ([[0, 4], [1, S4]], 0, -1))
        finish_branch(2, False)
        self.ld(O['win_s'][l, e_, 0:WB - S4, :], I['cache_win'][l, e_, S4:WB, :], (), ['Owin_s2'])
        mixa = B['mixa']
        self.tt('dve', mixa[0:S4, :], oacc[0:S4].rearrange("p h d -> p (h d)"), gates_s[0:S4, 0:512], ALU.mult, ['n_oacc', 'gates_s'], ['n_mixa'])
        for j in range(4):
            self.tr(self.bkb[7][:, j * P:j * P + S4], mixa[0:S4, j * P:(j + 1) * P], self.ident_b[0:S4, 0:S4], ['n_mixa', 'ident_b'], ['bk7'])
        self.cp('act', mixT_dst[:, 0:4, cols], self.bkb[7][:, 0:512].rearrange("p (k t) -> p k t", k=4)[:, :, 0:S4], ['bk7'], [mixk])

    def nsa_tile(self, ns, l, t, gnt, gate, gatek, mixT_dst, mixk):
        cfg, S, I, O, X = self.cfg, self.S, self.I, self.O, self.X
        B = ns['B']
        NT = cfg.NT
        NCT = ns['NCT']
        QA, QB, KS, VS, CK, CVP = ns['QA'], ns['QB'], ns['KS'], ns['VS'], ns['CK'], ns['CVP']
        rz, coef, imp, oacc, otmp = B['rz'], B['coef'], B['imp'], B['oacc'], B['otmp']
        OTS = ns['OTS']
        for g in range(NG):
            self.ld(QA[g][0:64, :], X['qT'][t][64 * g:64 * g + 64, :], ['XqT'], [f'n_QA{g}'])
            self.ts('dve', QA[g][64:65, :], QB[g][64:65, :], -float(t), None, ALU.mult, None, [f'n_QB{g}'], [f'n_QA{g}'])
        ncv = min(NCT, 8 * t + 7)
        nct = (ncv + P - 1) // P
        for c in range(nct):
            nk = min(P, ncv - c * P)
            last_n = c * P + nk - 1
            partial = 16 * last_n + 31 > 128 * t
            mask = ([[0, 4], [1, P]], 128 * t - 2048 * c - 31, -16) if partial else None
            for g in range(NG):
                ET = self.attn_scores(ns, g, CK[g][0:67, c * P:c * P + nk], nk, QA[g][0:67, :], 512, ns['negM'][:, 0:1], mask)
                self.mm(self.bk[2 + g][0:65, 0:512], CVP[g][0:nk, c, :], ET[0:nk, 0:512], c == 0, c == nct - 1, [f'n_ET{g}', f'n_CVP{g}'], [f'bk{2 + g}'])
                self.mm(self.bk[4 + g][:, 0:512], ns['POOLM'][0:nk, c, :], ET[0:nk, 0:512], c == 0, c == nct - 1, [f'n_ET{g}', 'n_POOLM'], [f'bk{4 + g}'])
        for g in range(NG):
            self.cp('act', OTS[0:65, g, :], self.bk[2 + g][0:65, 0:512], [f'bk{2 + g}'], ['n_OTS'])
            self.cp('act', OTS[:, 2 + g, :], self.bk[4 + g][:, 0:512], [f'bk{4 + g}'], ['n_OTS'])
        for g in range(NG):
            for hl in range(HPG):
                self.tr(self.bk[6 + g][:, hl * 65:(hl + 1) * 65], OTS[0:65, g, hl * P:(hl + 1) * P], self.ident_f[0:65, 0:65], ['n_OTS', 'ident_f'], [f'bk{6 + g}'])
                self.tr(self.bk[g][:, hl * P:(hl + 1) * P], OTS[:, 2 + g, hl * P:(hl + 1) * P], self.ident_f[:], ['n_OTS', 'ident_f'], [f'bk{g}'])
        for g in range(NG):
            ov = self.bk[6 + g][:, 0:260].rearrange("p (h w) -> p h w", h=HPG)
            k_ = f'bk{6 + g}'
            self.ts('dve', rz[:, 0, 4 * g:4 * g + 4], ov[:, :, 64], 1e-36, None, ALU.max, None, [k_], ['n_rz'])
            S.op('dve', lambda e: e.reciprocal(out=rz[:, 0, 4 * g:4 * g + 4], in_=rz[:, 0, 4 * g:4 * g + 4]), ['n_rz'], ['n_rz'])
            self.tt('dve', coef[:, 0, 4 * g:4 * g + 4], rz[:, 0, 4 * g:4 * g + 4], gnt[:, 4 * g:4 * g + 4], ALU.mult, ['n_rz', 'gnt_p'], ['n_coef'])
            self.tt('dve', oacc[:, 4 * g:4 * g + 4, :], ov[:, :, 0:64], coef[:, 0, 4 * g:4 * g + 4].unsqueeze(2).to_broadcast([P, HPG, 64]), ALU.mult, [k_, 'n_coef'], ['n_oacc'])
            for hl in range(HPG):
                h = 4 * g + hl
                src = self.bk[g][:, hl * P:(hl + 1) * P]
                if hl == 0:
                    self.ts('dve', imp[:, g, :], src, rz[:, 0, h:h + 1], None, ALU.mult, None, [f'bk{g}', 'n_rz'], ['n_imp'])
                else:
                    self.stt(imp[:, g, :], src, rz[:, 0, h:h + 1], imp[:, g, :], ALU.mult, ALU.add, [f'bk{g}', 'n_rz', 'n_imp'], ['n_imp'])
        if KNSA == 'cmp':
            return
        for h2 in range(2):
            cur = 2 * t + h2
            ps_ = slice(64 * h2, 64 * h2 + 64)
            if cur < P - 1:
                S.op('pool', lambda e: e.affine_select(out=imp[ps_, :, :], in_=imp[ps_, :, :], pattern=[[0, NG], [-1, P]], compare_op=ALU.is_ge,
                                                       fill=self.fillm1, base=cur, channel_multiplier=0), ['n_imp'], ['n_imp'])
            self.memset('pool', imp[ps_, :, 0:1], 1e9, ['n_imp'])
            lo = max(cur - 1, 0)
            self.memset('pool', imp[ps_, :, lo:cur + 1], 1e9, ['n_imp'])
        m8, sc2, selb, selT = B['m8'], B['sc2'], B['selb'], B['selT']
        for g in range(NG):
            S.op('dve', lambda e: e.max(out=m8[:, 0:8], in_=imp[:, g, :]), ['n_imp'], ['n_m8'])
            S.op('dve', lambda e: e.match_replace(out=sc2[:], in_to_replace=m8[:, 0:8], in_values=imp[:, g, :], imm_value=-3e9), ['n_imp', 'n_m8'], ['n_sc2'])
            S.op('dve', lambda e: e.max(out=m8[:, 8:16], in_=sc2[:]), ['n_sc2'], ['n_m8'])
            self.ts('dve', selb[:, g, :], imp[:, g, :], m8[:, 15:16], None, ALU.is_ge, None, ['n_imp', 'n_m8'], ['n_selb'])
            for hb in range(2):
                self.tr(self.bkb[7][0:64, (hb * NG + g) * P:(hb * NG + g + 1) * P], selb[:, g, hb * 64:(hb + 1) * 64], self.ident_b[:], ['n_selb', 'ident_b'], ['bk7'])
        self.cp('act', selT[:].rearrange("p a c -> p (a c)"), self.bkb[7][0:64, 0:2 * NG * P], ['bk7'], ['n_selT'])
        if KNSA == 'sel':
            return
        PM = B['PM']
        CHK = 8
        for kt in range(t + 1):
            cb, ci = (kt // CHK) % 2, kt % CHK
            if ci == 0:
                nkt = min(CHK, t + 1 - kt)
                for g in range(NG):
                    self.ld(KS[cb][g][0:64, 0:nkt * P].rearrange("p (t k) -> p t k", k=P), X['kTs'][kt:kt + nkt, 64 * g:64 * g + 64, :].rearrange("t p k -> p t k"), ['XkTs'], [f'n_KSc{cb}{g}'])
                    self.ld(KS[cb][g][64:67, 0:nkt * P], X['kaug'][:, kt * P:(kt + nkt) * P], ['Xkaug'], [f'n_KSc{cb}{g}'])
                self.ld(VS[cb][:, 0:nkt, :], X['vs'][kt:kt + nkt].rearrange("t p c -> p t c"), ['Xvs'], [f'n_VSc{cb}'])
            a2, r = kt // 32, kt % 32
            self.mm(self.bk[6][:, 0:NG * P], ns['ESEL'][0:64, r, :], selT[0:64, a2, :], True, True, ['n_ESEL', 'n_selT'], ['bk6'])
            for g in range(NG):
                ET = self.attn_scores(ns, g, KS[cb][g][0:67, ci * P:(ci + 1) * P], P, QA[g][0:67, :], 512, ns['negM'][:, 0:1], None)
                self.tt('dve', PM[g][:].rearrange("p (h q) -> p h q", h=HPG), ET[:].rearrange("p (h q) -> p h q", h=HPG),
                        self.bk[6][:, g * P:(g + 1) * P].unsqueeze(1).to_broadcast([P, HPG, P]), ALU.mult, [f'n_ET{g}', 'bk6'], [f'n_PM{g}'])
                if kt == t:
                    S.op('pool', lambda e: e.affine_select(out=PM[g][:], in_=PM[g][:], pattern=[[0, 4], [1, P]], compare_op=ALU.is_ge,
                                                           fill=self.fill0, base=0, channel_multiplier=-1), [f'n_PM{g}'], [f'n_PM{g}'])
                self.mm(self.bk[2 + g][0:65, 0:512], VS[cb][:, ci, 65 * g:65 * g + 65], PM[g][:, 0:512], kt == 0, kt == t, [f'n_PM{g}', f'n_VSc{cb}'], [f'bk{2 + g}'])
        if KNSA == 'slc':
            return
        kt0 = max(0, t - 4)
        nw = t - kt0 + 1
        KW, VW = B['KW'], B['VW']
        for g in range(NG):
            self.ld(KW[g][0:64, 0:nw * P].rearrange("p (t k) -> p t k", k=P), X['kTw'][kt0:t + 1, 64 * g:64 * g + 64, :].rearrange("t p k -> p t k"), ['XkTw'], [f'n_KW{g}'])
            self.ld(KW[g][64:67, 0:nw * P], X['kaug'][:, kt0 * P:(t + 1) * P], ['Xkaug'], [f'n_KW{g}'])
        self.ld(VW[:, 0:nw, :], X['vw'][kt0:t + 1].rearrange("t p c -> p t c"), ['Xvw'], ['n_VW'])
        for i in range(nw):
            kt = kt0 + i
            mask = None
            if kt == t:
                mask = ([[0, 4], [1, P]], 0, -1)
            elif kt == t - 4:
                mask = ([[0, 4], [-1, P]], 0, 1)
            for g in range(NG):
                ET = self.attn_scores(ns, g, KW[g][0:67, i * P:(i + 1) * P], P, QA[g][0:67, :], 512, ns['negM'][:, 0:1], mask)
                self.mm(self.bk[4 + g][0:65, 0:512], VW[:, i, 65 * g:65 * g + 65], ET[:, 0:512], i == 0, i == nw - 1, [f'n_ET{g}', 'n_VW'], [f'bk{4 + g}'])
        for bi_, br in ((2, 1), (4, 2)):
            for g in range(NG):
                self.cp('act', OTS[0:65, g, :], self.bk[bi_ + g][0:65, 0:512], [f'bk{bi_ + g}'], ['n_OTS'])
                for hl in range(HPG):
                    self.tr(self.bk[6 + g][:, hl * 65:(hl + 1) * 65], OTS[0:65, g, hl * P:(hl + 1) * P], self.ident_f[0:65, 0:65], ['n_OTS', 'ident_f'], [f'bk{6 + g}'])
                ov = self.bk[6 + g][:, 0:260].rearrange("p (h w) -> p h w", h=HPG)
                k_ = f'bk{6 + g}'
                self.ts('dve', rz[:, br, 4 * g:4 * g + 4], ov[:, :, 64], 1e-36, None, ALU.max, None, [k_], ['n_rz'])
                S.op('dve', lambda e: e.reciprocal(out=rz[:, br, 4 * g:4 * g + 4], in_=rz[:, br, 4 * g:4 * g + 4]), ['n_rz'], ['n_rz'])
                self.tt('dve', coef[:, br, 4 * g:4 * g + 4], rz[:, br, 4 * g:4 * g + 4], gnt[:, 8 * br + 4 * g:8 * br + 4 * g + 4], ALU.mult, ['n_rz', 'gnt_p'], ['n_coef'])
                self.tt('dve', otmp[:], ov[:, :, 0:64], coef[:, br, 4 * g:4 * g + 4].unsqueeze(2).to_broadcast([P, HPG, 64]), ALU.mult, [k_, 'n_coef'], ['n_otmp'])
                self.tt('dve', oacc[:, 4 * g:4 * g + 4, :], oacc[:, 4 * g:4 * g + 4, :], otmp[:], ALU.add, ['n_oacc', 'n_otmp'], ['n_oacc'])
        self.tt('dve', B['mixa'][:], oacc[:].rearrange("p h d -> p (h d)"), gate, ALU.mult, ['n_oacc', gatek], ['n_mixa'])
        for j in range(4):
            self.tr(self.bkb[7][:, j * P:(j + 1) * P], B['mixa'][:, j * P:(j + 1) * P], self.ident_b[:], ['n_mixa', 'ident_b'], ['bk7'])
        self.cp('act', mixT_dst[:, 0:4, :], self.bkb[7][:, 0:512].rearrange("p (k t) -> p k t", k=4), ['bk7'], [mixk])

    def mem_setup(self, st, l, gm, gmk):
        cfg, S, I, O, X = self.cfg, self.S, self.I, self.O, self.X
        mkT = self.sb(st, "mkT", [P, 4, NMEM], BF16)
        mv = self.sb(st, "mv", [P, 2, 4, 129], BF16)
        negM = self.sb(st, "mem_negM", [P, 2], F32)
        kn = self.sb(st, "mem_kn", [P, 4], F32)
        self.memset('pool', mv[:], 1.0, ['mkv'])
        self.memset('dve', kn[:], 0.0, ['mem_kn'])
        with ExitStack() as s2:
            wm = self.sb(s2, "w_mem", [P, 8, 1024], BF16)
            for k in range(8):
                self.ld(wm[:, k, :], I['w_mem_kv'][l, k * P:(k + 1) * P, :], (), [f'w_mem{k}'], q='pool')
            self.S.barrier()
            self.load_gain(I['mem_norm_g'][l:l + 1, :])
            xm = self.sb(s2, "xm", [P, D], F32)
            hb = self.sb(s2, "mhb", [P, D], BF16)
            hT = self.sb(s2, "mhT", [P, D], BF16)
            junk = self.sb(s2, "mjunk", [P, D], BF16)
            ss = self.sb(s2, "mss", [P, 2], F32)
            rstd = self.sb(s2, "mrstd", [P, 2], F32)
            stg = self.sb(s2, "mstg", [P, 1024], F32)
            kb = self.sb(s2, "mkb", [P, 512], BF16)
            sq = self.sb(s2, "msq", [P, 512], F32)
            n2 = self.sb(s2, "mn2", [P, 4], F32)
            pt = self.bkb[4]
            pz = [self.bk[0], self.bk[1]]
            for mt in range(2):
                self.ld(xm[:], I['mem_prompt'][mt * P:(mt + 1) * P, :], (), ['xm'])
                self.rmsnorm((junk, ss, rstd), xm[:], 'xm', hb[:], 'mhb', 'gtile')
                for k in range(8):
                    self.tr(pt[:, k * P:(k + 1) * P], hb[:, k * P:(k + 1) * P], self.ident_b[:], ['mhb', 'ident_b'], ['bk4'])
                self.cp('act', hT[:], pt[:], ['bk4'], ['mhT'])
                for c in range(2):
                    for k in range(8):
                        self.mm(pz[c][:, :], hT[:, k * P:(k + 1) * P], wm[:, k, c * 512:(c + 1) * 512], k == 0, k == 7, ['mhT', f'w_mem{k}'], [f'bk{c}'])
                    self.cp('act', stg[:, c * 512:(c + 1) * 512], pz[c][:, :], [f'bk{c}'], ['mstg'])
                self.ld(O['mem_p'][l, mt * P:(mt + 1) * P, :], stg[:], ['mstg'], ['Omem_p'], q='sp')
                self.cp('act', kb[:], pz[0][:, :], ['bk0'], ['mkb'])
                self.cp('act', mv[:, mt, :, 0:128], pz[1][:, :].rearrange("p (h d) -> p h d", h=4), ['bk1'], ['mkv'])
                self.tt('dve', sq[:], kb[:], kb[:], ALU.mult, ['mkb'], ['msq'])
                S.op('dve', lambda e: e.tensor_reduce(out=n2[:], in_=sq[:].rearrange("p (h d) -> p h d", d=128), op=ALU.add, axis=AX.X), ['msq'], ['mn2'])
                self.tt('dve', kn[:], kn[:], n2[:], ALU.max, ['mn2', 'mem_kn'], ['mem_kn'])
                for h in range(4):
                    self.tr(pt[:, h * P:(h + 1) * P], kb[:, h * P:(h + 1) * P], self.ident_b[:], ['mkb', 'ident_b'], ['bk4'])
                self.cp('act', mkT[:, :, mt * P:(mt + 1) * P], pt[:, 0:512].rearrange("p (h m) -> p h m", h=4), ['bk4'], ['mkv'])
            gk, gkk = self.gmax_bcast(s2, kn[:, 0:4], 'mem_kn', 4, f"mk{l}")
            self.mem_bound(s2, gm, gmk, gk, gkk, negM, 'mem_negM')
            self.S.barrier()
        mq = self.sb(st, "ma_mq", [P, 4, P], BF16)
        E = self.sb(st, "ma_E", [P, 2, 4, P], BF16)
        rz = self.sb(st, "ma_rz", [P, 4], F32)
        tmp = self.sb(st, "ma_tmp", [P, 512], F32)
        mkT_s = self.sb(st, "mkT_s", [P, 4, NMEM], BF16)
        mv_s = self.sb(st, "mv_s", [P, 2, 4, 129], BF16)
        negM_s = self.sb(st, "mem_negM_s", [P, 2], F32)
        kbs = self.sb(st, "mkb_s", [P, 512], BF16)
        sqs = self.sb(st, "msq_s", [P, 512], F32)
        kns = self.sb(st, "mkn_s", [P, 8], F32)
        self.memset('pool', mv_s[:], 1.0, ['mkv_s'])
        return dict(mkT=mkT, mv=mv, negM=negM, bufs=(mq, E, rz, tmp), mkT_s=mkT_s, mv_s=mv_s, negM_s=negM_s, sbufs=(kbs, sqs, kns))

    def mem_sample(self, mem, l, e_, stage, stagek, gm, gmk):
        cfg, S, I = self.cfg, self.S, self.I
        kbs, sqs, kns = mem['sbufs']
        mkT_s, mv_s, negM_s = mem['mkT_s'], mem['mv_s'], mem['negM_s']
        self.memset('dve', kns[:, 0:4], 0.0, ['mkn_s'])
        for mt in range(2):
            self.ld(sqs[:], I['cache_mem'][l, e_, mt * P:(mt + 1) * P, 0:512], (), ['msq_s'])
            self.cp('dve', kbs[:], sqs[:], ['msq_s'], ['mkb_s'])
            self.ld(sqs[:], I['cache_mem'][l, e_, mt * P:(mt + 1) * P, 512:1024], (), ['msq_s'])
            self.cp('dve', mv_s[:, mt, :, 0:128], sqs[:].rearrange("p (h d) -> p h d", h=4), ['msq_s'], ['mkv_s'])
            self.tt('dve', sqs[:], kbs[:], kbs[:], ALU.mult, ['mkb_s'], ['msq_s'])
            S.op('dve', lambda e: e.tensor_reduce(out=kns[:, 4:8], in_=sqs[:].rearrange("p (h d) -> p h d", d=128), op=ALU.add, axis=AX.X), ['msq_s'], ['mkn_s'])
            self.tt('dve', kns[:, 0:4], kns[:, 0:4], kns[:, 4:8], ALU.max, ['mkn_s'], ['mkn_s'])
            for h in range(4):
                self.tr(self.bkb[4][:, h * P:(h + 1) * P], kbs[:, h * P:(h + 1) * P], self.ident_b[:], ['mkb_s', 'ident_b'], ['bk4'])
            self.cp('act', mkT_s[:, :, mt * P:(mt + 1) * P], self.bkb[4][:, 0:512].rearrange("p (h m) -> p h m", h=4), ['bk4'], ['mkv_s'])
        with ExitStack() as s2:
            gk, gkk = self.gmax_bcast(s2, kns[:, 0:4], 'mkn_s', 4, f"mks{l}_{e_}")
            self.mem_bound(s2, gm, gmk, gk, gkk, negM_s, 'mem_negM')
            S.barrier()

    def mem_bound(self, st, gm, gmk, gk, gkk, negM, nk):
        S = self.S
        t1 = self.sb(st, "mb_t1", [P, 2], F32)
        S.op('dve', lambda e: e.reduce_max(out=t1[:, 0:1], in_=gm[:, 12:16], axis=AX.X), [gmk], ['mb_t1'])
        S.op('dve', lambda e: e.reduce_max(out=t1[:, 1:2], in_=gk[:, 0:4], axis=AX.X), [gkk], ['mb_t1'])
        self.tt('dve', t1[:, 0:1], t1[:, 0:1], t1[:, 1:2], ALU.mult, ['mb_t1'], ['mb_t1'])
        self.act(t1[:, 1:2], t1[:, 0:1], AF.Sqrt, ['mb_t1'], ['mb_t1'], scale=1.0 / 128.0)
        self.ts('dve', negM[:, 0:1], t1[:, 1:2], -1.0, None, ALU.mult, None, ['mb_t1'], [nk])

    def mem_attend(self, mem, t, NQ, mqT_src, gate, gatek, out_mix, mkT, mv, kvk, negM):
        S, X = self.S, self.X
        mq, E, rz, tmp = mem['bufs']
        pS = [self.bk[0], self.bk[1]]
        pO = [self.bk[2], self.bk[3]]
        self.ld(mq[:, :, 0:NQ], mqT_src.rearrange("p (h q) -> p h q", h=4)[:, :, 0:NQ] if NQ == P else mqT_src, ['XmqT'], ['ma_mq'])
        for mt in range(2):
            for h in range(4):
                self.mm(pS[mt][:, h * P:h * P + NQ], mkT[:, h, mt * P:(mt + 1) * P], mq[:, h, 0:NQ], True, True, ['ma_mq', kvk], [f'bk{mt}'])
            self.act(E[:, mt, :, 0:NQ], pS[mt][:, :].rearrange("p (h q) -> p h q", h=4)[:, :, 0:NQ], AF.Exp, [f'bk{mt}', 'mem_negM'], ['ma_E'],
                     scale=1.0 / math.sqrt(128.0), bias=negM[:, 0:1])
        for hp in range(2):
            for hh in range(2):
                h = hp * 2 + hh
                for mt in range(2):
                    self.mm(pO[hp][0:NQ, hh * 129:(hh + 1) * 129], E[:, mt, h, 0:NQ], mv[:, mt, h, :], mt == 0, mt == 1, ['ma_E', kvk], [f'bk{2 + hp}'])
            ov = pO[hp][0:NQ, 0:258].rearrange("p (h d) -> p h d", h=2)
            S.op('dve', lambda e: e.reciprocal(out=rz[0:NQ, hp * 2:hp * 2 + 2], in_=ov[:, :, 128]), [f'bk{2 + hp}'], ['ma_rz'])
            self.tt('dve', tmp[0:NQ, hp * 256:(hp + 1) * 256].rearrange("p (h d) -> p h d", h=2), ov[:, :, 0:128],
                    rz[0:NQ, hp * 2:hp * 2 + 2].unsqueeze(2).to_broadcast([NQ, 2, 128]), ALU.mult, [f'bk{2 + hp}', 'ma_rz'], ['ma_tmp'])
        self.tt('dve', out_mix[0:NQ, :], tmp[0:NQ, :], gate, ALU.mult, ['ma_tmp', gatek], ['mixc'])

    def phase_c(self, l):
        cfg, S, I, O, X = self.cfg, self.S, self.I, self.O, self.X
        NT = cfg.NT
        last = (l == DEPTH - 1)
        with ExitStack() as st:
            w1 = self.sb(st, "w_ff1", [P, 8, DFF], BF16)
            w2 = self.sb(st, "w_ff2", [P, 32, D], BF16)
            for k in range(8):
                self.ld(w1[:, k, :], I['w_ff1'][l, k * P:(k + 1) * P, :], (), [f'w_ff1{k}'], q='pool')
            for k in range(32):
                self.ld(w2[:, k, :], I['w_ff2'][l, k * P:(k + 1) * P, :], (), [f'w_ff2{k}'], q='pool')
            self.S.barrier()
            self.load_gain(I['norm2_g'][l:l + 1, :])
            gfin = None
            if last:
                gfin = self.sb(st, "gfin", [P, D], F32)
                self.ld(gfin[:], I['final_norm_g'][0:1, :].partition_broadcast(P), (), ['gfin'])
            xt = [self.sb(st, f"cx{i}", [P, D], F32) for i in range(4)]
            hb = self.sb(st, "chb", [P, D], BF16)
            hT = self.sb(st, "chT", [P, 8, 512], BF16)
            hid = self.sb(st, "hid", [P, 32, 512], BF16)
            rl = [self.sb(st, f"rl{i}", [P, 512], BF16) for i in range(2)]
            yo = [self.sb(st, f"yo{i}", [P, D], F32) for i in range(1)]
            junk = self.sb(st, "cjunk", [P, D], BF16)
            ss = self.sb(st, "css", [P, 2], F32)
            rstd = self.sb(st, "crstd", [P, 2], F32)
            pz = [self.ps(st, f"cpz{i}", [P, 512], F32) for i in range(4)]
            pt = [self.ps(st, f"cpt{i}", [P, 1024], BF16) for i in range(2)]
            zc = 0
            groups = [list(range(s_, min(s_ + 4, NT))) for s_ in range(0, NT, 4)] + [[NT]]
            for gi, tiles in enumerate(groups):
                W = len(tiles) * P
                for j, t in enumerate(tiles):
                    self.ld(xt[j][:], X['x1'][t * P:(t + 1) * P, :], ['x1'], [f'cx{j}'])
                    self.rmsnorm((junk, ss, rstd), xt[j][:], f'cx{j}', hb[:], 'chb', 'gtile')
                    pb = pt[j % 2]; pk = f'cpt{j % 2}'
                    for k in range(8):
                        self.tr(pb[:, k * P:(k + 1) * P], hb[:, k * P:(k + 1) * P], self.ident_b[:], ['chb', 'ident_b'], [pk])
                    self.cp('act', hT[:, :, j * P:(j + 1) * P], pb[:].rearrange("p (k t) -> p k t", k=8), [pk], ['chT'])
                for f in range(32):
                    z = pz[zc % 4]; zk = f'cpz{zc % 4}'; zc += 1
                    for k in range(8):
                        self.mm(z[:, 0:W], w1[:, k, f * P:(f + 1) * P], hT[:, k, 0:W], k == 0, k == 7, [f'w_ff1{k}', 'chT'], [zk])
                    r = rl[f % 2]; rk = f'rl{f % 2}'
                    self.act(r[:, 0:W], z[:, 0:W], AF.Relu, [zk], [rk])
                    self.tt('pool', hid[:, f, 0:W], r[:, 0:W], r[:, 0:W], ALU.mult, [rk], ['hid'])
                for j, t in enumerate(tiles):
                    y = yo[0]; yk = 'yo0'
                    for c in range(2):
                        z = pz[zc % 4]; zk = f'cpz{zc % 4}'; zc += 1
                        for f in range(32):
                            self.mm(z[:, :], hid[:, f, j * P:(j + 1) * P], w2[:, f, c * 512:(c + 1) * 512], f == 0, f == 31,
                                    ['hid', f'w_ff2{f}'], [zk])
                        self.tt('dve', y[:, c * 512:(c + 1) * 512], z[:, :], xt[j][:, c * 512:(c + 1) * 512], ALU.add,
                                [zk, f'cx{j}'], [yk])
                    samp = (t == NT)
                    rows = cfg.NSTOK if samp else P
                    if not last:
                        self.ld(X['x2'][t * P:(t + 1) * P, :], y[:], [yk], ['x2'], q='sp')
                    else:
                        self.act(junk[:], y[:], AF.Square, [yk], ['nrm_junk', 'nrm_ss'], accum_out=ss[:, 0:1])
                        self.act(rstd[:, 0:1], ss[:, 0:1], AF.Sqrt, ['nrm_ss'], ['nrm_rstd'], scale=1.0 / D, bias=EPS)
                        S.op('dve', lambda e: e.reciprocal(out=rstd[:, 1:2], in_=rstd[:, 0:1]), ['nrm_rstd'], ['nrm_rstd2'])
                        self.stt(xt[j][:], y[:], rstd[:, 1:2], gfin[:], ALU.mult, ALU.mult, [yk, 'nrm_rstd2', 'gfin'], [f'cx{j}'])
                        if samp:
                            self.ld(O['y_sample'][:, :], xt[j][0:rows, :], [f'cx{j}'], ['Oy_s'], q='sp')
                        else:
                            self.ld(O['y_prompt'][t * P:(t + 1) * P, :], xt[j][:], [f'cx{j}'], ['Oy_p'], q='sp')

    def page_index_cached(self, ns, e_, l):
        SB = ns['SB']
        return self.page_index(self._pb_stack, e_, "ns", l, SB['pit'])


_CACHE = {}


def _get_program(cfg_key):
    if cfg_key not in _CACHE:
        mk = MK(Cfg(*cfg_key))
        _CACHE[cfg_key] = mk.build()
    return _CACHE[cfg_key]


def kernel(**inp):
    xp = np.asarray(inp['x_prompt'])
    xs = np.asarray(inp['x_sample'])
    B, T, _ = xp.shape
    DB, S, _ = xs.shape
    pt = np.asarray(inp['page_table'])
    PAST = pt.shape[1] * P
    NS = DB // N_CORES
    ccmp = np.asarray(inp['cache_cmp_kv'])
    NPHYS = ccmp.shape[1]
    cfg_key = (T, PAST, NS, S, NPHYS)
    nc = _get_program(cfg_key)
    f32 = np.float32
    ccmp = ccmp.reshape(DEPTH * NPHYS * P, 256)
    cslc = np.asarray(inp['cache_slc_kv']).reshape(DEPTH * NPHYS * P, 256)
    cwin = np.asarray(inp['cache_win_kv']).reshape(DEPTH, DB, -1, 256)
    cmem = np.asarray(inp['cache_mem_kv']).reshape(DEPTH, DB, NMEM, 1024)
    sre = np.asarray(inp['state_ssm_re'])
    sim = np.asarray(inp['state_ssm_im'])
    memp = np.asarray(inp['mem_prompt'])
    shared = {
        'cache_cmp': ccmp, 'cache_slc': cslc,
        'norm1_g': inp['norm1_g'], 'w_in': inp['w_in'], 'cmp_pe': inp['cmp_pe'], 'cmp_w1': inp['cmp_w1'],
        'cmp_w2': inp['cmp_w2'], 'ssm_a_re': inp['ssm_a_re'], 'ssm_a_im': inp['ssm_a_im'],
        'ssm_log_dt': inp['ssm_log_dt'], 'ssm_b_re': inp['ssm_b_re'], 'ssm_b_im': inp['ssm_b_im'],
        'ssm_c_re': inp['ssm_c_re'], 'ssm_c_im': inp['ssm_c_im'],
        'ssm_d': np.asarray(inp['ssm_d']).reshape(DEPTH, NSG * 16),
        'w_glu': inp['w_glu'], 'b_glu': inp['b_glu'], 'mem_norm_g': inp['mem_norm_g'], 'w_mem_kv': inp['w_mem_kv'],
        'w_o': inp['w_o'], 'norm2_g': inp['norm2_g'], 'w_ff1': inp['w_ff1'], 'w_ff2': inp['w_ff2'],
        'final_norm_g': np.asarray(inp['final_norm_g']).reshape(1, D),
    }
    shared = {k: np.ascontiguousarray(np.asarray(v)) for k, v in shared.items()}
    in_maps = []
    for c in range(N_CORES):
        b = c % B
        m = dict(shared)
        m['x_prompt'] = np.ascontiguousarray(xp[b])
        m['mem_prompt'] = np.ascontiguousarray(memp[b])
        sl = slice(c * NS, (c + 1) * NS)
        m['x_sample'] = np.ascontiguousarray(xs[sl].reshape(NS * S, D))
        m['cache_win'] = np.ascontiguousarray(cwin[:, sl])
        m['ssm_re'] = np.ascontiguousarray(sre[:, sl])
        m['ssm_im'] = np.ascontiguousarray(sim[:, sl])
        m['cache_mem'] = np.ascontiguousarray(cmem[:, sl])
        m['page_table'] = np.ascontiguousarray(pt[sl])
        in_maps.append(m)
    res = run_bass_kernel_spmd(nc, in_maps, core_ids=list(range(N_CORES))).results
    WP = min(WINDOW, T)
    WB = min(WINDOW, PAST)

    def pstack(name, shape_tail):
        return np.stack([res[b][name] for b in range(B)], axis=1).reshape((DEPTH, B) + shape_tail)

    def sstack(name, shape_tail):
        return np.concatenate([res[c][name].reshape((DEPTH, NS) + shape_tail) for c in range(N_CORES)], axis=1)

    y_prompt = np.stack([res[b]['y_prompt'] for b in range(B)], axis=0)
    y_sample = np.concatenate([res[c]['y_sample'].reshape(NS, S, D) for c in range(N_CORES)], axis=0)
    outs = (
        y_prompt, y_sample,
        pstack('cmp_p', (T, 2, NG, HD)), sstack('cmp_s', (S, 2, NG, HD)),
        pstack('slc_p', (T, 2, NG, HD)), sstack('slc_s', (S, 2, NG, HD)),
        pstack('win_p', (WP, 2, NG, HD)), sstack('win_s', (WB, 2, NG, HD)),
        pstack('hr_p', (NSG, SST)), pstack('hi_p', (NSG, SST)),
        sstack('hr_s', (NSG, SST)), sstack('hi_s', (NSG, SST)),
        pstack('mem_p', (NMEM, 2, 4, 128)),
    )
    return tuple(np.ascontiguousarray(o.astype(f32)) for o in outs)
```

```python
import math
from contextlib import ExitStack

import numpy as np
import concourse.bass as bass
import concourse.mybir as mybir
from concourse.bass_utils import run_bass_kernel_spmd

F32 = mybir.dt.float32
BF16 = mybir.dt.bfloat16
I32 = mybir.dt.int32
U32 = mybir.dt.uint32
AF = mybir.ActivationFunctionType
ALU = mybir.AluOpType
AX = mybir.AxisListType

import os
KCUT = int(os.environ.get('KCUT', '0'))
KTILES = int(os.environ.get('KTILES', '0'))
KOUTQ = os.environ.get('KOUTQ', 'pool')
KVBENG = os.environ.get('KVBENG', 'act')
KSSM = os.environ.get('KSSM', '')
KSSMT = os.environ.get('KSSMT', '')
KNSA = os.environ.get('KNSA', '')
BRANCHES = set('abc')
D = 1024
DEPTH = 2
N_CORES = 8
NH = 8
HD = 64
NG = 2
HPG = 4
IN_W = 3864
OFF_KV = 512
OFF_GN = 1280
OFF_U = 1304
OFF_MQ = 1816
OFF_GB = 2328
MIXW = 1536
DFF = 4096
EPS = 1e-6
WINDOW = 512
TOPK = 16
NSG = 32
SST = 64
NMEM = 256
P = 128


class Sched:
    LIMIT = 30000
    NDMA = 40

    def __init__(self, nc):
        self.nc = nc
        self.eng = dict(pe=nc.tensor, dve=nc.vector, act=nc.scalar, pool=nc.gpsimd, sp=nc.sync)
        self.sem = {}
        self.cnt = {}
        self.nsem = 0
        for e in self.eng:
            self._new_sem(e)
        self.waited = {e: {} for e in self.eng}
        self.lastw = {}
        self.readers = {}
        self.dma_sems = [nc.alloc_semaphore(f"dq{i}") for i in range(self.NDMA)]
        self.dma_val = [0] * self.NDMA
        self.dma_rng = {'sp': (0, 16), 'act': (16, 24), 'pool': (24, self.NDMA)}
        self.dma_i = {q: lo for q, (lo, hi) in self.dma_rng.items()}
        self.n_instr = 0

    def _new_sem(self, e):
        self.sem[e] = self.nc.alloc_semaphore(f"es_{e}_{self.nsem}")
        self.nsem += 1
        self.cnt[e] = 0

    def _wait(self, e, tok):
        sem, val, src = tok
        if src == e and e == 'pe':
            return
        key = id(sem)
        if self.waited[e].get(key, 0) >= val:
            return
        self.eng[e].wait_ge(sem, val)
        self.waited[e][key] = val

    def _deps(self, e, R, W):
        for k in R:
            if k in self.lastw:
                self._wait(e, self.lastw[k])
        for k in W:
            if k in self.lastw:
                self._wait(e, self.lastw[k])
            for tok in self.readers.get(k, {}).values():
                self._wait(e, tok)

    def _record(self, tok, R, W):
        for k in W:
            self.lastw[k] = tok
            self.readers[k] = {}
        for k in R:
            self.readers.setdefault(k, {})[tok[2] if tok[2] != 'dma' else ('dma', id(tok[0]))] = tok

    def op(self, e, fn, R=(), W=()):
        self._deps(e, R, W)
        ins = fn(self.eng[e])
        if self.cnt[e] >= self.LIMIT:
            self._new_sem(e)
        self.cnt[e] += 1
        ins.then_inc(self.sem[e], 1)
        self._record((self.sem[e], self.cnt[e], e), R, W)
        self.n_instr += 1
        return ins

    def dma(self, q, out, in_, R=(), W=(), **kw):
        self._deps(q, R, W)
        i = self.dma_i[q]
        lo, hi = self.dma_rng[q]
        self.dma_i[q] = lo + (i + 1 - lo) % (hi - lo)
        if self.dma_val[i] > 0:
            self._wait(q, (self.dma_sems[i], self.dma_val[i], 'dma'))
        ins = self.eng[q].dma_start(out=out, in_=in_, **kw)
        self.dma_val[i] += 16
        ins.then_inc(self.dma_sems[i], 16)
        self._record((self.dma_sems[i], self.dma_val[i], 'dma'), R, W)
        self.n_instr += 1
        return ins

    def idma_batch(self, items):
        self.barrier()
        for (out, in_, idx_ap, R, W) in items:
            self._idma1(out, in_, idx_ap, R, W)
        self.barrier()

    def idma(self, out, in_, idx_ap, R=(), W=()):
        self.barrier()
        ins = self._idma1(out, in_, idx_ap, R, W)
        self.barrier()
        return ins

    def _idma1(self, out, in_, idx_ap, R=(), W=()):
        q = 'pool'
        self._deps(q, R, W)
        i = self.dma_i[q]
        lo, hi = self.dma_rng[q]
        self.dma_i[q] = lo + (i + 1 - lo) % (hi - lo)
        if self.dma_val[i] > 0:
            self._wait(q, (self.dma_sems[i], self.dma_val[i], 'dma'))
        ins = self.eng[q].indirect_dma_start(out=out, out_offset=None, in_=in_, in_offset=bass.IndirectOffsetOnAxis(ap=idx_ap, axis=0))
        self.dma_val[i] += 16
        ins.then_inc(self.dma_sems[i], 16)
        self._record((self.dma_sems[i], self.dma_val[i], 'dma'), R, W)
        self.n_instr += 1
        return ins

    def barrier(self):
        toks = [(self.sem[e], self.cnt[e], e) for e in self.eng if self.cnt[e] > 0]
        toks += [(self.dma_sems[i], self.dma_val[i], 'dma') for i in range(self.NDMA) if self.dma_val[i] > 0]
        for e in self.eng:
            for tok in toks:
                if tok[2] == e:
                    continue
                self._wait(e, tok)

    def finish(self):
        for i in range(self.NDMA):
            if self.dma_val[i] > 0:
                self._wait('sp', (self.dma_sems[i], self.dma_val[i], 'dma'))
        for e in self.eng:
            if e != 'sp' and self.cnt[e] > 0:
                self._wait('sp', (self.sem[e], self.cnt[e], e))


class Cfg:
    def __init__(self, T=8192, PAST=16384, NS=4, S=4, NPHYS=5120):
        self.T = T
        self.NT = T // P
        self.PAST = PAST
        self.NS = NS
        self.S = S
        self.NSTOK = NS * S
        self.NPHYS = NPHYS
        self.NPAGE = PAST // P
        self.WB = min(WINDOW, PAST)
        self.WP = min(WINDOW, T)


def slopes():
    return [2.0 ** (-(h + 1)) for h in range(NH)]


class MK:
    def __init__(self, cfg):
        self.cfg = cfg
        self.nc = bass.Bass("TRN2", target_bir_lowering=False)
        self.S = None
        self.uid = 0

    def sb(self, st, name, shape, dt):
        self.uid += 1
        return st.enter_context(self.nc.sbuf_tensor(f"{name}_{self.uid}", list(shape), dt))

    def ps(self, st, name, shape, dt):
        self.uid += 1
        return st.enter_context(self.nc.psum_tensor(f"{name}_{self.uid}", list(shape), dt))

    def dram(self, name, shape, dt, kind="Internal"):
        return self.nc.dram_tensor(name, list(shape), dt, kind=kind).ap()

    def mm(self, out, lhsT, rhs, start, stop, R, W):
        return self.S.op('pe', lambda e: e.matmul(out, lhsT=lhsT, rhs=rhs, start=start, stop=stop), R, W)

    def tr(self, out, in_, ident, R, W):
        return self.S.op('pe', lambda e: e.transpose(out, in_, ident), R, W)

    def act(self, out, in_, func, R, W, **kw):
        return self.S.op('act', lambda e: e.activation(out=out, in_=in_, func=func, **kw), R, W)

    def tt(self, eng, out, in0, in1, op, R, W):
        return self.S.op(eng, lambda e: e.tensor_tensor(out=out, in0=in0, in1=in1, op=op), R, W)

    def ts(self, eng, out, in0, s1, s2, op0, op1, R, W):
        if op1 is None:
            return self.S.op(eng, lambda e: e.tensor_scalar(out=out, in0=in0, scalar1=s1, scalar2=None, op0=op0), R, W)
        return self.S.op(eng, lambda e: e.tensor_scalar(out=out, in0=in0, scalar1=s1, scalar2=s2, op0=op0, op1=op1), R, W)

    def stt(self, out, in0, scalar, in1, op0, op1, R, W):
        return self.S.op('dve', lambda e: e.scalar_tensor_tensor(out=out, in0=in0, scalar=scalar, in1=in1, op0=op0, op1=op1), R, W)

    def cp(self, eng, out, in_, R, W):
        if eng == 'act':
            return self.S.op('act', lambda e: e.copy(out=out, in_=in_), R, W)
        return self.S.op(eng, lambda e: e.tensor_copy(out=out, in_=in_), R, W)

    def memset(self, eng, out, val, W):
        return self.S.op(eng, lambda e: e.memset(out, val), (), W)

    def ld(self, out, in_, R, W, q='sp', **kw):
        return self.S.dma(q, out, in_, R, W, **kw)

    def build(self):
        cfg = self.cfg
        nc = self.nc
        T, NT, NS, NSTOK = cfg.T, cfg.NT, cfg.NS, cfg.NSTOK
        NTT = NT + 1
        d = self.dram
        I = {}
        I['x_prompt'] = d("x_prompt", [T, D], F32, "ExternalInput")
        I['x_sample'] = d("x_sample", [NSTOK, D], F32, "ExternalInput")
        I['cache_cmp'] = d("cache_cmp", [DEPTH * cfg.NPHYS * P, 256], F32, "ExternalInput")
        I['cache_slc'] = d("cache_slc", [DEPTH * cfg.NPHYS * P, 256], F32, "ExternalInput")
        I['cache_win'] = d("cache_win", [DEPTH, NS, cfg.WB, 256], F32, "ExternalInput")
        I['ssm_re'] = d("ssm_re", [DEPTH, NS, NSG, SST], F32, "ExternalInput")
        I['ssm_im'] = d("ssm_im", [DEPTH, NS, NSG, SST], F32, "ExternalInput")
        I['cache_mem'] = d("cache_mem", [DEPTH, NS, NMEM, 1024], F32, "ExternalInput")
        I['page_table'] = d("page_table", [NS, cfg.NPAGE], I32, "ExternalInput")
        I['mem_prompt'] = d("mem_prompt", [NMEM, D], F32, "ExternalInput")
        I['norm1_g'] = d("norm1_g", [DEPTH, D], F32, "ExternalInput")
        I['w_in'] = d("w_in", [DEPTH, D, IN_W], F32, "ExternalInput")
        I['cmp_pe'] = d("cmp_pe", [DEPTH, 2, 32, 64], F32, "ExternalInput")
        I['cmp_w1'] = d("cmp_w1", [DEPTH, 2, 32, 64, 64], F32, "ExternalInput")
        I['cmp_w2'] = d("cmp_w2", [DEPTH, 2, 64, 64], F32, "ExternalInput")
        I['ssm_a_re'] = d("ssm_a_re", [DEPTH, NSG, SST], F32, "ExternalInput")
        I['ssm_a_im'] = d("ssm_a_im", [DEPTH, NSG, SST], F32, "ExternalInput")
        I['ssm_log_dt'] = d("ssm_log_dt", [DEPTH, NSG], F32, "ExternalInput")
        I['ssm_b_re'] = d("ssm_b_re", [DEPTH, NSG, SST, 16], F32, "ExternalInput")
        I['ssm_b_im'] = d("ssm_b_im", [DEPTH, NSG, SST, 16], F32, "ExternalInput")
        I['ssm_c_re'] = d("ssm_c_re", [DEPTH, NSG, 16, SST], F32, "ExternalInput")
        I['ssm_c_im'] = d("ssm_c_im", [DEPTH, NSG, 16, SST], F32, "ExternalInput")
        I['ssm_d'] = d("ssm_d", [DEPTH, NSG * 16], F32, "ExternalInput")
        I['w_glu'] = d("w_glu", [DEPTH, 512, 512], F32, "ExternalInput")
        I['b_glu'] = d("b_glu", [DEPTH, 512], F32, "ExternalInput")
        I['mem_norm_g'] = d("mem_norm_g", [DEPTH, D], F32, "ExternalInput")
        I['w_mem_kv'] = d("w_mem_kv", [DEPTH, D, 1024], F32, "ExternalInput")
        I['w_o'] = d("w_o", [DEPTH, MIXW, D], F32, "ExternalInput")
        I['norm2_g'] = d("norm2_g", [DEPTH, D], F32, "ExternalInput")
        I['w_ff1'] = d("w_ff1", [DEPTH, D, DFF], F32, "ExternalInput")
        I['w_ff2'] = d("w_ff2", [DEPTH, DFF, D], F32, "ExternalInput")
        I['final_norm_g'] = d("final_norm_g", [1, D], F32, "ExternalInput")
        O = {}
        O['y_prompt'] = d("y_prompt", [T, D], F32, "ExternalOutput")
        O['y_sample'] = d("y_sample", [NSTOK, D], F32, "ExternalOutput")
        O['cmp_p'] = d("cmp_p", [DEPTH, T, 256], F32, "ExternalOutput")
        O['cmp_s'] = d("cmp_s", [DEPTH, NSTOK, 256], F32, "ExternalOutput")
        O['slc_p'] = d("slc_p", [DEPTH, T, 256], F32, "ExternalOutput")
        O['slc_s'] = d("slc_s", [DEPTH, NSTOK, 256], F32, "ExternalOutput")
        O['win_p'] = d("win_p", [DEPTH, cfg.WP, 256], F32, "ExternalOutput")
        O['win_s'] = d("win_s", [DEPTH, NS, cfg.WB, 256], F32, "ExternalOutput")
        O['hr_p'] = d("hr_p", [DEPTH, NSG, SST], F32, "ExternalOutput")
        O['hi_p'] = d("hi_p", [DEPTH, NSG, SST], F32, "ExternalOutput")
        O['hr_s'] = d("hr_s", [DEPTH, NS, NSG, SST], F32, "ExternalOutput")
        O['hi_s'] = d("hi_s", [DEPTH, NS, NSG, SST], F32, "ExternalOutput")
        O['mem_p'] = d("mem_p", [DEPTH, NMEM, 1024], F32, "ExternalOutput")
        self.I, self.O = I, O
        X = {}
        X['x1'] = d("scr_x1", [NTT * P, D], F32)
        X['x2'] = d("scr_x2", [NTT * P, D], F32)
        X['qT'] = d("scr_qT", [NTT, P, 512], BF16)
        X['kTs'] = d("scr_kTs", [NTT, P, P], BF16)
        X['kTw'] = d("scr_kTw", [NTT, P, P], BF16)
        X['cTk'] = d("scr_cTk", [NTT, P, P], BF16)
        X['cTv'] = d("scr_cTv", [NTT, P, P], BF16)
        X['vs'] = d("scr_vs", [NTT, P, 130], BF16)
        X['vw'] = d("scr_vw", [NTT, P, 130], BF16)
        X['uT'] = d("scr_uT", [NTT, P, 512], BF16)
        X['mqT'] = d("scr_mqT", [NTT, P, 512], BF16)
        X['gb'] = d("scr_gb", [NTT, P, 1024], BF16)
        X['gbT'] = d("scr_gbT", [NTT, P, 512], BF16)
        X['gn'] = d("scr_gn", [NTT, P, 24], F32)
        X['kaug'] = d("scr_kaug", [3, T], BF16)
        NCS = cfg.PAST // 16 - 1
        NCSL = (NCS + P - 1) // P
        X['sck'] = d("scr_sck", [NS, NG, 64, NCSL * P], BF16)
        X['scv'] = d("scr_scv", [NS, NCSL, P, P], BF16)
        self.X = X

        self.S = Sched(nc)
        with ExitStack() as top:
            self.consts(top)
            import os as _os
            stop = _os.environ.get("KSTOP", "")
            for l in range(DEPTH):
                self.phase_a(l)
                self.S.barrier()
                if stop == f"a{l}":
                    break
                self.phase_b(l)
                self.S.barrier()
                if stop == f"b{l}":
                    break
                self.phase_c(l)
                self.S.barrier()
                if stop == f"c{l}":
                    break
            self.S.finish()
        return nc

    def consts(self, st):
        S = self.S
        self.ident_b = self.sb(st, "ident_b", [P, P], BF16)
        self.ident_f = self.sb(st, "ident_f", [P, P], F32)
        it = self.sb(st, "iota_tmp", [P, P], I32)
        self.iota_jp = it
        S.op('pool', lambda e: e.iota(it[:], pattern=[[1, P]], base=0, channel_multiplier=-1), (), ['iota_tmp'])
        self.ts('dve', self.ident_b[:], it[:], 0, None, ALU.is_equal, None, ['iota_tmp'], ['ident_b'])
        self.ts('dve', self.ident_f[:], it[:], 0, None, ALU.is_equal, None, ['iota_tmp'], ['ident_f'])
        self.fill0 = self.nc.gpsimd.to_reg(0.0)
        self.fillm1 = self.nc.gpsimd.to_reg(-1.0)
        self.ones_f = self.sb(st, "ones_f", [P, P], F32)
        self.memset('dve', self.ones_f[:], 1.0, ['ones_f'])
        self.nmax = self.sb(st, "nmax", [P, 16], F32)
        self.gtile = self.sb(st, "gtile", [P, D], F32)

    def rmsnorm(self, st_bufs, x, xkey, hout, hkey, gkey):
        junk, ss, rstd = st_bufs
        self.act(junk[:], x, AF.Square, [xkey], ['nrm_junk', 'nrm_ss'], accum_out=ss[:, 0:1])
        self.act(rstd[:, 0:1], ss[:, 0:1], AF.Sqrt, ['nrm_ss'], ['nrm_rstd'], scale=1.0 / D, bias=EPS)
        self.S.op('dve', lambda e: e.reciprocal(out=rstd[:, 1:2], in_=rstd[:, 0:1]), ['nrm_rstd'], ['nrm_rstd2'])
        self.stt(hout, x, rstd[:, 1:2], self.gtile[:], ALU.mult, ALU.mult, [xkey, 'nrm_rstd2', gkey], [hkey])

    def load_gain(self, g_ap_row):
        self.ld(self.gtile[:], g_ap_row.partition_broadcast(P), (), ['gtile'])

    def phase_a(self, l):
        cfg, S, I, O, X = self.cfg, self.S, self.I, self.O, self.X
        NT = cfg.NT
        with ExitStack() as st:
            win = self.sb(st, "w_in", [P, 8, IN_W], BF16)
            for k in range(8):
                self.ld(win[:, k, :], I['w_in'][l, k * P:(k + 1) * P, :], (), [f'w_in{k}'], q='pool')
            self.S.barrier()
            self.load_gain(I['norm1_g'][l:l + 1, :])
            xt = [self.sb(st, f"xt{i}", [P, D], F32) for i in range(2)]
            hb = [self.sb(st, f"hb{i}", [P, D], BF16) for i in range(2)]
            hT = [self.sb(st, f"hT{i}", [P, D], BF16) for i in range(2)]
            junk = self.sb(st, "junk", [P, D], BF16)
            ss = self.sb(st, "ss", [P, 2], F32)
            rstd = self.sb(st, "rstd", [P, 2], F32)
            kvst = [self.sb(st, f"kvst{i}", [P, 768], F32) for i in range(2)]
            qb = self.sb(st, "qb", [P, 512], BF16)
            sq = self.sb(st, "sq", [P, 512], F32)
            n2 = self.sb(st, "n2", [P, 16], F32)
            gnst = self.sb(st, "gnst", [P, 24], F32)
            kvb = self.sb(st, "kvb", [P, 768], BF16)
            tsb = [self.sb(st, f"tsb{i}", [P, 1024], BF16) for i in range(2)]
            vst = [self.sb(st, f"vst{i}", [P, 2, 2, 65], BF16) for i in range(2)]
            ub = self.sb(st, "ub", [P, 512], BF16)
            mqb = self.sb(st, "mqb", [P, 512], BF16)
            gbb = [self.sb(st, f"gbb{i}", [P, MIXW], BF16) for i in range(2)]
            pz = [self.ps(st, f"pz{i}", [P, 512], F32) for i in range(3)]
            pt = [self.ps(st, f"pt{i}", [P, 1024], BF16) for i in range(2)]
            for i in range(2):
                self.memset('pool', vst[i][:], 1.0, [f'vst{i}'])
            if l == 0:
                self.memset('dve', self.nmax[:], 0.0, ['nmax'])
            else:
                self.memset('dve', self.nmax[:], 0.0, ['nmax'])
            zc = 0
            ptc = 0
            for t in range(NT + 1):
                if KTILES and t >= KTILES:
                    continue
                b = t % 2
                samp = (t == NT)
                rows = cfg.NSTOK if samp else P
                if l == 0:
                    if samp:
                        self.memset('pool', xt[b][:], 0.0, [f'xt{b}'])
                        self.ld(xt[b][0:rows, :], I['x_sample'][:, :], (), [f'xt{b}'])
                    else:
                        self.ld(xt[b][:], I['x_prompt'][t * P:(t + 1) * P, :], (), [f'xt{b}'])
                else:
                    self.ld(xt[b][:], X['x2'][t * P:(t + 1) * P, :], ['x2'], [f'xt{b}'])
                self.rmsnorm((junk, ss, rstd), xt[b][:], f'xt{b}', hb[b][:], f'hb{b}', 'gtile')
                if KCUT == 1:
                    continue
                pb = pt[ptc % 2]; pk = f'pt{ptc % 2}'; ptc += 1
                for k in range(8):
                    self.tr(pb[:, k * P:(k + 1) * P], hb[b][:, k * P:(k + 1) * P], self.ident_b[:], [f'hb{b}', 'ident_b'], [pk])
                self.cp('act', hT[b][:], pb[:], [pk], [f'hT{b}'])

                def proj(c0, w):
                    nonlocal zc
                    z = pz[zc % 3]; zk = f'pz{zc % 3}'; zc += 1
                    for k in range(8):
                        self.mm(z[:, 0:w], hT[b][:, k * P:(k + 1) * P], win[:, k, c0:c0 + w], k == 0, k == 7,
                                [f'hT{b}', f'w_in{k}'], [zk])
                    return z, zk

                if KCUT == 2:
                    continue
                z, zk = proj(0, 512)
                self.cp('act', qb[:].rearrange("p (h g d) -> p h g d", h=HPG, g=NG),
                        z[:].rearrange("p (g h d) -> p h g d", g=NG, h=HPG), [zk], ['qb'])
                self.tt('dve', sq[:], qb[:], qb[:], ALU.mult, ['qb'], ['sq'])
                S.op('dve', lambda e: e.tensor_reduce(out=n2[:, 0:8], in_=sq[:].rearrange("p (h d) -> p h d", d=HD),
                                                      op=ALU.add, axis=AX.X), ['sq'], ['n2'])
                pb = pt[ptc % 2]; pk = f'pt{ptc % 2}'; ptc += 1
                for hl in range(HPG):
                    self.tr(pb[:, hl * P:(hl + 1) * P], qb[:, hl * P:(hl + 1) * P], self.ident_b[:], ['qb', 'ident_b'], [pk])
                self.cp('act', tsb[b][:, 0:512], pb[:, 0:512], [pk], [f'tsbq{b}'])
                self.ld(X['qT'][t], tsb[b][:, 0:512], [f'tsbq{b}'], ['XqT'], q='sp')
                if KCUT == 3:
                    continue
                z, zk = proj(512, 512)
                self.cp('act', kvst[b][:, 0:512], z[:], [zk], [f'kvst{b}'])
                if KCUT == 31:
                    continue
                self.cp(KVBENG, kvb[:, 0:512], z[:], [zk], ['kvb'])
                if KCUT == 32:
                    continue
                z2, zk2 = proj(1024, 280)
                self.cp('act', kvst[b][:, 512:768], z2[:, 0:256], [zk2], [f'kvst{b}'])
                self.cp(KVBENG, kvb[:, 512:768], z2[:, 0:256], [zk2], ['kvb'])
                if KCUT == 33:
                    continue
                self.act(gnst[:, :], z2[:, 256:280], AF.Sigmoid, [zk2], ['gnst'])
                self.ld(X['gn'][t], gnst[:, :], ['gnst'], ['Xgn'])
                if KCUT == 34:
                    continue
                if samp:
                    self.ld(O['cmp_s'][l], kvst[b][0:rows, 0:256], [f'kvst{b}'], ['Ocmp_s'], q='sp')
                    self.ld(O['slc_s'][l], kvst[b][0:rows, 256:512], [f'kvst{b}'], ['Oslc_s'], q='sp')
                    for e_ in range(cfg.NS):
                        self.ld(O['win_s'][l, e_, cfg.WB - cfg.S:cfg.WB, :], kvst[b][e_ * cfg.S:(e_ + 1) * cfg.S, 512:768],
                                [f'kvst{b}'], ['Owin_s'], q='sp')
                else:
                    self.ld(O['cmp_p'][l, t * P:(t + 1) * P, :], kvst[b][:, 0:256], [f'kvst{b}'], ['Ocmp_p'], q='sp')
                    self.ld(O['slc_p'][l, t * P:(t + 1) * P, :], kvst[b][:, 256:512], [f'kvst{b}'], ['Oslc_p'], q='sp')
                    r0 = t * P - (cfg.T - cfg.WP)
                    if r0 >= 0:
                        self.ld(O['win_p'][l, r0:r0 + P, :], kvst[b][:, 512:768], [f'kvst{b}'], ['Owin_p'], q='sp')
                if KCUT == 4:
                    continue
                self.tt('dve', sq[:, 0:128], kvb[:, 256:384], kvb[:, 256:384], ALU.mult, ['kvb'], ['sq'])
                self.tt('dve', sq[:, 128:256], kvb[:, 512:640], kvb[:, 512:640], ALU.mult, ['kvb'], ['sq'])
                S.op('dve', lambda e: e.tensor_reduce(out=n2[:, 8:12], in_=sq[:, 0:256].rearrange("p (h d) -> p h d", d=HD),
                                                      op=ALU.add, axis=AX.X), ['sq'], ['n2'])
                pb = pt[ptc % 2]; pk = f'pt{ptc % 2}'; ptc += 1
                for j, c0 in enumerate((0, 128, 256, 512)):
                    self.tr(pb[:, j * P:(j + 1) * P], kvb[:, c0:c0 + 128], self.ident_b[:], ['kvb', 'ident_b'], [pk])
                self.cp('act', tsb[b][:, 512:1024], pb[:, 0:512], [pk], [f'tsbk{b}'])
                self.ld(X['cTk'][t], tsb[b][:, 512:640], [f'tsbk{b}'], ['XcTk'], q='sp')
                self.ld(X['cTv'][t], tsb[b][:, 640:768], [f'tsbk{b}'], ['XcTv'], q='sp')
                self.ld(X['kTs'][t], tsb[b][:, 768:896], [f'tsbk{b}'], ['XkTs'], q='sp')
                self.ld(X['kTw'][t], tsb[b][:, 896:1024], [f'tsbk{b}'], ['XkTw'], q='sp')
                if KCUT == 5:
                    continue
                self.cp('dve', vst[b][:, 0, :, 0:64], kvb[:, 384:512].rearrange("p (g d) -> p g d", g=NG), ['kvb'], [f'vst{b}'])
                self.cp('dve', vst[b][:, 1, :, 0:64], kvb[:, 640:768].rearrange("p (g d) -> p g d", g=NG), ['kvb'], [f'vst{b}'])
                self.ld(X['vs'][t], vst[b][:, 0].rearrange("p g d -> p (g d)"), [f'vst{b}'], ['Xvs'], q='sp')
                self.ld(X['vw'][t], vst[b][:, 1].rearrange("p g d -> p (g d)"), [f'vst{b}'], ['Xvw'], q='sp')
                if KCUT == 6:
                    continue
                z, zk = proj(OFF_U, 512)
                self.cp('act', ub[:], z[:], [zk], ['ub'])
                pb = pt[ptc % 2]; pk = f'pt{ptc % 2}'; ptc += 1
                for j in range(4):
                    self.tr(pb[:, j * P:(j + 1) * P], ub[:, j * P:(j + 1) * P], self.ident_b[:], ['ub', 'ident_b'], [pk])
                z, zk = proj(OFF_MQ, 512)
                self.cp('act', mqb[:], z[:], [zk], ['mqb'])
                self.tt('dve', sq[:], mqb[:], mqb[:], ALU.mult, ['mqb'], ['sq'])
                S.op('dve', lambda e: e.tensor_reduce(out=n2[:, 12:16], in_=sq[:].rearrange("p (h d) -> p h d", d=128),
                                                      op=ALU.add, axis=AX.X), ['sq'], ['n2'])
                self.tt('dve', self.nmax[:], self.nmax[:], n2[:], ALU.max, ['n2', 'nmax'], ['nmax'])
                for j in range(4):
                    self.tr(pb[:, (4 + j) * P:(5 + j) * P], mqb[:, j * P:(j + 1) * P], self.ident_b[:], ['mqb', 'ident_b'], [pk])
                tb2 = tsb[b]
                self.cp('act', hb[b][:], pb[:], [pk], [f'hb{b}'])
                self.ld(X['uT'][t], hb[b][:, 0:512], [f'hb{b}'], ['XuT'], q='sp')
                self.ld(X['mqT'][t], hb[b][:, 512:1024], [f'hb{b}'], ['XmqT'], q='sp')
                if KCUT == 7:
                    continue
                for c in range(3):
                    z, zk = proj(OFF_GB + 512 * c, 512)
                    self.act(gbb[b][:, c * 512:(c + 1) * 512], z[:], AF.Sigmoid, [zk], [f'gbb{b}'])
                self.ld(X['gb'][t][:, 0:512], gbb[b][:, 0:512], [f'gbb{b}'], ['Xgb'], q='sp')
                self.ld(X['gb'][t][:, 512:1024], gbb[b][:, 1024:1536], [f'gbb{b}'], ['Xgb'], q='sp')
                pb = pt[ptc % 2]; pk = f'pt{ptc % 2}'; ptc += 1
                for j in range(4):
                    self.tr(pb[:, j * P:(j + 1) * P], gbb[b][:, 512 + j * P:512 + (j + 1) * P], self.ident_b[:], [f'gbb{b}', 'ident_b'], [pk])
                self.cp('act', tsb[b][:, 0:512], pb[:, 0:512], [pk], [f'tsbq{b}'])
                self.ld(X['gbT'][t], tsb[b][:, 0:512], [f'tsbq{b}'], ['XgbT'], q='sp')

    def gmax_bcast(self, st, src, srck, n, name):
        S = self.S
        pT = self.bk[6]
        pB = self.bk[7]
        vm = self.sb(st, f"gm_vm_{name}", [16, 1], F32)
        dg = self.sb(st, f"gm_dg_{name}", [16, 16], F32)
        out = self.sb(st, f"gm_out_{name}", [P, 16], F32)
        k = f"gm_{name}"
        self.tr(pT[0:n, 0:P], src, self.ident_f[:], [srck, 'ident_f'], ['bk6'])
        S.op('dve', lambda e: e.reduce_max(out=vm[0:n, :], in_=pT[0:n, 0:P], axis=AX.X), ['bk6'], [k + 'vm'])
        self.ts('dve', dg[0:n, 0:n], self.ident_f[0:n, 0:n], vm[0:n, 0:1], None, ALU.mult, None, [k + 'vm', 'ident_f'], [k + 'dg'])
        self.mm(pB[:, 0:n], self.ones_f[0:n, :], dg[0:n, 0:n], True, True, [k + 'dg', 'ones_f'], ['bk7'])
        self.cp('act', out[:, 0:n], pB[:, 0:n], ['bk7'], [k + 'out'])
        return out, k + 'out'

    def phase_b(self, l):
        cfg, S, I, O, X = self.cfg, self.S, self.I, self.O, self.X
        NT = cfg.NT
        with ExitStack() as st:
            self._pb_stack = st
            self.bk = [self.ps(st, f"bk{i}", [P, 512], F32) for i in range(8)]
            self.bkb = [b[:].bitcast(BF16) for b in self.bk]
            gm, gmk = self.gmax_bcast(st, self.nmax[:, 0:16], 'nmax', 16, f"a{l}")
            mem = self.mem_setup(st, l, gm, gmk)
            if 'a' in BRANCHES:
                ns = self.nsa_setup(st, l, gm, gmk)
            if 'b' in BRANCHES:
                ss = self.ssm_setup(st, l)
                sbufs = self.ssm_bufs(st)
                s_init = self.sb(st, "s_init", [P, NSG * cfg.NS], F32)
                s_h1 = self.sb(st, "s_h1", [P, NSG * cfg.NS], F32)
                s_h2 = self.sb(st, "s_h2", [P, NSG * cfg.NS], F32)
                self._st_tiles = (self.sb(st, "s_stT1", [P, P], F32), self.sb(st, "s_stT2", [P, P], F32))
                self.memset('dve', s_init[:], 0.0, ['s_init'])
            wo = self.sb(st, "w_o", [P, 12, D], BF16)
            for k in range(12):
                self.ld(wo[:, k, :], I['w_o'][l, k * P:(k + 1) * P, :], (), [f'w_o{k}'], q='pool')
            self.S.barrier()
            xt = [self.sb(st, f"bx{i}", [P, D], F32) for i in range(1)] * 2
            mixT = [self.sb(st, f"mixT{i}", [P, 12, P], BF16) for i in range(1)] * 2
            gbt = [self.sb(st, f"gbt{i}", [P, 1024], BF16) for i in range(1)] * 2
            mixc = self.sb(st, "mixc", [P, 512], BF16)
            gates_s = self.sb(st, "gates_s", [4, 1024], BF16)
            gnt_s = self.sb(st, "gnt_s", [4, 24], F32)
            gnt_p = self.sb(st, "gnt_p", [P, 24], F32)
            py = [self.bk[5], self.bk[6]]
            for t in range(NT + 1):
                b = 0
                samp = t == NT
                rows = cfg.NSTOK if samp else P
                if l == 0:
                    if samp:
                        self.memset('pool', xt[b][:], 0.0, [f'bx{b}'])
                        self.ld(xt[b][0:rows, :], I['x_sample'][:, :], (), [f'bx{b}'])
                    else:
                        self.ld(xt[b][:], I['x_prompt'][t * P:(t + 1) * P, :], (), [f'bx{b}'])
                else:
                    self.ld(xt[b][:], X['x2'][t * P:(t + 1) * P, :], ['x2'], [f'bx{b}'])
                self.memset('pool', mixT[b][:], 0.0, [f'mixT{b}'])
                if 'b' in BRANCHES and KSSM != 'setup' and not (KSSMT == 'p' and samp) and not (KSSMT == 's' and not samp):
                    if samp:
                        self.ssm_sample_init(ss, l, s_init, 's_init', s_h1, s_h2)
                        self.ssm_chunk(ss, sbufs, l, t, 4, cfg.NS, s_init, 's_init', X['gbT'][t], mixT[b], f'mixT{b}', True)
                    else:
                        self.ssm_chunk(ss, sbufs, l, t, P, 1, s_init, 's_init', X['gbT'][t], mixT[b], f'mixT{b}', True if t == NT - 1 else None)
                if samp:
                    for e_ in range(cfg.NS):
                        S4 = cfg.S
                        self.ld(gates_s[:, :], X['gb'][t][e_ * S4:(e_ + 1) * S4, :], ['Xgb'], ['gates_s'])
                        self.ld(gnt_s[:, :], X['gn'][t][e_ * S4:(e_ + 1) * S4, :], ['Xgn'], ['gnt_s'])
                        if 'c' in BRANCHES:
                            self.mem_sample(mem, l, e_, None, None, gm, gmk)
                            self.mem_attend(mem, t, S4, X['mqT'][t].rearrange("p (h q) -> p h q", h=4)[:, :, e_ * S4:(e_ + 1) * S4],
                                            gates_s[0:S4, 512:1024], 'gates_s', mixc, mem['mkT_s'], mem['mv_s'], 'mkv_s', mem['negM_s'])
                            pb = self.bkb[7]; pk = 'bk7'
                            for j in range(4):
                                self.tr(pb[:, j * P:j * P + S4], mixc[0:S4, j * P:(j + 1) * P], self.ident_b[0:S4, 0:S4], ['mixc', 'ident_b'], [pk])
                            self.cp('act', mixT[b][:, 8:12, e_ * S4:(e_ + 1) * S4], pb[:, 0:512].rearrange("p (k t) -> p k t", k=4)[:, :, 0:S4], [pk], [f'mixT{b}'])
                        if 'a' in BRANCHES and KNSA != 'p':
                            self.nsa_sample(ns, l, e_, gnt_s, gates_s, mixT[b], f'mixT{b}')
                if not samp:
                    self.ld(gbt[b][:], X['gb'][t], ['Xgb'], [f'gbt{b}'])
                    if 'a' in BRANCHES:
                        self.ld(gnt_p[:, :], X['gn'][t], ['Xgn'], ['gnt_p'])
                        self.nsa_tile(ns, l, t, gnt_p, gbt[b][:, 0:512], f'gbt{b}', mixT[b], f'mixT{b}')
                    if 'c' in BRANCHES:
                        self.mem_attend(mem, t, P, X['mqT'][t], gbt[b][:, 512:1024], f'gbt{b}', mixc, mem['mkT'], mem['mv'], 'mkv', mem['negM'])
                        pb = self.bkb[7]; pk = 'bk7'
                        for j in range(4):
                            self.tr(pb[:, j * P:(j + 1) * P], mixc[:, j * P:(j + 1) * P], self.ident_b[:], ['mixc', 'ident_b'], [pk])
                        self.cp('act', mixT[b][:, 8:12, :], pb[:, 0:512].rearrange("p (k t) -> p k t", k=4), [pk], [f'mixT{b}'])
                for c in range(2):
                    z = py[c]; zk = f'bk{5 + c}'
                    for k in range(12):
                        self.mm(z[:, :], mixT[b][:, k, :], wo[:, k, c * 512:(c + 1) * 512], k == 0, k == 11, [f'mixT{b}', f'w_o{k}'], [zk])
                    self.tt('dve', xt[b][:, c * 512:(c + 1) * 512], z[:, :], xt[b][:, c * 512:(c + 1) * 512], ALU.add, [zk, f'bx{b}'], [f'bx{b}'])
                self.ld(X['x1'][t * P:(t + 1) * P, :], xt[b][:], [f'bx{b}'], ['x1'], q='sp')


    def ssm_setup(self, st, l):
        cfg, S, I = self.cfg, self.S, self.I
        NS = cfg.NS
        nc = self.nc
        T_ = {}
        TAU = 129
        cos = self.sb(st, "s_cos", [P, NSG, TAU], BF16)
        sin = self.sb(st, "s_sin", [P, NSG, TAU], BF16)
        cc = self.sb(st, "s_cc", [P, 8, NSG], F32)
        sm = self.sb(st, "s_small", [P, 16, NSG], F32)
        WB = [self.sb(st, f"s_WB{i}", [P, 4, 8, P], BF16) for i in range(2)]
        WC = [self.sb(st, f"s_WC{i}", [P, 4, 8, P], BF16) for i in range(2)]
        dcol = self.sb(st, "s_dcol", [P, 4], F32)
        bcol = self.sb(st, "s_bcol", [P, 4], F32)
        pswap = self.sb(st, "s_pswap", [P, P], F32)
        wglu = self.sb(st, "s_wglu", [P, 4, 512], BF16)
        for k in range(4):
            self.ld(wglu[:, k, :], I['w_glu'][l, k * P:(k + 1) * P, :], (), [f's_wglu{k}'], q='pool')
        S.barrier()
        ARE, AIM, DT, MAG, TURN, ABR, ABI, FR, FI, F2, F4, CRL, CIL, T1, T2, T3 = [sm[:, i, :] for i in range(16)]
        with nc.allow_non_contiguous_dma(reason="small parameter loads"):
            for h in range(2):
                self.ld(sm[h * 64:(h + 1) * 64, 0, :], I['ssm_a_re'][l].rearrange("g n -> n g"), (), ['s_sm'])
                self.ld(sm[h * 64:(h + 1) * 64, 1, :], I['ssm_a_im'][l].rearrange("g n -> n g"), (), ['s_sm'])
            self.ld(sm[:, 2, :], I['ssm_log_dt'][l:l + 1, :].partition_broadcast(P), (), ['s_sm'])
            self.ld(dcol[:], I['ssm_d'][l].rearrange("(j p) -> p j", p=P), (), ['s_dcol'])
            self.ld(bcol[:], I['b_glu'][l].rearrange("(j p) -> p j", p=P), (), ['s_bcol'])
        K = ['s_sm']
        self.act(DT, DT, AF.Exp, K, K)
        self.tt('dve', T1, DT, ARE, ALU.mult, K, K)
        self.act(MAG, T1, AF.Exp, K, K)
        self.tt('dve', TURN, DT, AIM, ALU.mult, K, K)
        self.ts('dve', TURN, TURN, 1.0 / (2.0 * math.pi), None, ALU.mult, None, K, K)
        TSEL = (1, 3, 127, 128)
        with ExitStack() as s2:
            GC = 8
            ti = self.sb(s2, "s_ti", [P, GC, TAU], I32)
            arg = self.sb(s2, "s_arg", [P, GC, TAU], F32)
            tf = self.sb(s2, "s_tf", [P, GC, TAU], F32)
            tau_i = self.sb(s2, "s_taui", [P, TAU], I32)
            tau = self.sb(s2, "s_tau", [P, TAU], F32)
            S.op('pool', lambda e: e.iota(tau_i[:], pattern=[[1, TAU]], base=0, channel_multiplier=0), (), ['s_taui'])
            self.cp('dve', tau[:], tau_i[:], ['s_taui'], ['s_tau'])

            def wrap(x, xk):
                self.cp('dve', ti[:], x, [xk], ['s_ti'])
                self.cp('dve', tf[:], ti[:], ['s_ti'], ['s_tf'])
                self.tt('dve', x, x, tf[:], ALU.subtract, [xk, 's_tf'], [xk])
                self.ts('dve', tf[:], x, 0.5, None, ALU.is_gt, None, [xk], ['s_tf'])
                self.tt('dve', x, x, tf[:], ALU.subtract, [xk, 's_tf'], [xk])
                self.ts('dve', tf[:], x, -0.5, None, ALU.is_lt, None, [xk], ['s_tf'])
                self.tt('dve', x, x, tf[:], ALU.add, [xk, 's_tf'], [xk])
            for g0 in range(0, NSG, GC):
                self.tt('dve', arg[:], TURN[:, g0:g0 + GC].unsqueeze(2).to_broadcast([P, GC, TAU]), tau[:].unsqueeze(1).to_broadcast([P, GC, TAU]),
                        ALU.mult, K + ['s_tau'], ['s_arg'])
                wrap(arg[:], 's_arg')
                self.act(tf[:], arg[:], AF.Sin, ['s_arg'], ['s_tf'], scale=2.0 * math.pi)
                self.cp('dve', sin[:, g0:g0 + GC, :], tf[:], ['s_tf'], ['s_sin'])
                for i_, tt_ in enumerate(TSEL):
                    self.cp('dve', cc[:, 4 + i_, g0:g0 + GC], tf[:, :, tt_], ['s_tf'], ['s_cc'])
                self.ts('dve', arg[:], arg[:], 0.25, None, ALU.add, None, ['s_arg'], ['s_arg'])
                wrap(arg[:], 's_arg')
                self.act(tf[:], arg[:], AF.Sin, ['s_arg'], ['s_tf'], scale=2.0 * math.pi)
                self.cp('dve', cos[:, g0:g0 + GC, :], tf[:], ['s_tf'], ['s_cos'])
                for i_, tt_ in enumerate(TSEL):
                    self.cp('dve', cc[:, i_, g0:g0 + GC], tf[:, :, tt_], ['s_tf'], ['s_cc'])
            S.barrier()
        CS = ['s_cos', 's_sin']
        self.tt('dve', ABR, MAG, cc[:, 0, :], ALU.mult, K + ['s_cc'], K)
        self.tt('dve', ABI, MAG, cc[:, 4, :], ALU.mult, K + ['s_cc'], K)
        self.tt('dve', T1, ARE, ARE, ALU.mult, K, K)
        self.tt('dve', T2, AIM, AIM, ALU.mult, K, K)
        self.tt('dve', T1, T1, T2, ALU.add, K, K)
        S.op('dve', lambda e: e.reciprocal(out=T1, in_=T1), K, K)
        self.ts('dve', T2, ABR, -1.0, None, ALU.add, None, K, K)
        self.tt('dve', FR, T2, ARE, ALU.mult, K, K)
        self.tt('dve', T3, ABI, AIM, ALU.mult, K, K)
        self.tt('dve', FR, FR, T3, ALU.add, K, K)
        self.tt('dve', FR, FR, T1, ALU.mult, K, K)
        self.tt('dve', FI, ABI, ARE, ALU.mult, K, K)
        self.tt('dve', T3, T2, AIM, ALU.mult, K, K)
        self.tt('dve', FI, FI, T3, ALU.subtract, K, K)
        self.tt('dve', FI, FI, T1, ALU.mult, K, K)
        self.ts('dve', sm[0:64, 9, :], sm[0:64, 8, :], -1.0, None, ALU.mult, None, K, K)
        self.cp('dve', sm[64:128, 9, :], sm[64:128, 8, :], K, K)
        self.cp('dve', sm[0:64, 10, :], sm[0:64, 7, :], K, K)
        self.ts('dve', sm[64:128, 10, :], sm[64:128, 7, :], -1.0, None, ALU.mult, None, K, K)
        self.ts('dve', pswap[:], self.iota_jp[:], 64, None, ALU.is_equal, None, ['iota_tmp'], ['s_pswap'])
        with ExitStack() as s2:
            tmpf = self.sb(s2, "s_tmpf", [P, P], F32)
            self.ts('dve', tmpf[:], self.iota_jp[:], -64, None, ALU.is_equal, None, ['iota_tmp'], ['s_tmpf'])
            self.tt('dve', pswap[:], pswap[:], tmpf[:], ALU.subtract, ['s_tmpf', 's_pswap'], ['s_pswap'])
            bn1 = self.sb(s2, "s_bn1", [P, NSG, 16], F32)
            bn2 = self.sb(s2, "s_bn2", [P, NSG, 16], F32)
            bb = self.sb(s2, "s_bb", [P, NSG, 16], F32)
            bt = self.sb(s2, "s_bt", [P, NSG, 16], F32)
            with nc.allow_non_contiguous_dma(reason="small parameter loads"):
                self.ld(bn1[0:64], I['ssm_b_re'][l].rearrange("g n c -> n g c"), (), ['s_bn1'])
                self.ld(bn1[64:128], I['ssm_b_im'][l].rearrange("g n c -> n g c"), (), ['s_bn1'])
                self.ld(bn2[0:64], I['ssm_b_im'][l].rearrange("g n c -> n g c"), (), ['s_bn2'])
                self.ld(bn2[64:128], I['ssm_b_re'][l].rearrange("g n c -> n g c"), (), ['s_bn2'])
            rm = self.sb(s2, "s_rm", [P, 8], F32)
            rmi = self.sb(s2, "s_rmi", [P, 8], I32)
            S.op('pool', lambda e: e.iota(rmi[:], pattern=[[-16, 8]], base=0, channel_multiplier=1), (), ['s_rmi'])
            rm2 = self.sb(s2, "s_rm2", [P, 8], F32)
            self.ts('dve', rm[:], rmi[:], 0, None, ALU.is_ge, None, ['s_rmi'], ['s_rm'])
            self.ts('dve', rm2[:], rmi[:], 16, None, ALU.is_lt, None, ['s_rmi'], ['s_rm2'])
            self.tt('dve', rm[:], rm[:], rm2[:], ALU.mult, ['s_rm', 's_rm2'], ['s_rm'])
            bc = lambda a: a.unsqueeze(2).to_broadcast([P, NSG, 16])
            for tbl, (Fa, Na, Fb, Nb) in enumerate(((FR, bn1, F2, bn2), (F4, bn2, FI, bn1))):
                self.tt('dve', bb[:], bc(Fa), Na[:], ALU.mult, K + ['s_bn1', 's_bn2'], ['s_bb'])
                self.tt('dve', bt[:], bc(Fb), Nb[:], ALU.mult, K + ['s_bn1', 's_bn2'], ['s_bt'])
                self.tt('dve', bb[:], bb[:], bt[:], ALU.add, ['s_bb', 's_bt'], ['s_bb'])
                for j in range(4):
                    pT = self.bk[j % 2]
                    self.tr(pT[:, 0:P], bb[:, 8 * j:8 * j + 8, :].rearrange("p g c -> p (g c)"), self.ident_f[:], ['s_bb', 'ident_f'], [f'bk{j % 2}'])
                    self.tt('dve', WB[tbl][:, j, :, :], pT[:, 0:P].unsqueeze(1).to_broadcast([P, 8, P]),
                            rm[:].unsqueeze(2).to_broadcast([P, 8, P]), ALU.mult, [f'bk{j % 2}', 's_rm'], [f's_WB{tbl}'])
            cn = self.sb(s2, "s_cn", [P, 4, P], F32)
            cm_i = self.sb(s2, "s_cmi", [P, 8, P], I32)
            cm = self.sb(s2, "s_cm", [P, 8, P], F32)
            cm2 = self.sb(s2, "s_cm2", [P, 8, P], F32)
            S.op('pool', lambda e: e.iota(cm_i[:], pattern=[[-16, 8], [1, P]], base=0, channel_multiplier=0), (), ['s_cmi'])
            self.ts('dve', cm[:], cm_i[:], 0, None, ALU.is_ge, None, ['s_cmi'], ['s_cm'])
            self.ts('dve', cm2[:], cm_i[:], 16, None, ALU.is_lt, None, ['s_cmi'], ['s_cm2'])
            self.tt('dve', cm[:], cm[:], cm2[:], ALU.mult, ['s_cm', 's_cm2'], ['s_cm'])
            cre = I['ssm_c_re'][l].rearrange("(j gg) c n -> (gg c) j n", j=4)
            cim = I['ssm_c_im'][l].rearrange("(j gg) c n -> (gg c) j n", j=4)
            for tbl in range(2):
                with nc.allow_non_contiguous_dma(reason="small parameter loads"):
                    if tbl == 0:
                        self.ld(cn[:, :, 0:64], cre, (), ['s_cn'])
                        self.ld(cn[:, :, 64:128], cim, (), ['s_cn'])
                        self.ts('dve', cn[:, :, 64:128], cn[:, :, 64:128], -1.0, None, ALU.mult, None, ['s_cn'], ['s_cn'])
                    else:
                        self.ld(cn[:, :, 0:64], cim, (), ['s_cn'])
                        self.ld(cn[:, :, 64:128], cre, (), ['s_cn'])
                        self.ts('dve', cn[:], cn[:], -1.0, None, ALU.mult, None, ['s_cn'], ['s_cn'])
                for j in range(4):
                    pT = self.bk[j % 2]
                    self.tr(pT[:, 0:P], cn[:, j, :], self.ident_f[:], ['s_cn', 'ident_f'], [f'bk{j % 2}'])
                    self.tt('dve', WC[tbl][:, j, :, :], pT[:, 0:P].unsqueeze(1).to_broadcast([P, 8, P]), cm[:], ALU.mult,
                            [f'bk{j % 2}', 's_cm'], [f's_WC{tbl}'])
            S.barrier()
        return dict(cos=cos, sin=sin, cc=cc, sm=sm, WB=WB, WC=WC, dcol=dcol, bcol=bcol, pswap=pswap, wglu=wglu)

    def ssm_chunk(self, ss, bufs, l, t, L, NSEG, init, initk, gbT_src, mixT_dst, mixk, final):
        cfg, S, I, O, X = self.cfg, self.S, self.I, self.O, self.X
        W = NSEG * L
        uT, gin, gout, G12, t1, t2, gend, cst, yt, yg, sg, gbTt = bufs
        sm = ss['sm']
        ABR, ABI, CRL, CIL = sm[:, 5, :], sm[:, 6, :], sm[:, 11, :], sm[:, 12, :]
        cos, sin = ss['cos'], ss['sin']
        MAG = sm[:, 3, :]
        self.ld(uT[:, :, 0:W], X['uT'][t].rearrange("p (j q) -> p j q", j=4)[:, :, 0:W], ['XuT'], ['s_uT'])
        self.ld(gbTt[:, :, 0:W], gbT_src.rearrange("p (j q) -> p j q", j=4)[:, :, 0:W], ['XgbT'], ['s_gbT'])

        def tabv(tab, g0, ng, lo, n):
            v = tab[:, g0:g0 + ng, lo:lo + n]
            if NSEG == 1:
                return v
            return v.unsqueeze(2).to_broadcast([P, ng, NSEG, n])

        def dv(buf, g0, ng):
            v = buf[:, (g0 % 8) * W:(g0 % 8 + ng) * W]
            if NSEG == 1:
                return v.rearrange("p (g q) -> p g q", g=ng)
            return v.rearrange("p (g e q) -> p g e q", g=ng, e=NSEG)

        for half in range(4):
            for blk in range(2):
                g0 = half * 8 + blk * 4
                A = self.bk[(blk % 2) * 2]; Ak = f'bk{(blk % 2) * 2}'
                B = self.bk[(blk % 2) * 2 + 1]; Bk = f'bk{(blk % 2) * 2 + 1}'
                for gi in range(4):
                    g = g0 + gi
                    j, gg = g // 8, g % 8
                    self.mm(A[:, gi * W:(gi + 1) * W], ss['WB'][0][:, j, gg, :], uT[:, j, 0:W], True, True, ['s_uT', 's_WB0'], [Ak])
                    self.mm(B[:, gi * W:(gi + 1) * W], ss['WB'][1][:, j, gg, :], uT[:, j, 0:W], True, True, ['s_uT', 's_WB1'], [Bk])
                if KSSM == 'c05':
                    continue
                sh = (lambda v: v.rearrange("p (g q) -> p g q", g=4)) if NSEG == 1 else (lambda v: v.rearrange("p (g e q) -> p g e q", g=4, e=NSEG))
                self.tt('dve', sh(t1[:, 0:4 * W]), sh(A[:, 0:4 * W]), tabv(cos, g0, 4, 0, L), ALU.mult, [Ak, 's_cos'], ['s_t1'])
                self.tt('dve', sh(t2[:, 0:4 * W]), sh(B[:, 0:4 * W]), tabv(sin, g0, 4, 0, L), ALU.mult, [Bk, 's_sin'], ['s_t2'])
                self.tt('pool', gin[:, (g0 % 8) * W:(g0 % 8 + 4) * W], t1[:, 0:4 * W], t2[:, 0:4 * W], ALU.add, ['s_t1', 's_t2'], ['s_gin'])
            if KSSM in ('c1', 'c05'):
                continue
            h0 = half * 8
            for gl in range(8):
                g = h0 + gl
                for e_ in range(NSEG):
                    c0 = (gl * NSEG + e_) * L
                    S.op('dve', lambda e, c0=c0, g=g, e_=e_: e.tensor_tensor_scan(
                        out=gout[:, c0:c0 + L], data0=MAG[:, g:g + 1].to_broadcast([P, L]), data1=gin[:, c0:c0 + L],
                        initial=init[:, g * NSEG + e_:g * NSEG + e_ + 1], op0=ALU.mult, op1=ALU.add), ['s_gin', 's_sm', initk], ['s_gout'])
            if KSSM == 'c2':
                continue
            self.cp('dve', gend[:, h0 * NSEG:(h0 + 8) * NSEG], gout[:, 0:8 * W].rearrange("p (s q) -> p s q", q=L)[:, :, L - 1], ['s_gout'], ['s_gend'])
            if KSSM == 'c25':
                continue
            self.tt('dve', dv(G12, h0, 8), dv(gout, h0, 8), tabv(cos, h0, 8, 0, L), ALU.mult, ['s_gout', 's_cos'], ['s_G1'])
            self.tt('pool', dv(G12[:, 8 * W:16 * W], h0, 8), dv(gout, h0, 8), tabv(sin, h0, 8, 0, L), ALU.mult, ['s_gout', 's_sin'], ['s_G2'])
            if KSSM == 'c3':
                continue
            for jj in range(1):
                j = half
                pc = self.bk[4 + half % 2]; pck = f'bk{4 + half % 2}'
                for gg in range(8):
                    gl = gg
                    self.mm(pc[:, 0:W], ss['WC'][0][:, j, gg, :], G12[:, gl * W:(gl + 1) * W], gg == 0, False, ['s_G1', 's_WC0'], [pck])
                    self.mm(pc[:, 0:W], ss['WC'][1][:, j, gg, :], G12[:, 8 * W + gl * W:8 * W + (gl + 1) * W], False, gg == 7, ['s_G2', 's_WC1'], [pck])
                self.stt(yt[:, 0:W], uT[:, j, 0:W], ss['dcol'][:, j:j + 1], pc[:, 0:W], ALU.mult, ALU.add, ['s_uT', 's_dcol', pck], ['s_yt'])
                self.tt('dve', sg[:, 0:W], yt[:, 0:W], yt[:, 0:W], ALU.mult, ['s_yt'], ['s_sg'])
                self.ts('dve', sg[:, 0:W], sg[:, 0:W], 0.044715, 1.0, ALU.mult, ALU.add, ['s_sg'], ['s_sg'])
                self.tt('dve', sg[:, 0:W], sg[:, 0:W], yt[:, 0:W], ALU.mult, ['s_sg', 's_yt'], ['s_sg'])
                self.act(sg[:, 0:W], sg[:, 0:W], AF.Sigmoid, ['s_sg'], ['s_sg'], scale=2.0 * math.sqrt(2.0 / math.pi))
                self.tt('dve', yg[:, j, 0:W], sg[:, 0:W], yt[:, 0:W], ALU.mult, ['s_sg', 's_yt'], ['s_yg'])
        if KSSM in ('c05', 'c1', 'c2', 'c25', 'c3', 'c4'):
            return
        for jo in range(4):
            pg = self.bk[6 + jo % 2]; pgk = f'bk{6 + jo % 2}'
            for k in range(4):
                self.mm(pg[:, 0:W], ss['wglu'][:, k, jo * P:(jo + 1) * P], yg[:, k, 0:W], k == 0, k == 3, ['s_yg', f's_wglu{k}'], [pgk])
            self.act(sg[:, 0:W], pg[:, 0:W], AF.Sigmoid, [pgk, 's_bcol'], ['s_sg'], bias=ss['bcol'][:, jo:jo + 1])
            self.tt('dve', sg[:, 0:W], sg[:, 0:W], yg[:, jo, 0:W], ALU.mult, ['s_sg', 's_yg'], ['s_sg'])
            self.tt('dve', mixT_dst[:, 4 + jo, 0:W], sg[:, 0:W], gbTt[:, jo, 0:W], ALU.mult, ['s_sg', 's_gbT'], [mixk])
        if KSSM == 'c5':
            return
        psw = self.bk[0]
        NC = NSG * NSEG
        self.mm(psw[:, 0:NC], ss['pswap'][:], gend[:, 0:NC], True, True, ['s_gend', 's_pswap'], ['bk0'])
        bcs = (lambda a: a) if NSEG == 1 else (lambda a: a.unsqueeze(2).to_broadcast([P, NSG, NSEG]))
        shp = (lambda v: v) if NSEG == 1 else (lambda v: v.rearrange("p (g e) -> p g e", e=NSEG))
        if final is None:
            self.tt('dve', shp(cst[:, 0:NC]), shp(gend[:, 0:NC]), bcs(ss['cc'][:, 3, :]), ALU.mult, ['s_gend', 's_cc'], ['s_cst'])
            self.tt('dve', shp(init[:, 0:NC]), shp(psw[:, 0:NC]), bcs(ss['cc'][:, 7, :]), ALU.mult, ['bk0', 's_cc'], [initk])
            self.tt('dve', init[:, 0:NC], init[:, 0:NC], cst[:, 0:NC], ALU.add, [initk, 's_cst'], [initk])
        else:
            self.tt('dve', shp(cst[:, 0:NC]), shp(gend[:, 0:NC]), bcs(ss['cc'][:, 2 if L == P else 1, :]), ALU.mult, ['s_gend', 's_cc'], ['s_cst'])
            self.tt('dve', shp(gend[:, 0:NC]), shp(psw[:, 0:NC]), bcs(ss['cc'][:, 6 if L == P else 5, :]), ALU.mult, ['bk0', 's_cc'], ['s_gend'])
            self.tt('dve', cst[:, 0:NC], cst[:, 0:NC], gend[:, 0:NC], ALU.add, ['s_gend', 's_cst'], ['s_cst'])
            t1_, t2_ = self._st_tiles
            if NSEG == 1:
                self.tr(self.bk[6][0:NSG, 0:P], cst[:, 0:NSG], self.ident_f[:], ['s_cst', 'ident_f'], ['bk6'])
                self.cp('act', t1_[0:NSG, :], self.bk[6][0:NSG, 0:P], ['bk6'], ['s_stT1'])
                self.ld(O['hr_p'][l], t1_[0:NSG, 0:64], ['s_stT1'], ['Ohr_p'])
                self.ld(O['hi_p'][l], t1_[0:NSG, 64:128], ['s_stT1'], ['Ohi_p'])
            else:
                self.cp('dve', gend[:, 0:NC].rearrange("p (e g) -> p e g", e=NSEG), cst[:, 0:NC].rearrange("p (g e) -> p e g", e=NSEG), ['s_cst'], ['s_gend'])
                self.tr(self.bk[6][0:NC, 0:P], gend[:, 0:NC], self.ident_f[:], ['s_gend', 'ident_f'], ['bk6'])
                self.cp('act', t1_[0:NC, :], self.bk[6][0:NC, 0:P], ['bk6'], ['s_stT1'])
                self.ld(O['hr_s'][l].rearrange("e g n -> (e g) n"), t1_[0:NC, 0:64], ['s_stT1'], ['Ohr_s'])
                self.ld(O['hi_s'][l].rearrange("e g n -> (e g) n"), t1_[0:NC, 64:128], ['s_stT1'], ['Ohi_s'])

    def ssm_bufs(self, st):
        mk = lambda n, shp, dt: self.sb(st, n, shp, dt)
        return (mk("s_uT", [P, 4, P], BF16), mk("s_gin", [P, 8 * P], F32), mk("s_gout", [P, 8 * P], F32), mk("s_G12", [P, 16 * P], BF16),
                mk("s_t1", [P, 512], F32), mk("s_t2", [P, 512], F32), mk("s_gend", [P, NSG * self.cfg.NS], F32),
                mk("s_cst", [P, NSG * self.cfg.NS], F32), mk("s_yt", [P, P], F32), mk("s_yg", [P, 4, P], BF16), mk("s_sg", [P, P], F32),
                mk("s_gbT", [P, 4, P], BF16))

    def ssm_sample_init(self, ss, l, init, initk, h1, h2):
        cfg, S, I = self.cfg, self.S, self.I
        NS = cfg.NS
        sm = ss['sm']
        ABR, ABI = ss['cc'][:, 0, :], ss['cc'][:, 4, :]
        NC = NSG * NS
        t1_, t2_ = self._st_tiles
        re_v = I['ssm_re'][l].rearrange("e g n -> (e g) n")
        im_v = I['ssm_im'][l].rearrange("e g n -> (e g) n")
        self.ld(t1_[0:NC, 0:64], re_v, (), ['s_stT1'])
        self.ld(t1_[0:NC, 64:128], im_v, (), ['s_stT1'])
        self.ld(t2_[0:NC, 0:64], im_v, (), ['s_stT2'])
        self.ld(t2_[0:NC, 64:128], re_v, (), ['s_stT2'])
        for (src, sk, dst, dk, bi) in ((t1_, 's_stT1', h1, 's_h1', 6), (t2_, 's_stT2', h2, 's_h2', 7)):
            self.tr(self.bk[bi][:, 0:NC], src[0:NC, :], self.ident_f[0:NC, 0:NC], [sk, 'ident_f'], [f'bk{bi}'])
            self.cp('act', dst[:, 0:NC].rearrange("p (g e) -> p g e", e=NS), self.bk[bi][:, 0:NC].rearrange("p (e g) -> p g e", e=NS), [f'bk{bi}'], [dk])
        self.ts('dve', h2[0:64, :], h2[0:64, :], -1.0, None, ALU.mult, None, ['s_h2'], ['s_h2'])
        b3 = lambda a: a.unsqueeze(2).to_broadcast([P, NSG, NS])
        v3 = lambda a: a.rearrange("p (g e) -> p g e", e=NS)
        self.tt('dve', v3(h1[:, :]), v3(h1[:, :]), b3(ABR), ALU.mult, ['s_h1', 's_cc'], ['s_h1'])
        self.tt('dve', v3(h2[:, :]), v3(h2[:, :]), b3(ABI), ALU.mult, ['s_h2', 's_cc'], ['s_h2'])
        self.tt('dve', init[:, 0:NSG * NS], h1[:, :], h2[:, :], ALU.add, ['s_h1', 's_h2'], [initk])


    def make_aug_rows(self, st, dsts, N, kind, row0=64):
        S = self.S
        CH = min(N, 1024)
        ii = self.sb(st, f"aug_i_{kind}", [1, 2, CH], I32)
        af = self.sb(st, f"aug_f_{kind}", [1, 3, CH], F32)
        ab = self.sb(st, f"aug_b_{kind}", [1, 3, CH], BF16)
        k = f'aug_{kind}'
        sh, msk = (7, 127) if kind == 'tok' else (3, 7)
        for c0 in range(0, N, CH):
            S.op('pool', lambda e: e.iota(ii[:], pattern=[[0, 2], [1, CH]], base=c0, channel_multiplier=0), (), [k + 'i'])
            S.op('dve', lambda e: e.tensor_scalar(out=ii[:, 0, :], in0=ii[:, 0, :], scalar1=sh, scalar2=None, op0=ALU.logical_shift_right), [k + 'i'], [k + 'i'])
            S.op('dve', lambda e: e.tensor_scalar(out=ii[:, 1, :], in0=ii[:, 1, :], scalar1=msk, scalar2=None, op0=ALU.bitwise_and), [k + 'i'], [k + 'i'])
            self.cp('dve', af[:, 1:3, :], ii[:], [k + 'i'], [k + 'f'])
            if kind == 'tok':
                self.ts('dve', af[:, 2, :], af[:, 2, :], -64.0, None, ALU.add, None, [k + 'f'], [k + 'f'])
            else:
                self.ts('dve', af[:, 2, :], af[:, 2, :], 16.0, 31.0 - 64.0, ALU.mult, ALU.add, [k + 'f'], [k + 'f'])
            self.memset('dve', af[:, 0, :], 1.0, [k + 'f'])
            self.cp('dve', ab[:], af[:], [k + 'f'], [k + 'b'])
            for (dt_, dk) in dsts:
                for r in range(3):
                    self.ld(dt_[row0 + r:row0 + r + 1, c0:c0 + CH], ab[0:1, r, :], [k + 'b'], [dk])

    def nsa_setup(self, st, l, gm, gmk):
        cfg, S, I, O, X = self.cfg, self.S, self.I, self.O, self.X
        nc = self.nc
        T, NT = cfg.T, cfg.NT
        NCT = T // 16 - 1
        NCTL = (NCT + P - 1) // P
        N = {}
        CHK = 8
        KS = [[self.sb(st, f"n_KSc{c}{g}", [67, CHK * P], BF16) for g in range(NG)] for c in range(2)]
        VS = [self.sb(st, f"n_VSc{c}", [P, CHK, 130], BF16) for c in range(2)]
        CK = [self.sb(st, f"n_CK{g}", [67, NCTL * P], BF16) for g in range(NG)]
        CVP = [self.sb(st, f"n_CVP{g}", [P, NCTL, 65], BF16) for g in range(NG)]
        POOLM = self.sb(st, "n_POOLM", [P, NCTL, P], BF16)
        OTS = self.sb(st, "n_OTS", [P, 4, 512], F32)
        QB = [self.sb(st, f"n_QB{g}", [67, 512], BF16) for g in range(NG)]
        QA = [self.sb(st, f"n_QA{g}", [67, 512], BF16) for g in range(NG)]
        ESEL = self.sb(st, "n_ESEL", [P, 32, P], BF16)
        negM = self.sb(st, "n_negM", [P, 2], F32)
        with ExitStack() as s2:
            if l == 0:
                self.make_aug_rows(s2, [(X['kaug'], 'Xkaug')], T, 'tok', row0=0)
            self.make_aug_rows(s2, [(CK[g], f'n_CK{g}') for g in range(NG)], NCTL * P, 'cmp')
            qs = self.sb(s2, "n_qs", [1, 3, 512], F32)
            qsb = self.sb(s2, "n_qsb", [1, 3, 512], BF16)
            sl = slopes()
            for g in range(NG):
                for hl in range(HPG):
                    sv = sl[4 * g + hl]
                    self.memset('dve', qs[:, 0:2, hl * P:(hl + 1) * P], 1024.0 * sv, ['n_qs'])
                    self.memset('dve', qs[:, 2, hl * P:(hl + 1) * P], 8.0 * sv, ['n_qs'])
                self.cp('dve', qsb[:], qs[:], ['n_qs'], ['n_qsb'])
                for r in range(3):
                    self.ld(QB[g][64 + r:65 + r, :], qsb[0:1, r, :], ['n_qsb'], [f'n_QB{g}'])
                    self.ld(QA[g][64 + r:65 + r, :], qsb[0:1, r, :], ['n_qsb'], [f'n_QA{g}'])
                S.barrier()
            ei = self.sb(s2, "n_ei", [P, 32, 2], I32)
            ef = self.sb(s2, "n_ef", [P, 32, 2], F32)
            pi_ = self.sb(s2, "n_pi", [P, 1], I32)
            S.op('pool', lambda e: e.iota(ei[:], pattern=[[2, 32], [1, 2]], base=0, channel_multiplier=0), (), ['n_ei'])
            S.op('pool', lambda e: e.iota(pi_[:], pattern=[[0, 1]], base=0, channel_multiplier=1), (), ['n_pi'])
            S.op('dve', lambda e: e.tensor_scalar(out=pi_[:], in0=pi_[:], scalar1=63, scalar2=None, op0=ALU.bitwise_and), ['n_pi'], ['n_pi'])
            pf = self.sb(s2, "n_pf", [P, 1], F32)
            self.cp('dve', pf[:], pi_[:], ['n_pi'], ['n_pf'])
            self.cp('dve', ef[:], ei[:], ['n_ei'], ['n_ef'])
            self.ts('dve', ef[:], ef[:], pf[:, 0:1], None, ALU.is_equal, None, ['n_ef', 'n_pf'], ['n_ef'])
            self.cp('dve', ESEL[:].rearrange("p r (j k) -> p r j k", j=2), ef[:].unsqueeze(3).to_broadcast([P, 32, 2, 64]), ['n_ef'], ['n_ESEL'])
            w1s = self.sb(s2, "n_w1s", [P, 32, 64], F32)
            W1B = self.sb(s2, "n_W1B", [P, 32, P], BF16)
            w2s = self.sb(s2, "n_w2s", [P, 64], F32)
            W2B = self.sb(s2, "n_W2B", [P, P], BF16)
            W2P = [self.sb(s2, f"n_W2P{g}", [P, 64], BF16) for g in range(NG)]
            pes = self.sb(s2, "n_pes", [P, 32], F32)
            peb = self.sb(s2, "n_peb", [P, 32], BF16)
            bias = self.sb(s2, "n_bias", [P, 1], F32)
            cT = self.sb(s2, "n_cT", [P, T], BF16)
            H1 = self.sb(s2, "n_H1", [P, 512], BF16)
            hx = self.sb(s2, "n_hx", [P, 512], F32)
            hy = self.sb(s2, "n_hy", [P, 512], F32)
            sqk = self.sb(s2, "n_sqk", [64, NCTL * P], BF16)
            ckn = self.sb(s2, "n_ckn", [1, 4], F32)
            ones_b = self.sb(s2, "n_ones_b", [P, 1], BF16)
            self.memset('dve', ones_b[:], 1.0, ['n_ones_b'])
            self.memset('dve', ckn[:], 0.0, ['n_ckn'])
            for g in range(NG):
                self.memset('pool', CVP[g][:], 0.0, [f'n_CVP{g}'])
                self.memset('pool', CK[g][0:64, :], 0.0, [f'n_CK{g}'])
            for kv in range(2):
                src = 'cTk' if kv == 0 else 'cTv'
                self.ld(cT[:].rearrange("p (t k) -> p t k", k=P), X[src][0:NT].rearrange("t p k -> p t k"), ['X' + src], ['n_cT'])
                self.memset('pool', W1B[:], 0.0, ['n_W1B'])
                self.memset('pool', W2B[:], 0.0, ['n_W2B'])
                for g in range(NG):
                    self.memset('pool', W2P[g][:], 0.0, [f'n_W2P{g}'])
                with nc.allow_non_contiguous_dma(reason="small parameter loads"):
                    for g in range(NG):
                        self.ld(w1s[64 * g:64 * g + 64], I['cmp_w1'][l, kv].rearrange("l d e -> d l e"), (), ['n_w1s'])
                        self.ld(w2s[64 * g:64 * g + 64], I['cmp_w2'][l, kv], (), ['n_w2s'])
                        self.ld(pes[64 * g:64 * g + 64], I['cmp_pe'][l, kv].rearrange("l d -> d l"), (), ['n_pes'])
                for g in range(NG):
                    self.cp('dve', W1B[64 * g:64 * g + 64, :, 64 * g:64 * g + 64], w1s[64 * g:64 * g + 64], ['n_w1s'], ['n_W1B'])
                    self.cp('dve', W2B[64 * g:64 * g + 64, 64 * g:64 * g + 64], w2s[64 * g:64 * g + 64], ['n_w2s'], ['n_W2B'])
                    self.cp('dve', W2P[g][64 * g:64 * g + 64, :], w2s[64 * g:64 * g + 64], ['n_w2s'], [f'n_W2P{g}'])
                self.cp('dve', peb[:], pes[:], ['n_pes'], ['n_peb'])
                pb_ = self.bk[7]
                for l_ in range(32):
                    self.mm(pb_[:, 0:1], W1B[:, l_, :], peb[:, l_:l_ + 1], l_ == 0, l_ == 31, ['n_W1B', 'n_peb'], ['bk7'])
                self.cp('act', bias[:], pb_[:, 0:1], ['bk7'], ['n_bias'])
                for n0 in range(0, NCT, 512):
                    nn = min(512, NCT - n0)
                    acc = self.bk[0]
                    for l_ in range(32):
                        off = 16 * n0 + l_ + (0 if l_ < 16 else 0)
                        rhs = cT[:, off:off + 16 * (nn - 1) + 1:16]
                        self.mm(acc[:, 0:nn], W1B[:, l_, :], rhs, l_ == 0, l_ == 31, ['n_W1B', 'n_cT'], ['bk0'])
                    self.ts('dve', hx[:, 0:nn], acc[:, 0:nn], bias[:, 0:1], None, ALU.add, None, ['bk0', 'n_bias'], ['n_hx'])
                    self.tt('dve', hy[:, 0:nn], hx[:, 0:nn], hx[:, 0:nn], ALU.mult, ['n_hx'], ['n_hy'])
                    self.ts('dve', hy[:, 0:nn], hy[:, 0:nn], 0.044715, 1.0, ALU.mult, ALU.add, ['n_hy'], ['n_hy'])
                    self.tt('dve', hy[:, 0:nn], hy[:, 0:nn], hx[:, 0:nn], ALU.mult, ['n_hy', 'n_hx'], ['n_hy'])
                    self.act(hy[:, 0:nn], hy[:, 0:nn], AF.Sigmoid, ['n_hy'], ['n_hy'], scale=2.0 * math.sqrt(2.0 / math.pi))
                    self.tt('dve', H1[:, 0:nn], hy[:, 0:nn], hx[:, 0:nn], ALU.mult, ['n_hy', 'n_hx'], ['n_H1'])
                    if kv == 0:
                        for g in range(NG):
                            po = self.bk[1 + g]
                            self.mm(po[0:64, 0:nn], W2P[g][:, :], H1[:, 0:nn], True, True, ['n_H1', f'n_W2P{g}'], [f'bk{1 + g}'])
                            self.cp('act', CK[g][0:64, n0:n0 + nn], po[0:64, 0:nn], [f'bk{1 + g}'], [f'n_CK{g}'])
                            self.tt('dve', sqk[:, 0:nn], CK[g][0:64, n0:n0 + nn], CK[g][0:64, n0:n0 + nn], ALU.mult, [f'n_CK{g}'], ['n_sqk'])
                            pn = self.bk[3]
                            self.mm(pn[0:1, 0:nn], ones_b[0:64, 0:1], sqk[:, 0:nn], True, True, ['n_sqk', 'n_ones_b'], ['bk3'])
                            S.op('dve', lambda e: e.reduce_max(out=ckn[:, 1:2], in_=pn[0:1, 0:nn], axis=AX.X), ['bk3'], ['n_ckn'])
                            self.tt('dve', ckn[:, 0:1], ckn[:, 0:1], ckn[:, 1:2], ALU.max, ['n_ckn'], ['n_ckn'])
                    else:
                        for c0 in range(0, nn, P):
                            cn_ = min(P, nn - c0)
                            c = (n0 + c0) // P
                            po = self.bk[1]
                            self.mm(po[0:cn_, 0:P], H1[:, c0:c0 + cn_], W2B[:, :], True, True, ['n_H1', 'n_W2B'], ['bk1'])
                            for g in range(NG):
                                self.cp('act', CVP[g][0:cn_, c, 0:64], po[0:cn_, 64 * g:64 * g + 64], ['bk1'], [f'n_CVP{g}'])
                if KNSA != 'p':
                    self.sample_compress(s2, l, kv, W1B, W2B, W2P, bias, cT, H1, hx, hy)
                S.barrier()
            pi2 = self.sb(s2, "n_pi2", [P, NCTL, P], I32)
            pf2 = self.sb(s2, "n_pf2", [P, NCTL, P], F32)
            pf3 = self.sb(s2, "n_pf3", [P, NCTL, P], F32)
            S.op('pool', lambda e: e.iota(pi2[:], pattern=[[-128, NCTL], [4, P]], base=0, channel_multiplier=-1), (), ['n_pi2'])
            self.ts('dve', pf2[:], pi2[:], 0, None, ALU.is_le, None, ['n_pi2'], ['n_pf2'])
            self.ts('dve', pf3[:], pi2[:], -3, None, ALU.is_ge, None, ['n_pi2'], ['n_pf3'])
            self.tt('dve', POOLM[:], pf2[:], pf3[:], ALU.mult, ['n_pf2', 'n_pf3'], ['n_POOLM'])
            for g in range(NG):
                self.memset('dve', CVP[g][:, :, 64:65], 1.0, [f'n_CVP{g}'])
            kb_ = self.sb(s2, "n_kb", [P, 4], F32)
            pk_ = self.bk[4]
            self.mm(pk_[:, 0:1], self.ones_f[0:1, :], ckn[0:1, 0:1], True, True, ['n_ckn', 'ones_f'], ['bk4'])
            self.cp('act', kb_[:, 0:1], pk_[:, 0:1], ['bk4'], ['n_kb'])
            S.op('dve', lambda e: e.reduce_max(out=kb_[:, 1:2], in_=gm[:, 8:12], axis=AX.X), [gmk], ['n_kb'])
            self.tt('dve', kb_[:, 0:1], kb_[:, 0:1], kb_[:, 1:2], ALU.max, ['n_kb'], ['n_kb'])
            S.op('dve', lambda e: e.reduce_max(out=kb_[:, 2:3], in_=gm[:, 0:8], axis=AX.X), [gmk], ['n_kb'])
            self.tt('dve', kb_[:, 0:1], kb_[:, 0:1], kb_[:, 2:3], ALU.mult, ['n_kb'], ['n_kb'])
            self.act(kb_[:, 3:4], kb_[:, 0:1], AF.Sqrt, ['n_kb'], ['n_kb'], scale=1.0 / 64.0)
            self.ts('dve', negM[:, 0:1], kb_[:, 3:4], -1.0, None, ALU.mult, None, ['n_kb'], ['n_negM'])
            self.ts('dve', negM[:, 1:2], kb_[:, 3:4], -2.0, -10.0, ALU.mult, ALU.add, ['n_kb'], ['n_negM'])
            S.barrier()
        B = dict(
            ET=[self.sb(st, f"n_ET{g}", [P, 512], BF16) for g in range(NG)],
            PM=[self.sb(st, f"n_PM{g}", [P, 512], BF16) for g in range(NG)],
            KW=[self.sb(st, f"n_KW{g}", [67, 5 * P], BF16) for g in range(NG)],
            VW=self.sb(st, "n_VW", [P, 5, 130], BF16),
            rz=self.sb(st, "n_rz", [P, 3, 8], F32),
            coef=self.sb(st, "n_coef", [P, 3, 8], F32),
            imp=self.sb(st, "n_imp", [P, NG, P], F32),
            sc2=self.sb(st, "n_sc2", [P, P], F32),
            m8=self.sb(st, "n_m8", [P, 16], F32),
            selb=self.sb(st, "n_selb", [P, NG, P], BF16),
            selT=self.sb(st, "n_selT", [64, 2, NG * P], BF16),
            oacc=self.sb(st, "n_oacc", [P, 8, 64], F32),
            otmp=self.sb(st, "n_otmp", [P, 4, 64], F32),
            mixa=self.sb(st, "n_mixa", [P, 512], BF16),
        )
        SB = self.nsa_sample_bufs(st, l, QB)
        return dict(KS=KS, VS=VS, CK=CK, CVP=CVP, POOLM=POOLM, OTS=OTS, QB=QB, QA=QA, ESEL=ESEL, negM=negM, B=B, NCT=NCT, SB=SB)

    def attn_scores(self, ns, g, lhsT, nk, QA, NQW, negcol, mask):
        S = self.S
        ET = ns['B']['ET'][g]
        ps = self.bk[g]
        self.mm(ps[0:nk, 0:NQW], lhsT, QA, True, True, [f'n_KSc0{g}', f'n_KSc1{g}', f'n_CK{g}', f'n_KW{g}', f'n_QA{g}'], [f'bk{g}'])
        self.act(ET[0:nk, 0:NQW], ps[0:nk, 0:NQW], AF.Exp, [f'bk{g}', 'n_negM'], [f'n_ET{g}'], scale=0.125, bias=negcol[0:nk, :])
        if mask is not None:
            pattern, base, cm = mask
            S.op('pool', lambda e: e.affine_select(out=ET[0:nk, 0:NQW], in_=ET[0:nk, 0:NQW], pattern=pattern, compare_op=ALU.is_ge,
                                                   fill=self.fill0, base=base, channel_multiplier=cm), [f'n_ET{g}'], [f'n_ET{g}'])
        return ET

    def page_index(self, st, e_, name, l=0, tiles=None):
        cfg, S, I = self.cfg, self.S, self.I
        NPG = cfg.NPAGE
        if tiles is None:
            tiles = (self.sb(st, f"ptb_{name}", [P, NPG], I32), self.sb(st, f"ptf_{name}", [P, NPG], F32),
                     self.sb(st, f"pio_{name}", [P, 1], I32), self.sb(st, f"pif_{name}", [P, 1], F32))
        ptb, ptf, pio, pif = tiles
        k = f'idx_{id(ptb)}'
        self.ld(ptb[:], I['page_table'][e_:e_ + 1, :].partition_broadcast(P), (), [k])
        S.op('pool', lambda e: e.iota(pio[:], pattern=[[0, 1]], base=l * cfg.NPHYS * P, channel_multiplier=1), (), [k + 'p'])
        self.cp('dve', pif[:], pio[:], [k + 'p'], [k + 'pf'])
        self.cp('dve', ptf[:], ptb[:], [k], [k + 'f'])
        self.ts('dve', ptf[:], ptf[:], 128.0, pif[:, 0:1], ALU.mult, ALU.add, [k + 'f', k + 'pf'], [k + 'f'])
        self.cp('dve', ptb[:], ptf[:], [k + 'f'], [k])
        return ptb, k

    def sample_compress(self, s2, l, kv, W1B, W2B, W2P, bias, cT, H1, hx, hy):
        cfg, S, I, X = self.cfg, self.S, self.I, self.X
        NCS = cfg.PAST // 16 - 1
        CH = 128
        with ExitStack() as s3:
            pg = [self.sb(s3, f"sc_pg{i}", [P, 256], F32) for i in range(2)]
            cks = self.sb(s3, "sc_cks", [64, CH], BF16)
            cvs = self.sb(s3, "sc_cvs", [P, P], BF16)
            for e_ in range(cfg.NS):
                if e_ == 0:
                    NPG_ = cfg.NPAGE
                    pit = (self.sb(s3, "sc_ptb", [P, NPG_], I32), self.sb(s3, "sc_ptf", [P, NPG_], F32),
                           self.sb(s3, "sc_pio", [P, 1], I32), self.sb(s3, "sc_pif", [P, 1], F32))
                idx, idxk = self.page_index(s3, e_, "sc", l, pit)
                for n0 in range(0, NCS, CH):
                    nn = min(CH, NCS - n0)
                    p0 = n0 // 8
                    npg = min(17, cfg.NPAGE - p0)
                    for pj in range(npg):
                        b = pj % 2
                        if b == 0:
                            S.idma_batch([(pg[bb][:], I['cache_cmp'], idx[:, p0 + pj + bb:p0 + pj + bb + 1].bitcast(U32), [idxk], [f'sc_pg{bb}'])
                                          for bb in range(min(2, npg - pj))])
                        self.tr(self.bk[4 + b][:, 0:P], pg[b][:, kv * P:(kv + 1) * P], self.ident_f[:], [f'sc_pg{b}', 'ident_f'], [f'bk{4 + b}'])
                        self.cp('act', cT[:, pj * P:(pj + 1) * P], self.bk[4 + b][:, 0:P], [f'bk{4 + b}'], ['n_cT'])
                    acc = self.bk[0]
                    for l_ in range(32):
                        rhs = cT[:, l_:l_ + 16 * (nn - 1) + 1:16]
                        self.mm(acc[:, 0:nn], W1B[:, l_, :], rhs, l_ == 0, l_ == 31, ['n_W1B', 'n_cT'], ['bk0'])
                    self.ts('dve', hx[:, 0:nn], acc[:, 0:nn], bias[:, 0:1], None, ALU.add, None, ['bk0', 'n_bias'], ['n_hx'])
                    self.tt('dve', hy[:, 0:nn], hx[:, 0:nn], hx[:, 0:nn], ALU.mult, ['n_hx'], ['n_hy'])
                    self.ts('dve', hy[:, 0:nn], hy[:, 0:nn], 0.044715, 1.0, ALU.mult, ALU.add, ['n_hy'], ['n_hy'])
                    self.tt('dve', hy[:, 0:nn], hy[:, 0:nn], hx[:, 0:nn], ALU.mult, ['n_hy', 'n_hx'], ['n_hy'])
                    self.act(hy[:, 0:nn], hy[:, 0:nn], AF.Sigmoid, ['n_hy'], ['n_hy'], scale=2.0 * math.sqrt(2.0 / math.pi))
                    self.tt('dve', H1[:, 0:nn], hy[:, 0:nn], hx[:, 0:nn], ALU.mult, ['n_hy', 'n_hx'], ['n_H1'])
                    if kv == 0:
                        for g in range(NG):
                            po = self.bk[1 + g]
                            self.mm(po[0:64, 0:nn], W2P[g][:, :], H1[:, 0:nn], True, True, ['n_H1', f'n_W2P{g}'], [f'bk{1 + g}'])
                            self.cp('act', cks[:, 0:nn], po[0:64, 0:nn], [f'bk{1 + g}'], ['sc_cks'])
                            self.ld(X['sck'][e_, g, :, n0:n0 + nn], cks[:, 0:nn], ['sc_cks'], ['Xsck'])
                    else:
                        po = self.bk[1]
                        self.mm(po[0:nn, 0:P], H1[:, 0:nn], W2B[:, :], True, True, ['n_H1', 'n_W2B'], ['bk1'])
                        self.cp('act', cvs[0:nn, :], po[0:nn, 0:P], ['bk1'], ['sc_cvs'])
                        self.ld(X['scv'][e_, n0 // P, 0:nn, :], cvs[0:nn, :], ['sc_cvs'], ['Xscv'])
            S.barrier()

    def nsa_sample_bufs(self, st, l, QB):
        cfg, S = self.cfg, self.S
        NCS = cfg.PAST // 16 - 1
        NCSL = (NCS + P - 1) // P
        NBS = (cfg.PAST + cfg.S + 63) // 64
        NBT = (NBS + P - 1) // P
        SB = dict(NCS=NCS, NCSL=NCSL, NBS=NBS, NBT=NBT)
        SB['QA'] = [self.sb(st, f"ns_QA{g}", [67, 16], BF16) for g in range(NG)]
        SB['CK'] = [self.sb(st, f"ns_CK{g}", [67, NCSL * P], BF16) for g in range(NG)]
        SB['CVP'] = [self.sb(st, f"ns_CVP{g}", [P, NCSL, 65], BF16) for g in range(NG)]
        SB['POOL'] = self.sb(st, "ns_POOL", [P, NCSL, P], BF16)
        SB['KT'] = [self.sb(st, f"ns_KT{g}", [67, P], BF16) for g in range(NG)]
        SB['VP'] = self.sb(st, "ns_VP", [P, 130], BF16)
        SB['pg'] = [self.sb(st, f"ns_pg{i}", [P, 256], F32) for i in range(2)]
        SB['pit'] = (self.sb(st, "ns_ptb", [P, cfg.NPAGE], I32), self.sb(st, "ns_ptf", [P, cfg.NPAGE], F32),
                     self.sb(st, "ns_pio", [P, 1], I32), self.sb(st, "ns_pif", [P, 1], F32))
        SB['selT'] = self.sb(st, "ns_selT", [64, 2 * NBT, 8], BF16)
        SB['imp'] = self.sb(st, "ns_imp", [4, NG, NBT * P], F32)
        SB['sc2'] = self.sb(st, "ns_sc2", [4, NBT * P], F32)
        SB['selb'] = self.sb(st, "ns_selb", [4, NG, NBT * P], BF16)
        with ExitStack() as s2:
            self.make_aug_rows(s2, [(SB['CK'][g], f'ns_CK{g}') for g in range(NG)], NCSL * P, 'cmp')
            self.make_aug_rows(s2, [(SB['KT'][g], f'ns_KT{g}') for g in range(NG)], P, 'tok')
            zr = self.sb(s2, "ns_zr", [1, P], BF16)
            self.memset('dve', zr[:], 0.0, ['ns_zr'])
            for g in range(NG):
                self.ld(SB['KT'][g][65:66, :], zr[0:1, :], ['ns_zr'], [f'ns_KT{g}'])
                with self.nc.allow_non_contiguous_dma(reason="tiny static rows"):
                    self.ld(SB['QA'][g][65:67, :].rearrange("p (h q) -> p h q", h=4), QB[g][65:67, :].rearrange("p (h q) -> p h q", h=4)[:, :, 0:4], [f'n_QB{g}'], [f'ns_QA{g}'])
            pi2 = self.sb(s2, "ns_pi2", [P, NCSL, P], I32)
            pf2 = self.sb(s2, "ns_pf2", [P, NCSL, P], F32)
            pf3 = self.sb(s2, "ns_pf3", [P, NCSL, P], F32)
            for c in range(NCSL):
                S.op('pool', lambda e, c=c: e.iota(pi2[:, c, :], pattern=[[4, P]], base=512 * (c // 4) - 128 * c, channel_multiplier=-1), (), ['ns_pi2'])
            self.ts('dve', pf2[:], pi2[:], 0, None, ALU.is_le, None, ['ns_pi2'], ['ns_pf2'])
            self.ts('dve', pf3[:], pi2[:], -3, None, ALU.is_ge, None, ['ns_pi2'], ['ns_pf3'])
            self.tt('dve', SB['POOL'][:], pf2[:], pf3[:], ALU.mult, ['ns_pf2', 'ns_pf3'], ['ns_POOL'])
            for g in range(NG):
                self.memset('dve', SB['CVP'][g][:], 1.0, [f'ns_CVP{g}'])
            self.memset('dve', SB['VP'][:], 1.0, ['ns_VP'])
            S.barrier()
        return SB

    def nsa_sample(self, ns, l, e_, gnt_s, gates_s, mixT_dst, mixk):
        cfg, S, I, O, X = self.cfg, self.S, self.I, self.O, self.X
        SB, B = ns['SB'], ns['B']
        NT, S4, PAST = cfg.NT, cfg.S, cfg.PAST
        NCS, NCSL, NBS, NBT = SB['NCS'], SB['NCSL'], SB['NBS'], SB['NBT']
        QA, QB, CK, CVP, KT, VP, pg, selT = SB['QA'], ns['QB'], SB['CK'], SB['CVP'], SB['KT'], SB['VP'], SB['pg'], SB['selT']
        negc = ns['negM'][:, 1:2]
        NQW = 4 * S4
        rz, coef, oacc, otmp, OTS = B['rz'], B['coef'], B['oacc'], B['otmp'], ns['OTS']
        tq = (PAST - 64) / 128.0
        cols = slice(e_ * S4, (e_ + 1) * S4)
        qv = lambda a: a.rearrange("p (h q) -> p h q", h=4)

        def set_qrow(g, tval):
            self.ts('dve', qv(QA[g][64:65, :]), qv(QB[g][64:65, :])[:, :, 0:S4], -float(tval), None, ALU.mult, None, [f'n_QB{g}'], [f'ns_QA{g}'])

        def scores(g, lhsT, nk, keys, mask=None):
            ET = B['ET'][g]
            ps = self.bk[g]
            self.mm(ps[0:nk, 0:NQW], lhsT, QA[g][0:67, :], True, True, keys + [f'ns_QA{g}'], [f'bk{g}'])
            self.act(ET[0:nk, 0:NQW], ps[0:nk, 0:NQW], AF.Exp, [f'bk{g}', 'n_negM'], [f'n_ET{g}'], scale=0.125, bias=negc[0:nk, :])
            if mask is not None:
                pattern, base, cm = mask
                S.op('pool', lambda e: e.affine_select(out=ET[0:nk, 0:NQW], in_=ET[0:nk, 0:NQW], pattern=pattern, compare_op=ALU.is_ge,
                                                       fill=self.fill0, base=base, channel_multiplier=cm), [f'n_ET{g}'], [f'n_ET{g}'])
            return ET

        def finish_branch(br, first):
            for g in range(NG):
                self.cp('act', OTS[0:65, g, 0:NQW], self.bk[2 + g][0:65, 0:NQW], [f'bk{2 + g}'], ['n_OTS'])
                for hl in range(HPG):
                    self.tr(self.bk[6 + g][0:S4, hl * 65:(hl + 1) * 65], OTS[0:65, g, hl * S4:(hl + 1) * S4], self.ident_f[0:65, 0:65], ['n_OTS', 'ident_f'], [f'bk{6 + g}'])
                ov = self.bk[6 + g][0:S4, 0:260].rearrange("p (h w) -> p h w", h=HPG)
                k_ = f'bk{6 + g}'
                self.ts('dve', rz[0:S4, br, 4 * g:4 * g + 4], ov[:, :, 64], 1e-36, None, ALU.max, None, [k_], ['n_rz'])
                S.op('dve', lambda e: e.reciprocal(out=rz[0:S4, br, 4 * g:4 * g + 4], in_=rz[0:S4, br, 4 * g:4 * g + 4]), ['n_rz'], ['n_rz'])
                self.tt('dve', coef[0:S4, br, 4 * g:4 * g + 4], rz[0:S4, br, 4 * g:4 * g + 4], gnt_s[0:S4, 8 * br + 4 * g:8 * br + 4 * g + 4], ALU.mult, ['n_rz', 'gnt_s'], ['n_coef'])
                bc_ = coef[0:S4, br, 4 * g:4 * g + 4].unsqueeze(2).to_broadcast([S4, HPG, 64])
                if first:
                    self.tt('dve', oacc[0:S4, 4 * g:4 * g + 4, :], ov[:, :, 0:64], bc_, ALU.mult, [k_, 'n_coef'], ['n_oacc'])
                else:
                    self.tt('dve', otmp[0:S4], ov[:, :, 0:64], bc_, ALU.mult, [k_, 'n_coef'], ['n_otmp'])
                    self.tt('dve', oacc[0:S4, 4 * g:4 * g + 4, :], oacc[0:S4, 4 * g:4 * g + 4, :], otmp[0:S4], ALU.add, ['n_oacc', 'n_otmp'], ['n_oacc'])

        for g in range(NG):
            self.ld(qv(QA[g][0:64, :]), qv(X['qT'][NT][64 * g:64 * g + 64, :])[:, :, cols], ['XqT'], [f'ns_QA{g}'])
            set_qrow(g, tq)
            self.ld(CK[g][0:64, 0:NCS], X['sck'][e_, g][:, 0:NCS], ['Xsck'], [f'ns_CK{g}'])
            for c in range(NCSL):
                nk = min(P, NCS - c * P)
                self.ld(CVP[g][0:nk, c, 0:64], X['scv'][e_, c, 0:nk, 64 * g:64 * g + 64], ['Xscv'], [f'ns_CVP{g}'])
        for c in range(NCSL):
            nk = min(P, NCS - c * P)
            bt = c // 4
            for g in range(NG):
                ET = scores(g, CK[g][0:67, c * P:c * P + nk], nk, [f'ns_CK{g}'])
                self.mm(self.bk[2 + g][0:65, 0:NQW], CVP[g][0:nk, c, :], ET[0:nk, 0:NQW], c == 0, c == NCSL - 1, [f'n_ET{g}', f'ns_CVP{g}'], [f'bk{2 + g}'])
                pb_ = 4 + 2 * g + bt
                self.mm(self.bk[pb_][:, 0:NQW], SB['POOL'][0:nk, c, :], ET[0:nk, 0:NQW], c % 4 == 0, (c % 4 == 3) or (c == NCSL - 1), [f'n_ET{g}', 'ns_POOL'], [f'bk{pb_}'])
        imp, sc2, selb = SB['imp'], SB['sc2'], SB['selb']
        self.memset('dve', imp[:], 0.0, ['ns_imp'])
        nbt_c = (NCSL + 3) // 4
        for g in range(NG):
            for bt in range(nbt_c):
                pb_ = 4 + 2 * g + bt
                self.cp('act', OTS[:, 2 + bt, 0:NQW], self.bk[pb_][:, 0:NQW], [f'bk{pb_}'], ['n_OTS'])
            self.cp('act', OTS[0:65, g, 0:NQW], self.bk[2 + g][0:65, 0:NQW], [f'bk{2 + g}'], ['n_OTS'])
            for hl in range(HPG):
                self.tr(self.bk[g][0:S4, hl * 65:(hl + 1) * 65], OTS[0:65, g, hl * S4:(hl + 1) * S4], self.ident_f[0:65, 0:65], ['n_OTS', 'ident_f'], [f'bk{g}'])
            ov = self.bk[g][0:S4, 0:260].rearrange("p (h w) -> p h w", h=HPG)
            self.ts('dve', rz[0:S4, 0, 4 * g:4 * g + 4], ov[:, :, 64], 1e-36, None, ALU.max, None, [f'bk{g}'], ['n_rz'])
            S.op('dve', lambda e: e.reciprocal(out=rz[0:S4, 0, 4 * g:4 * g + 4], in_=rz[0:S4, 0, 4 * g:4 * g + 4]), ['n_rz'], ['n_rz'])
            self.tt('dve', coef[0:S4, 0, 4 * g:4 * g + 4], rz[0:S4, 0, 4 * g:4 * g + 4], gnt_s[0:S4, 4 * g:4 * g + 4], ALU.mult, ['n_rz', 'gnt_s'], ['n_coef'])
            self.tt('dve', oacc[0:S4, 4 * g:4 * g + 4, :], ov[:, :, 0:64], coef[0:S4, 0, 4 * g:4 * g + 4].unsqueeze(2).to_broadcast([S4, HPG, 64]), ALU.mult, [f'bk{g}', 'n_coef'], ['n_oacc'])
            for bt in range(nbt_c):
                for hl in range(HPG):
                    self.tr(self.bk[2 + g][0:S4, hl * P:(hl + 1) * P], OTS[:, 2 + bt, hl * S4:(hl + 1) * S4], self.ident_f[:], ['n_OTS', 'ident_f'], [f'bk{2 + g}'])
                for hl in range(HPG):
                    h = 4 * g + hl
                    self.stt(imp[0:S4, g, bt * P:(bt + 1) * P], self.bk[2 + g][0:S4, hl * P:(hl + 1) * P], rz[0:S4, 0, h:h + 1], imp[0:S4, g, bt * P:(bt + 1) * P],
                             ALU.mult, ALU.add, [f'bk{2 + g}', 'n_rz', 'ns_imp'], ['ns_imp'])
        cur = PAST // 64
        if NBT * P > NBS:
            self.memset('dve', imp[0:S4, :, NBS:NBT * P], -1.0, ['ns_imp'])
        self.memset('dve', imp[0:S4, :, 0:1], 1e9, ['ns_imp'])
        self.memset('dve', imp[0:S4, :, cur - 1:cur + 1], 1e9, ['ns_imp'])
        m8 = B['m8']
        for g in range(NG):
            S.op('dve', lambda e: e.max(out=m8[0:S4, 0:8], in_=imp[0:S4, g, :]), ['ns_imp'], ['n_m8'])
            S.op('dve', lambda e: e.match_replace(out=sc2[0:S4, :], in_to_replace=m8[0:S4, 0:8], in_values=imp[0:S4, g, :], imm_value=-3e9), ['ns_imp', 'n_m8'], ['ns_sc2'])
            S.op('dve', lambda e: e.max(out=m8[0:S4, 8:16], in_=sc2[0:S4, :]), ['ns_sc2'], ['n_m8'])
            self.ts('dve', selb[0:S4, g, :], imp[0:S4, g, :], m8[0:S4, 15:16], None, ALU.is_ge, None, ['ns_imp', 'n_m8'], ['ns_selb'])
            for hb in range(2 * NBT):
                self.tr(self.bkb[7][0:64, (hb * NG + g) * S4:(hb * NG + g + 1) * S4], selb[0:S4, g, hb * 64:(hb + 1) * 64], self.ident_b[0:S4, 0:S4], ['ns_selb', 'ident_b'], ['bk7'])
        self.cp('act', selT[:].rearrange("p t c -> p (t c)"), self.bkb[7][0:64, 0:2 * NBT * 8], ['bk7'], ['ns_selT'])
        idx, idxk = self.page_index_cached(ns, e_, l)
        npg = cfg.NPAGE

        def key_tile(src_rows, nk, jpage, first, last, sel_bt, sel_r, extra_mask):
            for g in range(NG):
                self.tr(self.bk[6][0:64, g * P:g * P + nk], src_rows[0:nk, 64 * g:64 * g + 64], self.ident_f[0:nk, 0:nk], ['ns_pgcur', 'ident_f'], ['bk6'])
                self.cp('act', KT[g][0:64, 0:nk], self.bk[6][0:64, g * P:g * P + nk], ['bk6'], [f'ns_KT{g}'])
            self.cp('dve', VP[0:nk, :].rearrange("p (g w) -> p g w", g=NG)[:, :, 0:64], src_rows[0:nk, 128:256].rearrange("p (g d) -> p g d", g=NG), ['ns_pgcur'], ['ns_VP'])
            if sel_bt is not None:
                a2, r = sel_r // 32, sel_r % 32
                self.mm(self.bk[7][0:nk, 0:8], ns['ESEL'][0:64, r, 0:nk], selT[0:64, 2 * sel_bt + a2, :], True, True, ['n_ESEL', 'ns_selT'], ['bk7'])
            for g in range(NG):
                set_qrow(g, tq - jpage)
                ET = scores(g, KT[g][0:67, 0:nk], nk, [f'ns_KT{g}'], extra_mask)
                src = ET
                if sel_bt is not None:
                    PM = B['PM'][g]
                    self.tt('dve', PM[0:nk, 0:NQW].rearrange("p (h q) -> p h q", h=HPG), ET[0:nk, 0:NQW].rearrange("p (h q) -> p h q", h=HPG),
                            self.bk[7][0:nk, g * S4:(g + 1) * S4].unsqueeze(1).to_broadcast([nk, HPG, S4]), ALU.mult, [f'n_ET{g}', 'bk7'], [f'n_PM{g}'])
                    src = PM
                self.mm(self.bk[2 + g][0:65, 0:NQW], VP[0:nk, 65 * g:65 * g + 65], src[0:nk, 0:NQW], first, last, [f'n_ET{g}', f'n_PM{g}', 'ns_VP'], [f'bk{2 + g}'])

        newr = SB['pg'][0]
        for j in range(npg):
            b = j % 2
            if b == 0:
                S.idma_batch([(pg[bb][:], I['cache_slc'], idx[:, j + bb:j + bb + 1].bitcast(U32), [idxk], ['ns_pgcur'])
                              for bb in range(min(2, npg - j))])
            key_tile(pg[b], P, j, j == 0, False, j // 64, j % 64, None)
        self.ld(newr[0:S4, :], O['slc_s'][l, e_ * S4:(e_ + 1) * S4, :], ['Oslc_s'], ['ns_pgcur'])
        key_tile(newr, S4, PAST // P, False, True, cur // P, (cur % P) // 2, ([[0, 4], [1, S4]], 0, -1))
        finish_branch(1, False)
        WB = cfg.WB
        nwt = WB // P
        for i in range(nwt):
            b = i % 2
            self.ld(pg[b][:], I['cache_win'][l, e_, i * P:(i + 1) * P, :], (), ['ns_pgcur'])
            mask = ([[0, 4], [-1, S4]], 0, 1) if i == 0 else None
            key_tile(pg[b], P, (PAST - WB) // P + i, i == 0, False, None, None, mask)
        self.ld(newr[0:S4, :], O['win_s'][l, e_, WB - S4:WB, :], ['Owin_s'], ['ns_pgcur'])
        key_tile(newr, S4, PAST // P, False, True, None, None, ([[0, 4], [1, S4]], 0, -1))
        finish_branch(2, False)
        self.ld(O['win_s'][l, e_, 0:WB - S4, :], I['cache_win'][l, e_, S4:WB, :], (), ['Owin_s2'])
        mixa = B['mixa']
        self.tt('dve', mixa[0:S4, :], oacc[0:S4].rearrange("p h d -> p (h d)"), gates_s[0:S4, 0:512], ALU.mult, ['n_oacc', 'gates_s'], ['n_mixa'])
        for j in range(4):
            self.tr(self.bkb[7][:, j * P:j * P + S4], mixa[0:S4, j * P:(j + 1) * P], self.ident_b[0:S4, 0:S4], ['n_mixa', 'ident_b'], ['bk7'])
        self.cp('act', mixT_dst[:, 0:4, cols], self.bkb[7][:, 0:512].rearrange("p (k t) -> p k t", k=4)[:, :, 0:S4], ['bk7'], [mixk])

    def nsa_tile(self, ns, l, t, gnt, gate, gatek, mixT_dst, mixk):
        cfg, S, I, O, X = self.cfg, self.S, self.I, self.O, self.X
        B = ns['B']
        NT = cfg.NT
        NCT = ns['NCT']
        QA, QB, KS, VS, CK, CVP = ns['QA'], ns['QB'], ns['KS'], ns['VS'], ns['CK'], ns['CVP']
        rz, coef, imp, oacc, otmp = B['rz'], B['coef'], B['imp'], B['oacc'], B['otmp']
        OTS = ns['OTS']
        for g in range(NG):
            self.ld(QA[g][0:64, :], X['qT'][t][64 * g:64 * g + 64, :], ['XqT'], [f'n_QA{g}'])
            self.ts('dve', QA[g][64:65, :], QB[g][64:65, :], -float(t), None, ALU.mult, None, [f'n_QB{g}'], [f'n_QA{g}'])
        ncv = min(NCT, 8 * t + 7)
        nct = (ncv + P - 1) // P
        for c in range(nct):
            nk = min(P, ncv - c * P)
            last_n = c * P + nk - 1
            partial = 16 * last_n + 31 > 128 * t
            mask = ([[0, 4], [1, P]], 128 * t - 2048 * c - 31, -16) if partial else None
            for g in range(NG):
                ET = self.attn_scores(ns, g, CK[g][0:67, c * P:c * P + nk], nk, QA[g][0:67, :], 512, ns['negM'][:, 0:1], mask)
                self.mm(self.bk[2 + g][0:65, 0:512], CVP[g][0:nk, c, :], ET[0:nk, 0:512], c == 0, c == nct - 1, [f'n_ET{g}', f'n_CVP{g}'], [f'bk{2 + g}'])
                self.mm(self.bk[4 + g][:, 0:512], ns['POOLM'][0:nk, c, :], ET[0:nk, 0:512], c == 0, c == nct - 1, [f'n_ET{g}', 'n_POOLM'], [f'bk{4 + g}'])
        for g in range(NG):
            self.cp('act', OTS[0:65, g, :], self.bk[2 + g][0:65, 0:512], [f'bk{2 + g}'], ['n_OTS'])
            self.cp('act', OTS[:, 2 + g, :], self.bk[4 + g][:, 0:512], [f'bk{4 + g}'], ['n_OTS'])
        for g in range(NG):
            for hl in range(HPG):
                self.tr(self.bk[6 + g][:, hl * 65:(hl + 1) * 65], OTS[0:65, g, hl * P:(hl + 1) * P], self.ident_f[0:65, 0:65], ['n_OTS', 'ident_f'], [f'bk{6 + g}'])
                self.tr(self.bk[g][:, hl * P:(hl + 1) * P], OTS[:, 2 + g, hl * P:(hl + 1) * P], self.ident_f[:], ['n_OTS', 'ident_f'], [f'bk{g}'])
        for g in range(NG):
            ov = self.bk[6 + g][:, 0:260].rearrange("p (h w) -> p h w", h=HPG)
            k_ = f'bk{6 + g}'
            self.ts('dve', rz[:, 0, 4 * g:4 * g + 4], ov[:, :, 64], 1e-36, None, ALU.max, None, [k_], ['n_rz'])
            S.op('dve', lambda e: e.reciprocal(out=rz[:, 0, 4 * g:4 * g + 4], in_=rz[:, 0, 4 * g:4 * g + 4]), ['n_rz'], ['n_rz'])
            self.tt('dve', coef[:, 0, 4 * g:4 * g + 4], rz[:, 0, 4 * g:4 * g + 4], gnt[:, 4 * g:4 * g + 4], ALU.mult, ['n_rz', 'gnt_p'], ['n_coef'])
            self.tt('dve', oacc[:, 4 * g:4 * g + 4, :], ov[:, :, 0:64], coef[:, 0, 4 * g:4 * g + 4].unsqueeze(2).to_broadcast([P, HPG, 64]), ALU.mult, [k_, 'n_coef'], ['n_oacc'])
            for hl in range(HPG):
                h = 4 * g + hl
                src = self.bk[g][:, hl * P:(hl + 1) * P]
                if hl == 0:
                    self.ts('dve', imp[:, g, :], src, rz[:, 0, h:h + 1], None, ALU.mult, None, [f'bk{g}', 'n_rz'], ['n_imp'])
                else:
                    self.stt(imp[:, g, :], src, rz[:, 0, h:h + 1], imp[:, g, :], ALU.mult, ALU.add, [f'bk{g}', 'n_rz', 'n_imp'], ['n_imp'])
        if KNSA == 'cmp':
            return
        for h2 in range(2):
            cur = 2 * t + h2
            ps_ = slice(64 * h2, 64 * h2 + 64)
            if cur < P - 1:
                S.op('pool', lambda e: e.affine_select(out=imp[ps_, :, :], in_=imp[ps_, :, :], pattern=[[0, NG], [-1, P]], compare_op=ALU.is_ge,
                                                       fill=self.fillm1, base=cur, channel_multiplier=0), ['n_imp'], ['n_imp'])
            self.memset('pool', imp[ps_, :, 0:1], 1e9, ['n_imp'])
            lo = max(cur - 1, 0)
            self.memset('pool', imp[ps_, :, lo:cur + 1], 1e9, ['n_imp'])
        m8, sc2, selb, selT = B['m8'], B['sc2'], B['selb'], B['selT']
        for g in range(NG):
            S.op('dve', lambda e: e.max(out=m8[:, 0:8], in_=imp[:, g, :]), ['n_imp'], ['n_m8'])
            S.op('dve', lambda e: e.match_replace(out=sc2[:], in_to_replace=m8[:, 0:8], in_values=imp[:, g, :], imm_value=-3e9), ['n_imp', 'n_m8'], ['n_sc2'])
            S.op('dve', lambda e: e.max(out=m8[:, 8:16], in_=sc2[:]), ['n_sc2'], ['n_m8'])
            self.ts('dve', selb[:, g, :], imp[:, g, :], m8[:, 15:16], None, ALU.is_ge, None, ['n_imp', 'n_m8'], ['n_selb'])
            for hb in range(2):
                self.tr(self.bkb[7][0:64, (hb * NG + g) * P:(hb * NG + g + 1) * P], selb[:, g, hb * 64:(hb + 1) * 64], self.ident_b[:], ['n_selb', 'ident_b'], ['bk7'])
        self.cp('act', selT[:].rearrange("p a c -> p (a c)"), self.bkb[7][0:64, 0:2 * NG * P], ['bk7'], ['n_selT'])
        if KNSA == 'sel':
            return
        PM = B['PM']
        CHK = 8
        for kt in range(t + 1):
            cb, ci = (kt // CHK) % 2, kt % CHK
            if ci == 0:
                nkt = min(CHK, t + 1 - kt)
                for g in range(NG):
                    self.ld(KS[cb][g][0:64, 0:nkt * P].rearrange("p (t k) -> p t k", k=P), X['kTs'][kt:kt + nkt, 64 * g:64 * g + 64, :].rearrange("t p k -> p t k"), ['XkTs'], [f'n_KSc{cb}{g}'])
                    self.ld(KS[cb][g][64:67, 0:nkt * P], X['kaug'][:, kt * P:(kt + nkt) * P], ['Xkaug'], [f'n_KSc{cb}{g}'])
                self.ld(VS[cb][:, 0:nkt, :], X['vs'][kt:kt + nkt].rearrange("t p c -> p t c"), ['Xvs'], [f'n_VSc{cb}'])
            a2, r = kt // 32, kt % 32
            self.mm(self.bk[6][:, 0:NG * P], ns['ESEL'][0:64, r, :], selT[0:64, a2, :], True, True, ['n_ESEL', 'n_selT'], ['bk6'])
            for g in range(NG):
                ET = self.attn_scores(ns, g, KS[cb][g][0:67, ci * P:(ci + 1) * P], P, QA[g][0:67, :], 512, ns['negM'][:, 0:1], None)
                self.tt('dve', PM[g][:].rearrange("p (h q) -> p h q", h=HPG), ET[:].rearrange("p (h q) -> p h q", h=HPG),
                        self.bk[6][:, g * P:(g + 1) * P].unsqueeze(1).to_broadcast([P, HPG, P]), ALU.mult, [f'n_ET{g}', 'bk6'], [f'n_PM{g}'])
                if kt == t:
                    S.op('pool', lambda e: e.affine_select(out=PM[g][:], in_=PM[g][:], pattern=[[0, 4], [1, P]], compare_op=ALU.is_ge,
                                                           fill=self.fill0, base=0, channel_multiplier=-1), [f'n_PM{g}'], [f'n_PM{g}'])
                self.mm(self.bk[2 + g][0:65, 0:512], VS[cb][:, ci, 65 * g:65 * g + 65], PM[g][:, 0:512], kt == 0, kt == t, [f'n_PM{g}', f'n_VSc{cb}'], [f'bk{2 + g}'])
        if KNSA == 'slc':
            return
        kt0 = max(0, t - 4)
        nw = t - kt0 + 1
        KW, VW = B['KW'], B['VW']
        for g in range(NG):
            self.ld(KW[g][0:64, 0:nw * P].rearrange("p (t k) -> p t k", k=P), X['kTw'][kt0:t + 1, 64 * g:64 * g + 64, :].rearrange("t p k -> p t k"), ['XkTw'], [f'n_KW{g}'])
            self.ld(KW[g][64:67, 0:nw * P], X['kaug'][:, kt0 * P:(t + 1) * P], ['Xkaug'], [f'n_KW{g}'])
        self.ld(VW[:, 0:nw, :], X['vw'][kt0:t + 1].rearrange("t p c -> p t c"), ['Xvw'], ['n_VW'])
        for i in range(nw):
            kt = kt0 + i
            mask = None
            if kt == t:
                mask = ([[0, 4], [1, P]], 0, -1)
            elif kt == t - 4:
                mask = ([[0, 4], [-1, P]], 0, 1)
            for g in range(NG):
                ET = self.attn_scores(ns, g, KW[g][0:67, i * P:(i + 1) * P], P, QA[g][0:67, :], 512, ns['negM'][:, 0:1], mask)
                self.mm(self.bk[4 + g][0:65, 0:512], VW[:, i, 65 * g:65 * g + 65], ET[:, 0:512], i == 0, i == nw - 1, [f'n_ET{g}', 'n_VW'], [f'bk{4 + g}'])
        for bi_, br in ((2, 1), (4, 2)):
            for g in range(NG):
                self.cp('act', OTS[0:65, g, :], self.bk[bi_ + g][0:65, 0:512], [f'bk{bi_ + g}'], ['n_OTS'])
                for hl in range(HPG):
                    self.tr(self.bk[6 + g][:, hl * 65:(hl + 1) * 65], OTS[0:65, g, hl * P:(hl + 1) * P], self.ident_f[0:65, 0:65], ['n_OTS', 'ident_f'], [f'bk{6 + g}'])
                ov = self.bk[6 + g][:, 0:260].rearrange("p (h w) -> p h w", h=HPG)
                k_ = f'bk{6 + g}'
                self.ts('dve', rz[:, br, 4 * g:4 * g + 4], ov[:, :, 64], 1e-36, None, ALU.max, None, [k_], ['n_rz'])
                S.op('dve', lambda e: e.reciprocal(out=rz[:, br, 4 * g:4 * g + 4], in_=rz[:, br, 4 * g:4 * g + 4]), ['n_rz'], ['n_rz'])
                self.tt('dve', coef[:, br, 4 * g:4 * g + 4], rz[:, br, 4 * g:4 * g + 4], gnt[:, 8 * br + 4 * g:8 * br + 4 * g + 4], ALU.mult, ['n_rz', 'gnt_p'], ['n_coef'])
                self.tt('dve', otmp[:], ov[:, :, 0:64], coef[:, br, 4 * g:4 * g + 4].unsqueeze(2).to_broadcast([P, HPG, 64]), ALU.mult, [k_, 'n_coef'], ['n_otmp'])
                self.tt('dve', oacc[:, 4 * g:4 * g + 4, :], oacc[:, 4 * g:4 * g + 4, :], otmp[:], ALU.add, ['n_oacc', 'n_otmp'], ['n_oacc'])
        self.tt('dve', B['mixa'][:], oacc[:].rearrange("p h d -> p (h d)"), gate, ALU.mult, ['n_oacc', gatek], ['n_mixa'])
        for j in range(4):
            self.tr(self.bkb[7][:, j * P:(j + 1) * P], B['mixa'][:, j * P:(j + 1) * P], self.ident_b[:], ['n_mixa', 'ident_b'], ['bk7'])
        self.cp('act', mixT_dst[:, 0:4, :], self.bkb[7][:, 0:512].rearrange("p (k t) -> p k t", k=4), ['bk7'], [mixk])

    def mem_setup(self, st, l, gm, gmk):
        cfg, S, I, O, X = self.cfg, self.S, self.I, self.O, self.X
        mkT = self.sb(st, "mkT", [P, 4, NMEM], BF16)
        mv = self.sb(st, "mv", [P, 2, 4, 129], BF16)
        negM = self.sb(st, "mem_negM", [P, 2], F32)
        kn = self.sb(st, "mem_kn", [P, 4], F32)
        self.memset('pool', mv[:], 1.0, ['mkv'])
        self.memset('dve', kn[:], 0.0, ['mem_kn'])
        with ExitStack() as s2:
            wm = self.sb(s2, "w_mem", [P, 8, 1024], BF16)
            for k in range(8):
                self.ld(wm[:, k, :], I['w_mem_kv'][l, k * P:(k + 1) * P, :], (), [f'w_mem{k}'], q='pool')
            self.S.barrier()
            self.load_gain(I['mem_norm_g'][l:l + 1, :])
            xm = self.sb(s2, "xm", [P, D], F32)
            hb = self.sb(s2, "mhb", [P, D], BF16)
            hT = self.sb(s2, "mhT", [P, D], BF16)
            junk = self.sb(s2, "mjunk", [P, D], BF16)
            ss = self.sb(s2, "mss", [P, 2], F32)
            rstd = self.sb(s2, "mrstd", [P, 2], F32)
            stg = self.sb(s2, "mstg", [P, 1024], F32)
            kb = self.sb(s2, "mkb", [P, 512], BF16)
            sq = self.sb(s2, "msq", [P, 512], F32)
            n2 = self.sb(s2, "mn2", [P, 4], F32)
            pt = self.bkb[4]
            pz = [self.bk[0], self.bk[1]]
            for mt in range(2):
                self.ld(xm[:], I['mem_prompt'][mt * P:(mt + 1) * P, :], (), ['xm'])
                self.rmsnorm((junk, ss, rstd), xm[:], 'xm', hb[:], 'mhb', 'gtile')
                for k in range(8):
                    self.tr(pt[:, k * P:(k + 1) * P], hb[:, k * P:(k + 1) * P], self.ident_b[:], ['mhb', 'ident_b'], ['bk4'])
                self.cp('act', hT[:], pt[:], ['bk4'], ['mhT'])
                for c in range(2):
                    for k in range(8):
                        self.mm(pz[c][:, :], hT[:, k * P:(k + 1) * P], wm[:, k, c * 512:(c + 1) * 512], k == 0, k == 7, ['mhT', f'w_mem{k}'], [f'bk{c}'])
                    self.cp('act', stg[:, c * 512:(c + 1) * 512], pz[c][:, :], [f'bk{c}'], ['mstg'])
                self.ld(O['mem_p'][l, mt * P:(mt + 1) * P, :], stg[:], ['mstg'], ['Omem_p'], q='sp')
                self.cp('act', kb[:], pz[0][:, :], ['bk0'], ['mkb'])
                self.cp('act', mv[:, mt, :, 0:128], pz[1][:, :].rearrange("p (h d) -> p h d", h=4), ['bk1'], ['mkv'])
                self.tt('dve', sq[:], kb[:], kb[:], ALU.mult, ['mkb'], ['msq'])
                S.op('dve', lambda e: e.tensor_reduce(out=n2[:], in_=sq[:].rearrange("p (h d) -> p h d", d=128), op=ALU.add, axis=AX.X), ['msq'], ['mn2'])
                self.tt('dve', kn[:], kn[:], n2[:], ALU.max, ['mn2', 'mem_kn'], ['mem_kn'])
                for h in range(4):
                    self.tr(pt[:, h * P:(h + 1) * P], kb[:, h * P:(h + 1) * P], self.ident_b[:], ['mkb', 'ident_b'], ['bk4'])
                self.cp('act', mkT[:, :, mt * P:(mt + 1) * P], pt[:, 0:512].rearrange("p (h m) -> p h m", h=4), ['bk4'], ['mkv'])
            gk, gkk = self.gmax_bcast(s2, kn[:, 0:4], 'mem_kn', 4, f"mk{l}")
            self.mem_bound(s2, gm, gmk, gk, gkk, negM, 'mem_negM')
            self.S.barrier()
        mq = self.sb(st, "ma_mq", [P, 4, P], BF16)
        E = self.sb(st, "ma_E", [P, 2, 4, P], BF16)
        rz = self.sb(st, "ma_rz", [P, 4], F32)
        tmp = self.sb(st, "ma_tmp", [P, 512], F32)
        mkT_s = self.sb(st, "mkT_s", [P, 4, NMEM], BF16)
        mv_s = self.sb(st, "mv_s", [P, 2, 4, 129], BF16)
        negM_s = self.sb(st, "mem_negM_s", [P, 2], F32)
        kbs = self.sb(st, "mkb_s", [P, 512], BF16)
        sqs = self.sb(st, "msq_s", [P, 512], F32)
        kns = self.sb(st, "mkn_s", [P, 8], F32)
        self.memset('pool', mv_s[:], 1.0, ['mkv_s'])
        return dict(mkT=mkT, mv=mv, negM=negM, bufs=(mq, E, rz, tmp), mkT_s=mkT_s, mv_s=mv_s, negM_s=negM_s, sbufs=(kbs, sqs, kns))

    def mem_sample(self, mem, l, e_, stage, stagek, gm, gmk):
        cfg, S, I = self.cfg, self.S, self.I
        kbs, sqs, kns = mem['sbufs']
        mkT_s, mv_s, negM_s = mem['mkT_s'], mem['mv_s'], mem['negM_s']
        self.memset('dve', kns[:, 0:4], 0.0, ['mkn_s'])
        for mt in range(2):
            self.ld(sqs[:], I['cache_mem'][l, e_, mt * P:(mt + 1) * P, 0:512], (), ['msq_s'])
            self.cp('dve', kbs[:], sqs[:], ['msq_s'], ['mkb_s'])
            self.ld(sqs[:], I['cache_mem'][l, e_, mt * P:(mt + 1) * P, 512:1024], (), ['msq_s'])
            self.cp('dve', mv_s[:, mt, :, 0:128], sqs[:].rearrange("p (h d) -> p h d", h=4), ['msq_s'], ['mkv_s'])
            self.tt('dve', sqs[:], kbs[:], kbs[:], ALU.mult, ['mkb_s'], ['msq_s'])
            S.op('dve', lambda e: e.tensor_reduce(out=kns[:, 4:8], in_=sqs[:].rearrange("p (h d) -> p h d", d=128), op=ALU.add, axis=AX.X), ['msq_s'], ['mkn_s'])
            self.tt('dve', kns[:, 0:4], kns[:, 0:4], kns[:, 4:8], ALU.max, ['mkn_s'], ['mkn_s'])
            for h in range(4):
                self.tr(self.bkb[4][:, h * P:(h + 1) * P], kbs[:, h * P:(h + 1) * P], self.ident_b[:], ['mkb_s', 'ident_b'], ['bk4'])
            self.cp('act', mkT_s[:, :, mt * P:(mt + 1) * P], self.bkb[4][:, 0:512].rearrange("p (h m) -> p h m", h=4), ['bk4'], ['mkv_s'])
        with ExitStack() as s2:
            gk, gkk = self.gmax_bcast(s2, kns[:, 0:4], 'mkn_s', 4, f"mks{l}_{e_}")
            self.mem_bound(s2, gm, gmk, gk, gkk, negM_s, 'mem_negM')
            S.barrier()

    def mem_bound(self, st, gm, gmk, gk, gkk, negM, nk):
        S = self.S
        t1 = self.sb(st, "mb_t1", [P, 2], F32)
        S.op('dve', lambda e: e.reduce_max(out=t1[:, 0:1], in_=gm[:, 12:16], axis=AX.X), [gmk], ['mb_t1'])
        S.op('dve', lambda e: e.reduce_max(out=t1[:, 1:2], in_=gk[:, 0:4], axis=AX.X), [gkk], ['mb_t1'])
        self.tt('dve', t1[:, 0:1], t1[:, 0:1], t1[:, 1:2], ALU.mult, ['mb_t1'], ['mb_t1'])
        self.act(t1[:, 1:2], t1[:, 0:1], AF.Sqrt, ['mb_t1'], ['mb_t1'], scale=1.0 / 128.0)
        self.ts('dve', negM[:, 0:1], t1[:, 1:2], -1.0, None, ALU.mult, None, ['mb_t1'], [nk])

    def mem_attend(self, mem, t, NQ, mqT_src, gate, gatek, out_mix, mkT, mv, kvk, negM):
        S, X = self.S, self.X
        mq, E, rz, tmp = mem['bufs']
        pS = [self.bk[0], self.bk[1]]
        pO = [self.bk[2], self.bk[3]]
        self.ld(mq[:, :, 0:NQ], mqT_src.rearrange("p (h q) -> p h q", h=4)[:, :, 0:NQ] if NQ == P else mqT_src, ['XmqT'], ['ma_mq'])
        for mt in range(2):
            for h in range(4):
                self.mm(pS[mt][:, h * P:h * P + NQ], mkT[:, h, mt * P:(mt + 1) * P], mq[:, h, 0:NQ], True, True, ['ma_mq', kvk], [f'bk{mt}'])
            self.act(E[:, mt, :, 0:NQ], pS[mt][:, :].rearrange("p (h q) -> p h q", h=4)[:, :, 0:NQ], AF.Exp, [f'bk{mt}', 'mem_negM'], ['ma_E'],
                     scale=1.0 / math.sqrt(128.0), bias=negM[:, 0:1])
        for hp in range(2):
            for hh in range(2):
                h = hp * 2 + hh
                for mt in range(2):
                    self.mm(pO[hp][0:NQ, hh * 129:(hh + 1) * 129], E[:, mt, h, 0:NQ], mv[:, mt, h, :], mt == 0, mt == 1, ['ma_E', kvk], [f'bk{2 + hp}'])
            ov = pO[hp][0:NQ, 0:258].rearrange("p (h d) -> p h d", h=2)
            S.op('dve', lambda e: e.reciprocal(out=rz[0:NQ, hp * 2:hp * 2 + 2], in_=ov[:, :, 128]), [f'bk{2 + hp}'], ['ma_rz'])
            self.tt('dve', tmp[0:NQ, hp * 256:(hp + 1) * 256].rearrange("p (h d) -> p h d", h=2), ov[:, :, 0:128],
                    rz[0:NQ, hp * 2:hp * 2 + 2].unsqueeze(2).to_broadcast([NQ, 2, 128]), ALU.mult, [f'bk{2 + hp}', 'ma_rz'], ['ma_tmp'])
        self.tt('dve', out_mix[0:NQ, :], tmp[0:NQ, :], gate, ALU.mult, ['ma_tmp', gatek], ['mixc'])

    def phase_c(self, l):
        cfg, S, I, O, X = self.cfg, self.S, self.I, self.O, self.X
        NT = cfg.NT
        last = (l == DEPTH - 1)
        with ExitStack() as st:
            w1 = self.sb(st, "w_ff1", [P, 8, DFF], BF16)
            w2 = self.sb(st, "w_ff2", [P, 32, D], BF16)
            for k in range(8):
                self.ld(w1[:, k, :], I['w_ff1'][l, k * P:(k + 1) * P, :], (), [f'w_ff1{k}'], q='pool')
            for k in range(32):
                self.ld(w2[:, k, :], I['w_ff2'][l, k * P:(k + 1) * P, :], (), [f'w_ff2{k}'], q='pool')
            self.S.barrier()
            self.load_gain(I['norm2_g'][l:l + 1, :])
            gfin = None
            if last:
                gfin = self.sb(st, "gfin", [P, D], F32)
                self.ld(gfin[:], I['final_norm_g'][0:1, :].partition_broadcast(P), (), ['gfin'])
            xt = [self.sb(st, f"cx{i}", [P, D], F32) for i in range(4)]
            hb = self.sb(st, "chb", [P, D], BF16)
            hT = self.sb(st, "chT", [P, 8, 512], BF16)
            hid = self.sb(st, "hid", [P, 32, 512], BF16)
            rl = [self.sb(st, f"rl{i}", [P, 512], BF16) for i in range(2)]
            yo = [self.sb(st, f"yo{i}", [P, D], F32) for i in range(1)]
            junk = self.sb(st, "cjunk", [P, D], BF16)
            ss = self.sb(st, "css", [P, 2], F32)
            rstd = self.sb(st, "crstd", [P, 2], F32)
            pz = [self.ps(st, f"cpz{i}", [P, 512], F32) for i in range(4)]
            pt = [self.ps(st, f"cpt{i}", [P, 1024], BF16) for i in range(2)]
            zc = 0
            groups = [list(range(s_, min(s_ + 4, NT))) for s_ in range(0, NT, 4)] + [[NT]]
            for gi, tiles in enumerate(groups):
                W = len(tiles) * P
                for j, t in enumerate(tiles):
                    self.ld(xt[j][:], X['x1'][t * P:(t + 1) * P, :], ['x1'], [f'cx{j}'])
                    self.rmsnorm((junk, ss, rstd), xt[j][:], f'cx{j}', hb[:], 'chb', 'gtile')
                    pb = pt[j % 2]; pk = f'cpt{j % 2}'
                    for k in range(8):
                        self.tr(pb[:, k * P:(k + 1) * P], hb[:, k * P:(k + 1) * P], self.ident_b[:], ['chb', 'ident_b'], [pk])
                    self.cp('act', hT[:, :, j * P:(j + 1) * P], pb[:].rearrange("p (k t) -> p k t", k=8), [pk], ['chT'])
                for f in range(32):
                    z = pz[zc % 4]; zk = f'cpz{zc % 4}'; zc += 1
                    for k in range(8):
                        self.mm(z[:, 0:W], w1[:, k, f * P:(f + 1) * P], hT[:, k, 0:W], k == 0, k == 7, [f'w_ff1{k}', 'chT'], [zk])
                    r = rl[f % 2]; rk = f'rl{f % 2}'
                    self.act(r[:, 0:W], z[:, 0:W], AF.Relu, [zk], [rk])
                    self.tt('pool', hid[:, f, 0:W], r[:, 0:W], r[:, 0:W], ALU.mult, [rk], ['hid'])
                for j, t in enumerate(tiles):
                    y = yo[0]; yk = 'yo0'
                    for c in range(2):
                        z = pz[zc % 4]; zk = f'cpz{zc % 4}'; zc += 1
                        for f in range(32):
                            self.mm(z[:, :], hid[:, f, j * P:(j + 1) * P], w2[:, f, c * 512:(c + 1) * 512], f == 0, f == 31,
                                    ['hid', f'w_ff2{f}'], [zk])
                        self.tt('dve', y[:, c * 512:(c + 1) * 512], z[:, :], xt[j][:, c * 512:(c + 1) * 512], ALU.add,
                                [zk, f'cx{j}'], [yk])
                    samp = (t == NT)
                    rows = cfg.NSTOK if samp else P
                    if not last:
                        self.ld(X['x2'][t * P:(t + 1) * P, :], y[:], [yk], ['x2'], q='sp')
                    else:
                        self.act(junk[:], y[:], AF.Square, [yk], ['nrm_junk', 'nrm_ss'], accum_out=ss[:, 0:1])
                        self.act(rstd[:, 0:1], ss[:, 0:1], AF.Sqrt, ['nrm_ss'], ['nrm_rstd'], scale=1.0 / D, bias=EPS)
                        S.op('dve', lambda e: e.reciprocal(out=rstd[:, 1:2], in_=rstd[:, 0:1]), ['nrm_rstd'], ['nrm_rstd2'])
                        self.stt(xt[j][:], y[:], rstd[:, 1:2], gfin[:], ALU.mult, ALU.mult, [yk, 'nrm_rstd2', 'gfin'], [f'cx{j}'])
                        if samp:
                            self.ld(O['y_sample'][:, :], xt[j][0:rows, :], [f'cx{j}'], ['Oy_s'], q='sp')
                        else:
                            self.ld(O['y_prompt'][t * P:(t + 1) * P, :], xt[j][:], [f'cx{j}'], ['Oy_p'], q='sp')

    def page_index_cached(self, ns, e_, l):
        SB = ns['SB']
        return self.page_index(self._pb_stack, e_, "ns", l, SB['pit'])


_CACHE = {}


def _get_program(cfg_key):
    if cfg_key not in _CACHE:
        mk = MK(Cfg(*cfg_key))
        _CACHE[cfg_key] = mk.build()
    return _CACHE[cfg_key]


def kernel(**inp):
    xp = np.asarray(inp['x_prompt'])
    xs = np.asarray(inp['x_sample'])
    B, T, _ = xp.shape
    DB, S, _ = xs.shape
    pt = np.asarray(inp['page_table'])
    PAST = pt.shape[1] * P
    NS = DB // N_CORES
    ccmp = np.asarray(inp['cache_cmp_kv'])
    NPHYS = ccmp.shape[1]
    cfg_key = (T, PAST, NS, S, NPHYS)
    nc = _get_program(cfg_key)
    f32 = np.float32
    ccmp = ccmp.reshape(DEPTH * NPHYS * P, 256)
    cslc = np.asarray(inp['cache_slc_kv']).reshape(DEPTH * NPHYS * P, 256)
    cwin = np.asarray(inp['cache_win_kv']).reshape(DEPTH, DB, -1, 256)
    cmem = np.asarray(inp['cache_mem_kv']).reshape(DEPTH, DB, NMEM, 1024)
    sre = np.asarray(inp['state_ssm_re'])
    sim = np.asarray(inp['state_ssm_im'])
    memp = np.asarray(inp['mem_prompt'])
    shared = {
        'cache_cmp': ccmp, 'cache_slc': cslc,
        'norm1_g': inp['norm1_g'], 'w_in': inp['w_in'], 'cmp_pe': inp['cmp_pe'], 'cmp_w1': inp['cmp_w1'],
        'cmp_w2': inp['cmp_w2'], 'ssm_a_re': inp['ssm_a_re'], 'ssm_a_im': inp['ssm_a_im'],
        'ssm_log_dt': inp['ssm_log_dt'], 'ssm_b_re': inp['ssm_b_re'], 'ssm_b_im': inp['ssm_b_im'],
        'ssm_c_re': inp['ssm_c_re'], 'ssm_c_im': inp['ssm_c_im'],
        'ssm_d': np.asarray(inp['ssm_d']).reshape(DEPTH, NSG * 16),
        'w_glu': inp['w_glu'], 'b_glu': inp['b_glu'], 'mem_norm_g': inp['mem_norm_g'], 'w_mem_kv': inp['w_mem_kv'],
        'w_o': inp['w_o'], 'norm2_g': inp['norm2_g'], 'w_ff1': inp['w_ff1'], 'w_ff2': inp['w_ff2'],
        'final_norm_g': np.asarray(inp['final_norm_g']).reshape(1, D),
    }
    shared = {k: np.ascontiguousarray(np.asarray(v)) for k, v in shared.items()}
    in_maps = []
    for c in range(N_CORES):
        b = c % B
        m = dict(shared)
        m['x_prompt'] = np.ascontiguousarray(xp[b])
        m['mem_prompt'] = np.ascontiguousarray(memp[b])
        sl = slice(c * NS, (c + 1) * NS)
        m['x_sample'] = np.ascontiguousarray(xs[sl].reshape(NS * S, D))
        m['cache_win'] = np.ascontiguousarray(cwin[:, sl])
        m['ssm_re'] = np.ascontiguousarray(sre[:, sl])
        m['ssm_im'] = np.ascontiguousarray(sim[:, sl])
        m['cache_mem'] = np.ascontiguousarray(cmem[:, sl])
        m['page_table'] = np.ascontiguousarray(pt[sl])
        in_maps.append(m)
    res = run_bass_kernel_spmd(nc, in_maps, core_ids=list(range(N_CORES))).results
    WP = min(WINDOW, T)
    WB = min(WINDOW, PAST)

    def pstack(name, shape_tail):
        return np.stack([res[b][name] for b in range(B)], axis=1).reshape((DEPTH, B) + shape_tail)

    def sstack(name, shape_tail):
        return np.concatenate([res[c][name].reshape((DEPTH, NS) + shape_tail) for c in range(N_CORES)], axis=1)

    y_prompt = np.stack([res[b]['y_prompt'] for b in range(B)], axis=0)
    y_sample = np.concatenate([res[c]['y_sample'].reshape(NS, S, D) for c in range(N_CORES)], axis=0)
    outs = (
        y_prompt, y_sample,
        pstack('cmp_p', (T, 2, NG, HD)), sstack('cmp_s', (S, 2, NG, HD)),
        pstack('slc_p', (T, 2, NG, HD)), sstack('slc_s', (S, 2, NG, HD)),
        pstack('win_p', (WP, 2, NG, HD)), sstack('win_s', (WB, 2, NG, HD)),
        pstack('hr_p', (NSG, SST)), pstack('hi_p', (NSG, SST)),
        sstack('hr_s', (NSG, SST)), sstack('hi_s', (NSG, SST)),
        pstack('mem_p', (NMEM, 2, 4, 128)),
    )
    return tuple(np.ascontiguousarray(o.astype(f32)) for o in outs)
```
